# Optimizing a Trainium2 kernel written in Bass

```python
import jax, jax.numpy as jnp
from jax import lax
import numpy as np

D_MODEL = 1024
BATCH = 2
SEQ = 8192
DEPTH = 2

EXPAND = 2
D_INNER = EXPAND * D_MODEL
N_MIXERS = 4
D_BRANCH = D_INNER // N_MIXERS
EPS = 1e-6
NEG_BIG = -1e30

GLA_HEADS = 4
GLA_DV = D_BRANCH // GLA_HEADS
GLA_DK = GLA_DV // 2
GLA_GATE_RANK = 16
GLA_GATE_NORM = 16.0
GLA_CHUNK = 64

MLSTM_HEADS = 4
MLSTM_DH = D_BRANCH // MLSTM_HEADS
MLSTM_CONV = 4
MLSTM_CHUNK = 64

HGRN_HEADS = 4
HGRN_DK = 128
HGRN_DV = D_BRANCH // HGRN_HEADS
HGRN_CHUNK = 64

SSD_HEAD_DIM = 64
SSD_HEADS = D_BRANCH // SSD_HEAD_DIM
SSD_GROUPS = 2
SSD_STATE = 128
SSD_CONV = 4
SSD_CHUNK = 128

GLA_QK = GLA_HEADS * GLA_DK
HGRN_QF = HGRN_HEADS * HGRN_DK
SSD_BC = SSD_GROUPS * SSD_STATE
PROJ_SIZES = (
    GLA_QK, GLA_QK, D_BRANCH, GLA_GATE_RANK, D_BRANCH,
    D_BRANCH, D_BRANCH, D_BRANCH, MLSTM_HEADS, MLSTM_HEADS, D_BRANCH, D_BRANCH,
    HGRN_QF, HGRN_QF, D_BRANCH, D_BRANCH,
    D_BRANCH, SSD_BC, SSD_BC, SSD_HEADS, D_BRANCH,
)
D_PROJ = sum(PROJ_SIZES)

kernel_name = "hymba_gla_mlstm_hgrn2_ssd_hybrid"


def rmsnorm(x, w):
    xf = x.astype(jnp.float32)
    y = xf * lax.rsqrt(jnp.mean(xf * xf, axis=-1, keepdims=True) + EPS)
    return (y * w.astype(jnp.float32)).astype(x.dtype)


def grouped_rmsnorm(x, w, n_groups):
    shp = x.shape
    xg = x.reshape(shp[:-1] + (n_groups, shp[-1] // n_groups))
    return rmsnorm(xg, w.reshape(n_groups, -1)).reshape(shp)


def causal_dwconv(x, w, b):
    K = w.shape[0]
    y = lax.conv_general_dilated(x, w[:, None, :].astype(x.dtype), window_strides=(1,),
                                 padding=[(K - 1, 0)], dimension_numbers=('NWC', 'WIO', 'NWC'),
                                 feature_group_count=x.shape[-1])
    return y + b.astype(x.dtype)


def masked_exp(logw, mask):
    return jnp.where(mask, jnp.exp(jnp.where(mask, logw, 0.0)), 0.0)


def to_chunks(t, c):
    Bsz, S = t.shape[:2]
    t = t.reshape((Bsz, S // c, c) + t.shape[2:])
    if t.ndim == 5:
        return t.transpose(1, 0, 3, 2, 4)
    return t.transpose(1, 0, 3, 2)


def from_chunks(o):
    NC, Bsz, H, c, d = o.shape
    return o.transpose(1, 0, 3, 2, 4).reshape(Bsz, NC * c, H * d)


def chunk_gated_linear_attention(q, k, v, log_g, chunk):
    Bsz, S, H, dk = q.shape
    dv = v.shape[-1]
    causal = jnp.tril(jnp.ones((chunk, chunk), bool))[:, :, None]

    def step(state, inp):
        qb, kb, vb, gb = inp
        G = jnp.cumsum(gb, axis=2)
        diff = G[:, :, :, None, :] - G[:, :, None, :, :]
        decay = masked_exp(diff, causal)
        scores = jnp.einsum('bhid,bhjd,bhijd->bhij', qb, kb, decay)
        out = (jnp.einsum('bhij,bhje->bhie', scores, vb)
               + jnp.einsum('bhid,bhde->bhie', qb * jnp.exp(G), state))
        G_last = G[:, :, -1]
        k_dec = kb * jnp.exp(G_last[:, :, None, :] - G)
        state = jnp.exp(G_last)[..., None] * state + jnp.einsum('bhcd,bhce->bhde', k_dec, vb)
        return state, out

    state0 = jnp.zeros((Bsz, H, dk, dv), q.dtype)
    _, out = lax.scan(step, state0, (to_chunks(q, chunk), to_chunks(k, chunk),
                                     to_chunks(v, chunk), to_chunks(log_g, chunk)))
    return from_chunks(out)


def chunk_mlstm(q, k, v, i_pre, log_f, chunk):
    Bsz, S, H, dk = q.shape
    dv = v.shape[-1]
    causal = jnp.tril(jnp.ones((chunk, chunk), bool))

    def step(carry, inp):
        C, n, m = carry
        qb, kb, vb, ib, fb = inp
        b = jnp.cumsum(fb, axis=-1)
        logw = jnp.where(causal, b[..., :, None] - b[..., None, :] + ib[..., None, :], NEG_BIG)
        m_inter = b + m[..., None]
        m_row = jnp.maximum(jnp.max(logw, axis=-1), m_inter)
        s = jnp.einsum('bhid,bhjd->bhij', qb, kb) * masked_exp(logw - m_row[..., None], causal)
        inter = jnp.exp(m_inter - m_row)
        num = (jnp.einsum('bhij,bhje->bhie', s, vb)
               + inter[..., None] * jnp.einsum('bhid,bhde->bhie', qb, C))
        den = jnp.sum(s, axis=-1) + inter * jnp.einsum('bhid,bhd->bhi', qb, n)
        h = num / jnp.maximum(jnp.abs(den), jnp.exp(-m_row))[..., None]
        b_last = b[..., -1]
        logw_end = b_last[..., None] - b + ib
        m_new = jnp.maximum(b_last + m, jnp.max(logw_end, axis=-1))
        carry_decay = jnp.exp(b_last + m - m_new)
        k_w = kb * jnp.exp(logw_end - m_new[..., None])[..., None]
        C = carry_decay[..., None, None] * C + jnp.einsum('bhcd,bhce->bhde', k_w, vb)
        n = carry_decay[..., None] * n + jnp.sum(k_w, axis=2)
        return (C, n, m_new), h

    carry0 = (jnp.zeros((Bsz, H, dk, dv), q.dtype), jnp.zeros((Bsz, H, dk), q.dtype),
              jnp.zeros((Bsz, H), q.dtype))
    _, out = lax.scan(step, carry0, (to_chunks(q, chunk), to_chunks(k, chunk), to_chunks(v, chunk),
                                     to_chunks(i_pre, chunk), to_chunks(log_f, chunk)))
    return from_chunks(out)


def segsum_exp(a):
    T = a.shape[-1]
    rep = jnp.broadcast_to(a[..., :, None], a.shape + (T,))
    rep = jnp.where(jnp.tril(jnp.ones((T, T), bool), -1), rep, 0.0)
    ss = jnp.cumsum(rep, axis=-2)
    return masked_exp(ss, jnp.tril(jnp.ones((T, T), bool)))


def ssd_chunked(x, a, Bm, Cm, chunk):
    Bsz, S, H, P = x.shape
    G, N = Bm.shape[2], Bm.shape[3]
    R = H // G
    NC = S // chunk
    x = x.reshape(Bsz, NC, chunk, G, R, P)
    Bm = Bm.reshape(Bsz, NC, chunk, G, N)
    Cm = Cm.reshape(Bsz, NC, chunk, G, N)
    a = a.reshape(Bsz, NC, chunk, G, R).transpose(0, 3, 4, 1, 2)
    a_cs = jnp.cumsum(a, axis=-1)
    L = segsum_exp(a)
    CB = jnp.einsum('bclgn,bcsgn->bgcls', Cm, Bm)
    y_diag = jnp.einsum('bgrcls,bcsgrp->bclgrp', CB[:, :, None] * L, x)
    decay_states = jnp.exp(a_cs[..., -1:] - a_cs)
    states = jnp.einsum('bclgn,bgrcl,bclgrp->bcgrpn', Bm, decay_states, x)
    states = jnp.concatenate([jnp.zeros_like(states[:, :1]), states], axis=1)
    chunk_a = jnp.pad(a_cs[..., -1], ((0, 0), (0, 0), (0, 0), (1, 0)))
    decay_chunk = segsum_exp(chunk_a)
    states = jnp.einsum('bgrzc,bcgrpn->bzgrpn', decay_chunk, states)[:, :-1]
    y_off = jnp.einsum('bclgn,bcgrpn,bgrcl->bclgrp', Cm, states, jnp.exp(a_cs))
    return (y_diag + y_off).reshape(Bsz, S, H, P)


def gla_branch(q_raw, k_raw, v_raw, gr, z, gate_w, gate_b, norm_w):
    Bsz, S = q_raw.shape[:2]
    q = q_raw.reshape(Bsz, S, GLA_HEADS, GLA_DK) * (GLA_DK ** -0.5)
    k = k_raw.reshape(Bsz, S, GLA_HEADS, GLA_DK)
    v = v_raw.reshape(Bsz, S, GLA_HEADS, GLA_DV)
    log_g = jax.nn.log_sigmoid(gr @ gate_w + gate_b) / GLA_GATE_NORM
    log_g = log_g.reshape(Bsz, S, GLA_HEADS, GLA_DK)
    o = chunk_gated_linear_attention(q, k, v, log_g, GLA_CHUNK)
    return grouped_rmsnorm(o, norm_w, GLA_HEADS) * jax.nn.silu(z)


def mlstm_branch(q_raw, k_raw, v_raw, i_raw, f_raw, o_raw, z, conv_w, conv_b, i_b, f_b, norm_w):
    Bsz, S = q_raw.shape[:2]
    qk = jax.nn.silu(causal_dwconv(jnp.concatenate([q_raw, k_raw], axis=-1), conv_w, conv_b))
    q = qk[..., :D_BRANCH].reshape(Bsz, S, MLSTM_HEADS, MLSTM_DH)
    k = qk[..., D_BRANCH:].reshape(Bsz, S, MLSTM_HEADS, MLSTM_DH) * (MLSTM_DH ** -0.5)
    v = v_raw.reshape(Bsz, S, MLSTM_HEADS, MLSTM_DH)
    i_pre = i_raw + i_b
    log_f = jax.nn.log_sigmoid(f_raw + f_b)
    h = chunk_mlstm(q, k, v, i_pre, log_f, MLSTM_CHUNK)
    h = jax.nn.sigmoid(o_raw) * h
    return grouped_rmsnorm(h, norm_w, MLSTM_HEADS) * jax.nn.silu(z)


def hgrn2_branch(q_raw, f_raw, i_raw, z, lb, norm_w):
    Bsz, S = q_raw.shape[:2]
    lb = lb.reshape(HGRN_HEADS, HGRN_DK)
    fr = f_raw.reshape(Bsz, S, HGRN_HEADS, HGRN_DK)
    f = lb + (1.0 - lb) * jax.nn.sigmoid(fr)
    log_f = jnp.log(jnp.maximum(f, 1e-30))
    k = (1.0 - lb) * jax.nn.sigmoid(-fr)
    q = q_raw.reshape(Bsz, S, HGRN_HEADS, HGRN_DK) * (HGRN_DK ** -0.5)
    v = i_raw.reshape(Bsz, S, HGRN_HEADS, HGRN_DV)
    o = chunk_gated_linear_attention(q, k, v, log_f, HGRN_CHUNK)
    return grouped_rmsnorm(o, norm_w, HGRN_HEADS) * jax.nn.silu(z)


def ssd_branch(x_raw, B_raw, C_raw, dt_raw, z, conv_w, conv_b, dt_bias, A_log, D, norm_w):
    Bsz, S = x_raw.shape[:2]
    xbc = jax.nn.silu(causal_dwconv(jnp.concatenate([x_raw, B_raw, C_raw], axis=-1), conv_w, conv_b))
    xs = xbc[..., :D_BRANCH].reshape(Bsz, S, SSD_HEADS, SSD_HEAD_DIM)
    Bm = xbc[..., D_BRANCH:D_BRANCH + SSD_BC].reshape(Bsz, S, SSD_GROUPS, SSD_STATE)
    Cm = xbc[..., D_BRANCH + SSD_BC:].reshape(Bsz, S, SSD_GROUPS, SSD_STATE)
    dt = jax.nn.softplus(dt_raw + dt_bias)
    A = -jnp.exp(A_log)
    y = ssd_chunked(xs * dt[..., None], dt * A, Bm, Cm, SSD_CHUNK)
    y = (y + D[:, None] * xs).reshape(Bsz, S, D_BRANCH)
    return grouped_rmsnorm(y * jax.nn.silu(z), norm_w, SSD_GROUPS)


def setup_inputs(seed: int = 0) -> dict:
    key = jax.random.key(seed)
    ks = jax.random.split(key, 24)
    f32 = jnp.float32
    nrm = lambda k, shape, s: s * jax.random.normal(k, shape, f32)
    dt0 = jnp.exp(jax.random.uniform(ks[15], (DEPTH, SSD_HEADS), f32, np.log(1e-3), np.log(1e-1)))
    return {
        "x": jax.random.normal(ks[0], (BATCH, SEQ, D_MODEL), f32),
        "norm_w": 1.0 + nrm(ks[1], (DEPTH, D_MODEL), 0.02),
        "w_in": nrm(ks[2], (DEPTH, D_MODEL, D_PROJ), D_MODEL ** -0.5),
        "gla_gate_w": nrm(ks[3], (DEPTH, GLA_GATE_RANK, GLA_QK), GLA_GATE_RANK ** -0.5),
        "gla_gate_b": nrm(ks[4], (DEPTH, GLA_QK), 0.1),
        "gla_norm_w": 1.0 + nrm(ks[5], (DEPTH, D_BRANCH), 0.02),
        "ml_conv_w": nrm(ks[6], (DEPTH, MLSTM_CONV, 2 * D_BRANCH), MLSTM_CONV ** -0.5),
        "ml_conv_b": nrm(ks[7], (DEPTH, 2 * D_BRANCH), 0.02),
        "ml_i_b": nrm(ks[8], (DEPTH, MLSTM_HEADS), 0.1),
        "ml_f_b": jnp.linspace(3.0, 6.0, MLSTM_HEADS, dtype=f32)[None] + nrm(ks[9], (DEPTH, MLSTM_HEADS), 0.1),
        "ml_norm_w": 1.0 + nrm(ks[10], (DEPTH, D_BRANCH), 0.02),
        "hg_lb_logits": nrm(ks[11], (DEPTH, HGRN_QF), 0.1),
        "hg_norm_w": 1.0 + nrm(ks[12], (DEPTH, D_BRANCH), 0.02),
        "ssd_conv_w": nrm(ks[13], (DEPTH, SSD_CONV, D_BRANCH + 2 * SSD_BC), SSD_CONV ** -0.5),
        "ssd_conv_b": nrm(ks[14], (DEPTH, D_BRANCH + 2 * SSD_BC), 0.02),
        "ssd_dt_bias": dt0 + jnp.log(-jnp.expm1(-dt0)),
        "ssd_A_log": jnp.log(jax.random.uniform(ks[16], (DEPTH, SSD_HEADS), f32, 1.0, 16.0)),
        "ssd_D": 1.0 + nrm(ks[17], (DEPTH, SSD_HEADS), 0.02),
        "ssd_norm_w": 1.0 + nrm(ks[18], (DEPTH, D_BRANCH), 0.02),
        "w_out": nrm(ks[19], (DEPTH, D_INNER, D_MODEL), D_INNER ** -0.5),
        "final_norm_w": 1.0 + nrm(ks[20], (D_MODEL,), 0.02),
    }


def reference(x, norm_w, w_in, gla_gate_w, gla_gate_b, gla_norm_w, ml_conv_w, ml_conv_b, ml_i_b,
              ml_f_b, ml_norm_w, hg_lb_logits, hg_norm_w, ssd_conv_w, ssd_conv_b, ssd_dt_bias,
              ssd_A_log, ssd_D, ssd_norm_w, w_out, final_norm_w):
    f32 = jnp.float32
    split_at = [int(s) for s in np.cumsum(PROJ_SIZES)[:-1]]
    p = jax.nn.softmax(hg_lb_logits.astype(f32), axis=0)
    lower_bounds = jnp.cumsum(p, axis=0) - p[0:1]
    h = x
    for l in range(DEPTH):
        u = rmsnorm(h, norm_w[l])
        proj = (u @ w_in[l]).astype(f32)
        (a_q, a_k, a_v, a_gr, a_z,
         b_q, b_k, b_v, b_i, b_f, b_o, b_z,
         c_q, c_f, c_i, c_z,
         d_x, d_B, d_C, d_dt, d_z) = jnp.split(proj, split_at, axis=-1)
        y_a = gla_branch(a_q, a_k, a_v, a_gr, a_z, gla_gate_w[l].astype(f32), gla_gate_b[l].astype(f32),
                         gla_norm_w[l])
        y_b = mlstm_branch(b_q, b_k, b_v, b_i, b_f, b_o, b_z, ml_conv_w[l].astype(f32), ml_conv_b[l],
                           ml_i_b[l].astype(f32), ml_f_b[l].astype(f32), ml_norm_w[l])
        y_c = hgrn2_branch(c_q, c_f, c_i, c_z, lower_bounds[l], hg_norm_w[l])
        y_d = ssd_branch(d_x, d_B, d_C, d_dt, d_z, ssd_conv_w[l].astype(f32), ssd_conv_b[l],
                         ssd_dt_bias[l].astype(f32), ssd_A_log[l].astype(f32), ssd_D[l].astype(f32),
                         ssd_norm_w[l])
        mixed = jnp.concatenate([y_a, y_b, y_c, y_d], axis=-1).astype(h.dtype)
        h = h + mixed @ w_out[l]
    return rmsnorm(h, final_norm_w)
```

```python
import math
import numpy as np
from contextlib import ExitStack
import concourse.bass as bass
import concourse.mybir as mybir
from concourse.bass_utils import run_bass_kernel_spmd

F32 = mybir.dt.float32
BF16 = mybir.dt.bfloat16
AF = mybir.ActivationFunctionType
ALU = mybir.AluOpType

NCORES = 8
T = 2048
NT = 16
ST = 512
TPS = 4
NS = 4
DM = 1024
KC = 8
EPS = 1e-6
DPROJ = 7712
NSTATE = 1808
NDEC = 18
NSUM = NSTATE + NDEC
OFF_A, OFF_B, OFF_C, OFF_D = 0, 256, 784, 1296
DEC_A, DEC_B, DEC_C, DEC_D = 0, 2, 6, 10
NPP = 90
NPR = 2080
PP_GB, PP_MCW, PP_MCB, PP_SCW, PP_SCB, PP_LB = 0, 2, 34, 42, 74, 82
PR_IB, PR_FB, PR_DTB, PR_ALOG, PR_D, PR_NWA, PR_NWB, PR_NWC, PR_NWD = 0, 4, 8, 16, 24, 32, 544, 1056, 1568


class Buf:
    __slots__ = ("t", "name", "w", "r", "sem", "excl")

    def __init__(self, t, name, excl=False):
        self.t = t
        self.name = name
        self.w = None
        self.r = []
        self.sem = None
        self.excl = excl


class Sched:
    def __init__(self, nc):
        self.nc = nc
        self.engs = []
        self.sems = {}
        self._ctx = []
        self.nself = {"tensor"}
        self.ops = []

    def add_engine(self, name):
        key = "e_" + name
        self.sems[key] = self._alloc(key)
        self.engs.append(name)

    def _alloc(self, key):
        cm = self.nc.semaphore(key)
        h = cm.__enter__()
        self._ctx.append(cm)
        return h

    def dma_sem(self, name):
        key = "d_" + name
        self.sems[key] = self._alloc(key)
        return key

    def _record(self, eng, fn, kind, sem, reads, writes, cost, lat):
        ex = [r for r in reads if r.excl]
        if ex:
            writes = list(writes) + [r for r in ex if r not in writes]
            reads = [r for r in reads if not r.excl]
        deps = set()
        for r in reads:
            if r.w is not None:
                deps.add(r.w)
        for w in writes:
            if w.w is not None:
                deps.add(w.w)
            deps.update(w.r)
        oid = len(self.ops)
        self.ops.append(dict(id=oid, eng=eng, fn=fn, kind=kind, sem=sem, deps=deps, cost=cost, lat=lat))
        for r in reads:
            r.r.append(oid)
        for w in writes:
            w.w = oid
            w.r = []
        return oid

    def op(self, eng, fn, reads=(), writes=(), cost=300):
        return self._record(eng, fn, "op", "e_" + eng, reads, writes, cost, cost)

    def dma(self, eng, fn, semkey, reads=(), writes=(), cost=100, lat=4000):
        return self._record(eng, fn, "dma", semkey, reads, writes, cost, lat)

    def coll(self, eng, fn, semkey, reads=(), writes=()):
        return self._record(eng, fn, "coll", semkey, reads, writes, 2000, 40000)

    def wait_all(self, eng, ids):
        oid = len(self.ops)
        self.ops.append(dict(id=oid, eng=eng, fn=None, kind="wait", sem=None, deps=set(ids), cost=10, lat=10))
        return oid

    def schedule(self, reorder=True):
        import heapq
        ops = self.ops
        n = len(ops)
        ndeps = [len(o["deps"]) for o in ops]
        users = [[] for _ in range(n)]
        for o in ops:
            for d in o["deps"]:
                users[d].append(o["id"])
        finish = [0.0] * n
        ready_t = [0.0] * n
        efree = {e: 0.0 for e in self.engs}
        pending = {e: [] for e in self.engs}
        avail = {e: [] for e in self.engs}
        queues = {e: [] for e in self.engs}
        for o in ops:
            if ndeps[o["id"]] == 0:
                heapq.heappush(pending[o["eng"]], (0.0, o["id"]))
        done = 0
        if not reorder:
            for o in ops:
                queues[o["eng"]].append(o["id"])
            self.queues = queues
            self.makespan = 0
            return
        while done < n:
            best = None
            for e in self.engs:
                pe, av = pending[e], avail[e]
                while pe and pe[0][0] <= efree[e]:
                    _, i = heapq.heappop(pe)
                    heapq.heappush(av, i)
                if av:
                    cand = (efree[e], av[0], e, True)
                elif pe:
                    cand = (pe[0][0], pe[0][1], e, False)
                else:
                    continue
                if best is None or cand[:2] < best[:2]:
                    best = cand
            assert best is not None, "scheduler deadlock"
            st, i, e, from_av = best
            if from_av:
                heapq.heappop(avail[e])
            else:
                heapq.heappop(pending[e])
            o = ops[i]
            efree[e] = st + o["cost"]
            finish[i] = st + o["lat"]
            queues[e].append(i)
            done += 1
            for u in users[i]:
                ndeps[u] -= 1
                if finish[i] > ready_t[u]:
                    ready_t[u] = finish[i]
                if ndeps[u] == 0:
                    heapq.heappush(pending[ops[u]["eng"]], (ready_t[u], u))
        self.queues = queues
        self.makespan = max(finish)

    def emit(self):
        nc = self.nc
        ops = self.ops
        semval = {}
        cnt = {}
        for e in self.engs:
            for i in self.queues[e]:
                o = ops[i]
                if o["kind"] == "op":
                    cnt[o["sem"]] = cnt.get(o["sem"], 0) + 1
                    semval[i] = (o["sem"], cnt[o["sem"]], 1)
        for o in ops:
            if o["kind"] in ("dma", "coll"):
                inc = 16 if o["kind"] == "dma" else 1
                cnt[o["sem"]] = cnt.get(o["sem"], 0) + inc
                semval[o["id"]] = (o["sem"], cnt[o["sem"]], inc)
        with nc.Block() as block:
            for e in self.engs:
                deco = getattr(block, e)
                q = self.queues[e]
                own = "e_" + e

                def body(h, q=q, e=e, own=own):
                    seen = {}
                    for i in q:
                        o = ops[i]
                        need = {}
                        for d in o["deps"]:
                            k, v, _ = semval[d]
                            if k == own and e in self.nself:
                                continue
                            if need.get(k, 0) < v:
                                need[k] = v
                        for k, v in need.items():
                            if seen.get(k, 0) >= v:
                                continue
                            seen[k] = v
                            h.wait_ge(self.sems[k], v)
                        if o["fn"] is not None:
                            k, v, inc = semval[i]
                            o["fn"](h).then_inc(self.sems[k], inc)
                deco(body)

    def close(self):
        for cm in reversed(self._ctx):
            cm.__exit__(None, None, None)


def build(debug=False, MIXERS="ABCD", STOP=99, NLAYERS=2, REORDER=True):
    nc = bass.Bass("TRN2", target_bir_lowering=False)
    es = ExitStack()
    S = Sched(nc)
    for n in ("sync", "gpsimd", "tensor", "vector", "scalar"):
        S.add_engine(n)

    def dram(name, shape, dt=F32, kind="ExternalInput"):
        return nc.dram_tensor(name, list(shape), dt, kind=kind).ap()

    layer = 0
    P2 = False
    last = False

    d_hin = dram("hin", [T, DM])
    d_halo = dram("halo", [3, DM])
    d_win_all = dram("w_in", [2, DM, DPROJ])
    d_wout_all = dram("w_out", [2, 2048, DM])
    d_nw_all = dram("nw", [2, DM])
    d_fnw = dram("fnw", [DM])
    d_pp_all = dram("pp", [2, 128, NPP])
    d_pr_all = dram("pr", [2, NPR])
    d_gw_all = dram("gw", [2, 16, 256])
    d_pm = dram("pm", [128, 9])
    d_out = dram("hout", [T, DM], kind="ExternalOutput")
    if debug:
        d_dbg = dram("dbg", [T, 2048], BF16, kind="ExternalOutput")
    d_sloc = [nc.dram_tensor(f"sloc{l}", [128, NSUM], F32).ap() for l in range(2)]
    d_sgat = [nc.dram_tensor(f"sgat{l}", [4 * 128, NSUM], F32).ap() for l in range(2)]
    d_hloc = nc.dram_tensor("hloc", [3, DM], F32).ap()
    d_hgat = nc.dram_tensor("hgat", [12, DM], F32).ap()
    GROUPS = [[0, 1, 2, 3], [4, 5, 6, 7]]

    cnt = [0]
    dbg_ids = []

    def sb(shape, dt, name=None):
        cnt[0] += 1
        name = "s_" + (name or f"sb{cnt[0]}")
        t = es.enter_context(nc.sbuf_tensor(name, list(shape), dt))
        return Buf(t, name)

    def ps(shape, dt, name):
        return es.enter_context(nc.psum_tensor(name, list(shape), dt))

    def withsem(b):
        b.sem = S.dma_sem(b.name)
        return b

    def fsz(ap):
        n = 1
        for (_, c) in list(ap.ap)[1:]:
            n *= c
        return n

    def is_psum(ap):
        return "PSum" in type(ap.tensor).__name__

    def ACT(out, in_, func, R, W, **kw):
        c = 200 + 0.85 * fsz(out)
        return S.op("scalar", lambda e: e.activation(out=out, in_=in_, func=func, **kw), R, W, cost=c)

    def _dve_cost(out, ins):
        c = 70 + 1.05 * fsz(out)
        if any(is_psum(a) for a in ins):
            c += 60
        return c

    def TT(out, in0, in1, op, R, W, eng="vector"):
        return S.op(eng, lambda e: e.tensor_tensor(out=out, in0=in0, in1=in1, op=op), R, W,
                    cost=_dve_cost(out, [in0, in1]))

    def TS(out, in0, s1, s2, op0, op1, R, W, eng="vector"):
        c = _dve_cost(out, [in0])
        if s2 is None:
            return S.op(eng, lambda e: e.tensor_scalar(out=out, in0=in0, scalar1=s1, scalar2=None, op0=op0), R, W, cost=c)
        return S.op(eng, lambda e: e.tensor_scalar(out=out, in0=in0, scalar1=s1, scalar2=s2, op0=op0, op1=op1), R, W, cost=c)

    def STT(out, in0, scalar, in1, op0, op1, R, W, eng="vector"):
        return S.op(eng, lambda e: e.scalar_tensor_tensor(out=out, in0=in0, scalar=scalar, in1=in1, op0=op0, op1=op1), R, W,
                    cost=_dve_cost(out, [in0, in1]))

    def CP(out, in_, R, W, eng="vector"):
        return S.op(eng, lambda e: e.tensor_copy(out=out, in_=in_), R, W, cost=_dve_cost(out, [in_]))

    def MSET(ap, val, W, eng="gpsimd"):
        return S.op(eng, lambda e: e.memset(ap, val), (), W, cost=200 + fsz(ap))

    def MM(out, lhsT, rhs, start, stop, R, W):
        n = fsz(rhs)
        f32 = "float32" in str(rhs.tensor.dtype)
        c = (64 + 1.7 * n) if f32 else (64 + 0.45 * n)
        return S.op("tensor", lambda e: e.matmul(out, lhsT=lhsT, rhs=rhs, start=start, stop=stop), R, W, cost=c)

    def TR(out, in_, ident, R, W):
        return S.op("tensor", lambda e: e.transpose(out, in_, ident), R, W, cost=150)

    def DMA(q, out, in_, buf_sem, R, W):
        nbytes = fsz(out) * 128 * 4
        issue = 1000 if q == "gpsimd" else 80
        return S.dma(q, lambda e: e.dma_start(out=out, in_=in_), buf_sem.sem, R, W, cost=issue,
                     lat=issue + 2000 + nbytes / 120.0)

    identf = sb([128, 128], F32, "identf")
    identb = sb([128, 128], BF16, "identb")
    trif = sb([128, 128], F32, "trif")
    suf = sb([128, 128], F32, "suf")
    negb_ = sb([128, 128], BF16, "negmask")
    onesf = sb([128, 128], F32, "onesf")
    rst = sb([128, 512], F32, "rst")
    MSET(onesf.t[:], 1.0, [onesf])
    MSET(identf.t[:], 1.0, [identf])
    S.op("gpsimd", lambda e: e.affine_select(out=identf.t[:], in_=identf.t[:], pattern=[[-1, 128]],
                                             compare_op=ALU.is_equal, fill=0.0, base=0, channel_multiplier=1),
         [identf], [identf])
    CP(identb.t[:], identf.t[:], [identf], [identb])
    MSET(trif.t[:], 1.0, [trif])
    S.op("gpsimd", lambda e: e.affine_select(out=trif.t[:], in_=trif.t[:], pattern=[[1, 128]],
                                             compare_op=ALU.is_ge, fill=0.0, base=0, channel_multiplier=-1),
         [trif], [trif])
    TS(suf.t[:], trif.t[:], -1.0, 1.0, ALU.mult, ALU.add, [trif], [suf])
    TS(negb_.t[:], suf.t[:], -30000.0, None, ALU.mult, None, [suf], [negb_])
    MSET(rst.t[:], 1.0, [rst])
    rst3 = rst.t[:, :].rearrange("p (c k) -> p c k", k=64)
    MSET(rst3[:, :, 0:1], 0.0, [rst])

    hres_t = es.enter_context(nc.sbuf_tensor("hres", [128, NT, DM], F32))
    HT = [withsem(Buf(hres_t, f"ht{t}")) for t in range(NT)]
    nwbc = withsem(sb([128, DM], F32, "nwbc"))
    uT = sb([128, KC, 515], BF16, "uT")
    ubf = sb([128, DM], BF16, "ubf")
    junk = sb([128, DM], BF16, "junk")
    NSLOT = 2
    slots = [withsem(sb([128, 4096], BF16, f"wslot{i}")) for i in range(NSLOT)]
    wsm = withsem(sb([128, KC, 32], BF16, "wsm"))
    pp = withsem(sb([128, NPP], F32, "pp"))
    pr = withsem(sb([128, NPR], F32, "pr"))
    gwf = withsem(sb([16, 256], F32, "gwf"))
    gwb = sb([16, 256], BF16, "gwb")
    grt = sb([16, 512], BF16, "grt")
    pm = withsem(sb([128, 9], F32, "pm"))
    stf_t = es.enter_context(nc.sbuf_tensor("stf", [128, NSUM], F32))
    stb_t = es.enter_context(nc.sbuf_tensor("stb", [128, NSTATE], BF16))
    STF = {m: Buf(stf_t, "stf" + m) for m in "ABCD"}
    STB = {m: Buf(stb_t, "stb" + m) for m in "ABCD"}
    STDEC = Buf(stf_t, "stdec")
    stsem = withsem(Buf(stf_t, "stout"))
    halo_raw = sb([128, 16, 3], F32, "halo_raw")
    BFB = [sb([128, TPS, 528], BF16, f"bfb{i}") for i in range(6)]
    FB = [sb([128, 515], F32, f"fb{i}") for i in range(6)]
    TTB = [sb([128, 512], F32, f"ttb{i}") for i in range(4)]
    SMF = [sb([128, 128], F32, f"smf{i}") for i in range(4)]
    SMB = [sb([128, 128], BF16, f"smb{i}") for i in range(4)]
    VDB = [sb([128, 512], BF16, f"vdb{i}") for i in range(2)]
    MIXB = [sb([128, 512], BF16, f"mix{i}") for i in range(2)]
    mixT = sb([128, 4, ST], BF16, "mixT")
    sml = sb([128, 512], F32, "sml")
    SML = {}
    _smo = [0]

    def small(name, n):
        o = _smo[0]
        _smo[0] += n
        assert _smo[0] <= 512
        SML[name] = (o, n)
        return sml.t[:, o:o + n]

    gates_t = es.enter_context(nc.sbuf_tensor("gates", [128, 512], F32))
    GATES = Buf(gates_t, "gates")
    egl_t = es.enter_context(nc.sbuf_tensor("egl", [128, 4, 8], F32))
    EGL = Buf(egl_t, "egl")
    halo_h = withsem(sb([128, DM], F32, "halo_h"))
    cmb = withsem(sb([128, NSUM], F32, "cmb"))
    for i in range(4):
        withsem(TTB[i])

    pj_t = [ps([128, 512], F32, f"pj{i}") for i in range(2)]
    PJ = [Buf(t, f"pj{i}", excl=True) for i, t in enumerate(pj_t)]
    ptu_t = ps([128, 1024], BF16, "ptu")
    PTU = Buf(ptu_t, "ptu", excl=True)
    pts_t = ps([128, 1024], BF16, "pts")
    PTSB = Buf(pts_t, "pts", excl=True)
    PTS = []
    for i in range(8):
        PTS.append((PTSB, i * 128))
        PTS.append((PTU, i * 128))
    pf_t = [ps([128, 512], F32, f"pf{i}") for i in range(2)]
    PFB = [Buf(pf_t[i], f"pf{i}", excl=True) for i in range(2)]
    PF = [(PFB[i % 2], (i // 2) * 128) for i in range(8)]
    pw_t = [ps([128, 512], F32, f"pw{i}") for i in range(2)]
    PWB = [Buf(pw_t[i], f"pw{i}", excl=True) for i in range(2)]
    PW = [(PWB[i % 2], (i // 2) * 256) for i in range(4)]
    rot = {}

    def nxt(pool, key):
        i = rot.get(key, 0)
        rot[key] = i + 1
        return pool[i % len(pool)]

    def pjn():
        return nxt(PJ, "pj")

    def pfn():
        b, o = nxt(PF, "pf")
        return b, b.t[:, o:o + 128]

    def pwn():
        b, o = nxt(PW, "pw")
        return b, b.t[:, o:o + 256]

    def ptsn():
        b, o = nxt(PTS, "pts")
        return b, b.t[:, o:o + 128]

    def fbn():
        return nxt(FB, "fb")

    def slot3(s):
        return s.t[:, :].rearrange("p (k c) -> p k c", k=KC)

    def slot_wo(s):
        return s.t[:, :].rearrange("p (k c) -> p k c", k=4)

    win3 = None
    wout3 = None

    wcache = {}
    d_wscr = nc.dram_tensor("wscr", [48, 128, 4096], BF16).ap()

    def _load_cached(key, slot_view, src_ap, n):
        s = nxt(slots, "slot")
        dst = slot_view(s)
        if key not in wcache:
            scr = d_wscr[len(wcache)]
            sbuf = withsem(Buf(None, "wscr%d" % len(wcache)))
            wcache[key] = (scr, sbuf)
            DMA("gpsimd", dst, src_ap, s, [], [s])
            DMA("sync", scr, s.t[:, :], sbuf, [s], [sbuf])
        else:
            scr, sbuf = wcache[key]
            DMA("sync", s.t[:, :], scr, s, [sbuf], [s])
        return s

    def load_w(c0, c1):
        n = c1 - c0
        return _load_cached((layer, "i", c0, c1), lambda s: slot3(s)[:, :, 0:n], win3[:, :, c0:c1], n)

    def load_wo(m):
        return _load_cached((layer, "o", m), lambda s: slot_wo(s)[:, :, :], wout3[:, m * 4:(m + 1) * 4, :], 4096)

    DMA("sync", pm.t[:], d_pm, pm, [], [pm])
    for t in range(NT):
        DMA("sync", hres_t[:, t, :], d_hin[t * 128:(t + 1) * 128, :], HT[t], [], [HT[t]])
    MSET(halo_h.t[:], 0.0, [halo_h])
    DMA("sync", halo_h.t[0:3, :], d_halo, halo_h, [], [halo_h])

    negb = small("negb", 2)
    ib2 = small("ib2", 4)
    aneg = small("aneg", 8)
    lbv = small("lbv", 4)
    omlb = small("omlb", 4)
    ss4 = small("ss4", 4)
    rs4 = small("rs4", 4)
    rr4 = small("rr4", 4)
    ssn = small("ssn", 2)
    rsn = small("rsn", 2)
    decp = small("decp", NDEC)
    acs_s = small("acs", 8)
    ee_s = small("ee", 8)
    etot_s = small("etot", 8)
    dec_s = small("dec", 8)
    dd_s = small("dd", 8)
    allst = list(STF.values()) + [STDEC]
    SLOC = [withsem(Buf(None, f"sloc{l}")) for l in range(2)]
    SGAT = [withsem(Buf(None, f"sgat{l}")) for l in range(2)]
    HLOC = withsem(Buf(None, "hloc"))
    HGAT = withsem(Buf(None, "hgat"))

    def load_layer_params(l):
        DMA("sync", nwbc.t[:], d_nw_all[l].partition_broadcast(128), nwbc, [], [nwbc])
        DMA("sync", pp.t[:], d_pp_all[l], pp, [], [pp])
        DMA("sync", pr.t[:], d_pr_all[l].partition_broadcast(128), pr, [], [pr])
        DMA("sync", gwf.t[:], d_gw_all[l], gwf, [], [gwf])
        CP(gwb.t[:], gwf.t[:], [gwf], [gwb])
        DMA("gpsimd", wsm.t[:, :, 0:16], win3[:, :, 1024:1040], wsm, [], [wsm])
        DMA("gpsimd", wsm.t[:, :, 16:24], win3[:, :, 3088:3096], wsm, [], [wsm])
        DMA("gpsimd", wsm.t[:, :, 24:32], win3[:, :, 7192:7200], wsm, [], [wsm])
        TS(negb, pp.t[:, PP_GB:PP_GB + 2], -1.0, None, ALU.mult, None, [pp], [sml])
        TS(ib2, pr.t[:, PR_IB:PR_IB + 4], math.log(128 ** -0.5), None, ALU.add, None, [pr], [sml])
        ACT(aneg, pr.t[:, PR_ALOG:PR_ALOG + 8], AF.Exp, [pr], [sml])
        TS(aneg, aneg, -1.0, None, ALU.mult, None, [sml], [sml])
        lg3 = pp.t[:, PP_LB:PP_LB + 8].rearrange("p (b l) -> p b l", l=2)
        if l == 0:
            MSET(lbv, 0.0, [sml], eng="vector")
        else:
            TT(lbv, lg3[:, :, 1], lg3[:, :, 0], ALU.subtract, [pp], [sml])
            ACT(lbv, lbv, AF.Sigmoid, [sml], [sml])
        TS(omlb, lbv, -1.0, 1.0, ALU.mult, ALU.add, [sml], [sml])

    cgroups = []
    for blk in range(2):
        cgroups.append((OFF_A + blk * 128, 128, DEC_A + blk))
    for h in range(4):
        cgroups.append((OFF_B + h * 132, 132, DEC_B + h))
    for h in range(4):
        cgroups.append((OFF_C + h * 128, 128, DEC_C + h))
    for h in range(8):
        cgroups.append((OFF_D + h * 64, 64, DEC_D + h))

    def init_pass():
        MSET(stf_t[:, 0:NSTATE], 0.0, allst, eng="vector")
        if not P2:
            MSET(stf_t[:, NSTATE:NSUM], 1.0, allst, eng="vector")
        else:
            for j in range(3):
                DMA("sync", cmb.t[:], d_sgat[layer][j * 128:(j + 1) * 128, :], cmb, [SGAT[layer]], [cmb])
                TS(decp, cmb.t[:, NSTATE:NSUM], pm.t[:, j:j + 1], pm.t[:, 3 + j:4 + j], ALU.mult, ALU.add,
                   [cmb, pm], [sml])
                TS(cmb.t[:, 0:NSTATE], cmb.t[:, 0:NSTATE], pm.t[:, j:j + 1], None, ALU.mult, None, [cmb, pm], [cmb])
                for (o, n, g) in cgroups:
                    STT(stf_t[:, o:o + n], stf_t[:, o:o + n], decp[:, g:g + 1], cmb.t[:, o:o + n], ALU.mult, ALU.add,
                        allst + [cmb, sml], allst)
        CP(stb_t[:, :], stf_t[:, 0:NSTATE], allst, list(STB.values()))

    def end_p1():
        l = layer
        DMA("sync", d_sloc[l], stf_t[:, :], SLOC[l], allst, [SLOC[l]])
        S.coll("gpsimd", lambda e: e.collective_compute("AllGather", ALU.bypass, replica_groups=GROUPS,
                                                        ins=[d_sloc[l]], outs=[d_sgat[l]]),
               SGAT[l].sem, [SLOC[l]], [SGAT[l]])

    def exchange_halo():
        DMA("sync", d_hloc, hres_t[125:128, NT - 1, :], HLOC, [HT[NT - 1]], [HLOC])
        S.coll("gpsimd", lambda e: e.collective_compute("AllGather", ALU.bypass, replica_groups=GROUPS,
                                                        ins=[d_hloc], outs=[d_hgat]),
               HGAT.sem, [HLOC], [HGAT])
        MSET(halo_h.t[:], 0.0, [halo_h])
        for j in range(3):
            for half in range(2):
                tb = nxt(TTB, "ttb")
                hs = slice(half * 512, (half + 1) * 512)
                DMA("sync", tb.t[0:3, :], d_hgat[j * 3:(j + 1) * 3, hs], tb, [HGAT], [tb])
                STT(halo_h.t[0:3, hs], tb.t[0:3, :], pm.t[0:3, 6 + j:7 + j], halo_h.t[0:3, hs], ALU.mult, ALU.add,
                    [tb, pm, halo_h], [halo_h])

    def tokc(t):
        return slice(3 + t * 128, 3 + (t + 1) * 128)

    def rstd_of(out_ap, ss_ap, n):
        TS(out_ap, ss_ap, 1.0 / n, EPS, ALU.mult, ALU.add, [sml], [sml])
        ACT(out_ap, out_ap, AF.Ln, [sml], [sml])
        ACT(out_ap, out_ap, AF.Exp, [sml], [sml], scale=-0.5)

    def make_uT_tile(hbuf, h_ap, dst_cols, ncol):
        ACT(junk.t[:], h_ap, AF.Square, [hbuf], [junk, sml], accum_out=ss4[:, 0:1])
        rstd_of(rs4[:, 0:1], ss4[:, 0:1], DM)
        STT(ubf.t[:], h_ap, rs4[:, 0:1], nwbc.t[:], ALU.mult, ALU.mult, [hbuf, sml, nwbc], [ubf])
        for k in range(KC):
            TR(ptu_t[:, k * 128:(k + 1) * 128], ubf.t[:, k * 128:(k + 1) * 128], identb.t[:], [ubf, identb], [PTU])
        src = ptu_t[:, :].rearrange("p (k c) -> p k c", k=KC)[:, :, 0:ncol]
        CP(uT.t[:, :, dst_cols], src, [PTU], [uT])

    def proj_fm(slot, off, M, cols=slice(3, 515), n=512):
        pj = pjn()
        v = slot3(slot)
        for k in range(KC):
            MM(pj.t[0:M, 0:n], v[:, k, off:off + M], uT.t[:, k, cols], k == 0, k == KC - 1, [slot, uT], [pj])
        return pj

    def proj_tm(slot, ncols, t):
        pj = pjn()
        v = slot3(slot)
        for k in range(KC):
            MM(pj.t[:, 0:ncols], uT.t[:, k, tokc(t)], v[:, k, 0:ncols], k == 0, k == KC - 1, [slot, uT], [pj])
        return pj

    def proj_small(c0, c1, t):
        b, ap = pfn()
        n = c1 - c0
        for k in range(KC):
            MM(ap[:, 0:n], uT.t[:, k, tokc(t)], wsm.t[:, k, c0:c1], k == 0, k == KC - 1, [wsm, uT], [b])
        return b, ap

    def conv_block(s_idx, slot, off, hidx, wcol, bcol, out_ap, out_buf):
        pj = proj_fm(slot, off, 128)
        raw = fbn()
        if s_idx == 0:
            b, ap = pfn()
            v = slot3(slot)
            for k in range(KC):
                MM(ap[:, 0:3], v[:, k, off:off + 128], uT.t[:, k, 0:3], k == 0, k == KC - 1, [slot, uT], [b])
            CP(raw.t[:, 0:3], ap[:, 0:3], [b], [raw])
        else:
            CP(raw.t[:, 0:3], halo_raw.t[:, hidx, :], [halo_raw], [raw])
        ACT(raw.t[:, 3:515], pj.t[:, :], AF.Copy, [pj], [raw])
        CP(halo_raw.t[:, hidx, :], raw.t[:, 512:515], [raw], [halo_raw])
        acc = fbn()
        w = lambda k: pp.t[:, wcol + k:wcol + k + 1]
        TS(acc.t[:, 0:512], raw.t[:, 0:512], w(0), pp.t[:, bcol:bcol + 1], ALU.mult, ALU.add, [raw, pp], [acc])
        for k in (1, 2, 3):
            STT(acc.t[:, 0:512], raw.t[:, k:k + 512], w(k), acc.t[:, 0:512], ALU.mult, ALU.add, [raw, pp, acc], [acc])
        ACT(out_ap, acc.t[:, 0:512], AF.Silu, [acc], [out_buf])

    def finish_mixer(s_idx, m, t, mix):
        if debug:
            tg = s_idx * TPS + t
            dbg_ids.append(S.dma("sync", lambda e: e.dma_start(out=d_dbg[tg * 128:(tg + 1) * 128, m * 512:(m + 1) * 512], in_=mix.t[:]),
                                 mix.sem, [mix], []))
        for b4 in range(4):
            pb, pap = ptsn()
            TR(pap, mix.t[:, b4 * 128:(b4 + 1) * 128], identb.t[:], [mix, identb], [pb])
            CP(mixT.t[:, b4, t * 128:(t + 1) * 128], pap, [pb], [mixT], eng="vector")

    def out_proj(s_idx, m):
        s = load_wo(m)
        v = slot_wo(s)
        for t in range(TPS):
            tg = s_idx * TPS + t
            for half in range(2):
                pj = pjn()
                for kc in range(4):
                    MM(pj.t[:, :], mixT.t[:, kc, t * 128:(t + 1) * 128], v[:, kc, half * 512:(half + 1) * 512],
                       kc == 0, kc == 3, [s, mixT], [pj])
                hap = hres_t[:, tg, half * 512:(half + 1) * 512]
                TT(hap, hap, pj.t[:, :], ALU.add, [HT[tg], pj], [HT[tg]])

    def scalar_decay(t, nh, a_ap, a_buf, groups, ybuf, yap, dvh, pad):
        bpa, pa = pfn()
        MM(pa[:, 0:nh], trif.t[:], a_ap, True, True, [trif, a_buf], [bpa])
        MM(pa[:, 64:64 + nh], onesf.t[:], a_ap, True, True, [onesf, a_buf], [bpa])
        CP(acs_s[:, 0:nh], pa[:, 0:nh], [bpa], [sml])
        if P2:
            ACT(ee_s[:, 0:nh], pa[:, 0:nh], AF.Exp, [bpa], [sml])
        ACT(etot_s[:, 0:nh], pa[:, 64:64 + nh], AF.Exp, [bpa], [sml])
        TT(dd_s[:, 0:nh], pa[:, 64:64 + nh], acs_s[:, 0:nh], ALU.subtract, [bpa, sml], [sml])
        ACT(dec_s[:, 0:nh], dd_s[:, 0:nh], AF.Exp, [sml], [sml])
        for g in groups:
            ncg = g["ncg"]
            heads = g["heads"]
            if P2:
                bqk, pqk = pfn()
                MM(pqk, g["kT"], g["qT"], True, True, g["qkbufs"], [bqk])
                byo, pyo = pwn()
                MM(pyo[:, 0:ncg], g["qT"], g["sb"][:, 0:ncg], True, True, g["qkbufs"] + [g["stb"]], [byo])
                byd, pyd = pwn()
            vd = nxt(VDB, "vdb")
            for j, h in enumerate(heads):
                cs = slice(j * pad, j * pad + dvh)
                if P2:
                    lh = nxt(SMF, "smf")
                    ACT(lh.t[:], suf.t[:], AF.Identity, [suf, a_buf], [lh], scale=a_ap[:, h:h + 1])
                    bsg, psg = pfn()
                    MM(psg, lh.t[:], trif.t[:], True, False, [lh, trif], [bsg])
                    MM(psg, identb.t[:], negb_.t[:], False, True, [identb, negb_], [bsg])
                    esg = nxt(SMF, "smf")
                    ACT(esg.t[:], psg, AF.Exp, [bsg], [esg])
                    wb = nxt(SMB, "smb")
                    TT(wb.t[:], pqk, esg.t[:], ALU.mult, [bqk, esg], [wb])
                    MM(pyd[:, cs], wb.t[:], g["v"][:, cs], True, True, [wb, g["vbuf"]], [byd])
                if len(heads) == 1:
                    TS(vd.t[:, cs], g["v"][:, cs], dec_s[:, h:h + 1], None, ALU.mult, None, [g["vbuf"], sml], [vd])
            if len(heads) > 1:
                nhh = len(heads)
                so = SML["dec"][0] + heads[0]
                decb = bass.AP(sml.t, so, [[512, 128], [1, nhh], [0, dvh]])
                TT(vd.t[:, 0:ncg].rearrange("p (h c) -> p h c", c=dvh),
                   g["v"][:, 0:ncg].rearrange("p (h c) -> p h c", c=dvh), decb, ALU.mult, [g["vbuf"], sml], [vd])
            bu, pu = pwn()
            MM(pu[:, 0:ncg], g["ktok"], vd.t[:, 0:ncg], True, True, g["ktokbufs"] + [vd], [bu])
            if P2:
                for j, h in enumerate(heads):
                    cs = slice(j * pad, j * pad + dvh)
                    ACT(yap(h), pyd[:, cs], AF.Copy, [byd], [ybuf])
                    STT(yap(h), pyo[:, cs], ee_s[:, h:h + 1], yap(h), ALU.mult, ALU.add, [byo, sml, ybuf], [ybuf])
            for j, h in enumerate(heads):
                cs = slice(j * pad, j * pad + dvh)
                STT(g["sf"][:, cs], g["sf"][:, cs], etot_s[:, h:h + 1], pu[:, cs], ALU.mult, ALU.add,
                    [g["stf"], sml, bu], [g["stf"]])
                if not P2:
                    dc = stf_t[:, NSTATE + g["decoff"] + j:NSTATE + g["decoff"] + j + 1]
                    TT(dc, dc, etot_s[:, h:h + 1], ALU.mult, [STDEC, sml], [STDEC])
            ACT(g["sb"], g["sf"], AF.Copy, [g["stf"]], [g["stb"]])


    xst = sb([128, 4, 512], F32, "xst")
    yb = sb([128, 4, 132], F32, "yb")
    VTD = [sb([128, 512], BF16, f"vtd{i}") for i in range(2)]
    DBGS = [S.dma_sem(f"dbg{i}") for i in range(2)] if debug else None
    for i, mb in enumerate(MIXB):
        mb.sem = DBGS[i] if debug else None

    def recur_vd(s_idx, m, hpb, dk, qTb, kTb, kdtokb, vTb, zGb, soff, doff, nwoff, midx):
        for t in range(TPS):
            pouts = []
            for h in range(4):
                blk = h // hpb
                r0 = (h % hpb) * dk
                rows = slice(r0, r0 + dk)
                hc = slice(h * 128, (h + 1) * 128)
                scol = soff + blk * 128
                sf = stf_t[rows, scol:scol + 128]
                sbf = stb_t[rows, scol:scol + 128]
                tcs = slice(t * 128, (t + 1) * 128)
                if P2:
                    bsc, psc = pfn()
                    MM(psc, kTb.t[rows, blk, tcs], qTb.t[rows, blk, tcs], True, True, [kTb, qTb], [bsc])
                    sm = nxt(SMB, "smb")
                    TT(sm.t[:], psc, trif.t[:], ALU.mult, [bsc, trif], [sm])
                    bo, po = pwn()
                for c in range(2):
                    tr = slice(c * 64, (c + 1) * 64)
                    if P2:
                        MM(po[tr, 0:128], sm.t[tr, tr], vTb.t[tr, t, hc], True, False, [sm, vTb], [bo])
                        MM(po[tr, 0:128], qTb.t[rows, blk, t * 128 + c * 64:t * 128 + (c + 1) * 64], sbf,
                           False, True, [qTb, STB[m]], [bo])
                    bu, pu = pfn()
                    MM(pu[rows, 0:128], kdtokb.t[tr, t, blk * 128 + r0:blk * 128 + r0 + dk], vTb.t[tr, t, hc],
                       True, True, [kdtokb, vTb], [bu])
                    eg = egl_t[rows, blk, t * 2 + c:t * 2 + c + 1]
                    STT(sf, sf, eg, pu[rows, 0:128], ALU.mult, ALU.add, [STF[m], EGL, bu], [STF[m]])
                    ACT(sbf, sf, AF.Copy, [STF[m]], [STB[m]])
                    if not P2:
                        dc = stf_t[rows, NSTATE + doff + blk:NSTATE + doff + blk + 1]
                        TT(dc, dc, eg, ALU.mult, [STDEC, EGL], [STDEC])
                if P2:
                    ACT(junk.t[:, 0:128], po[:, 0:128], AF.Square, [bo], [junk, sml], accum_out=ss4[:, h:h + 1])
                    pouts.append((bo, po))
            if P2:
                rstd_of(rs4[:, 0:4], ss4[:, 0:4], 128)
                nwz = nxt(TTB, "ttb")
                TT(nwz.t[:], pr.t[:, nwoff:nwoff + 512], zGb.t[:, t, 0:512], ALU.mult, [pr, zGb], [nwz])
                mix = nxt(MIXB, "mixb")
                for h in range(4):
                    hc = slice(h * 128, (h + 1) * 128)
                    STT(mix.t[:, hc], pouts[h][1][:, 0:128], rs4[:, h:h + 1], nwz.t[:, hc], ALU.mult, ALU.mult,
                        [pouts[h][0], sml, nwz], [mix])
                finish_mixer(s_idx, midx, t, mix)
        if P2:
            out_proj(s_idx, midx)

    def vd_front(blk, kf, csb, sG, qslot, qoff, qscale, qTb, kTb, kdTb):
        ACT(egl_t[:, blk, :], csb.t[:, 63:512:64], AF.Exp, [csb], [EGL], scale=sG)
        tmp = fbn()
        if STOP == 31:
            return
        if P2:
            qf = fbn()
            pq = proj_fm(qslot, qoff, 128)
            ACT(qf.t[:, 0:512], pq.t[:, :], AF.Identity, [pq], [qf], scale=qscale)
            if STOP == 32:
                return
            ACT(tmp.t[:, 0:512], csb.t[:, 0:512], AF.Exp, [csb], [tmp], scale=sG)
            TT(qTb.t[:, blk, 0:512], qf.t[:, 0:512], tmp.t[:, 0:512], ALU.mult, [qf, tmp], [qTb])
            if STOP == 33:
                return
            ACT(tmp.t[:, 0:512], csb.t[:, 0:512], AF.Exp, [csb], [tmp], scale=-sG)
            TT(kTb.t[:, blk, 0:512], kf.t[:, 0:512], tmp.t[:, 0:512], ALU.mult, [kf, tmp], [kTb])
        if STOP == 34:
            return
        glb = bass.AP(csb.t, 63, [[515, 128], [64, 8], [0, 64]])
        cs3 = csb.t[:, 0:512].rearrange("p (c k) -> p c k", k=64)
        TT(tmp.t[:, 0:512].rearrange("p (c k) -> p c k", k=64), glb, cs3, ALU.subtract, [csb], [tmp])
        if STOP == 35:
            return
        ACT(tmp.t[:, 0:512], tmp.t[:, 0:512], AF.Exp, [tmp], [tmp], scale=sG)
        if STOP == 36:
            return
        TT(kdTb.t[:, blk, 0:512], kf.t[:, 0:512], tmp.t[:, 0:512], ALU.mult, [kf, tmp], [kdTb])
        if STOP == 37:
            return

    def scan(csb, src):
        S.op("vector", lambda e: e.tensor_tensor_scan(out=csb.t[:, 0:512], data0=rst.t[:, :], data1=src.t[:, 0:512],
                                                      initial=0.0, op0=ALU.mult, op1=ALU.add), [rst, src], [csb],
             cost=1250)

    def vz_and_kdtok(nblk, vslot_cols, zslot_cols, vTb, zGb, kdTb, kdtokb):
        s_v = load_w(*vslot_cols)
        for t in range(TPS):
            pv = proj_tm(s_v, 512, t)
            ACT(vTb.t[:, t, 0:512], pv.t[:, :], AF.Copy, [pv], [vTb])
        if STOP == 41:
            return
        if P2:
            s_z = load_w(*zslot_cols)
            for t in range(TPS):
                pz = proj_tm(s_z, 512, t)
                ACT(zGb.t[:, t, 0:512], pz.t[:, :], AF.Silu, [pz], [zGb])
        if STOP == 42:
            return
        for t in range(TPS):
            for blk in range(nblk):
                pb, pap = ptsn()
                TR(pap, kdTb.t[:, blk, t * 128:(t + 1) * 128], identb.t[:], [kdTb, identb], [pb])
                CP(kdtokb.t[:, t, blk * 128:(blk + 1) * 128], pap, [pb], [kdtokb])

    def mixer_A(s_idx):
        qTb, kTb, kdTb, vTb, zGb, kdtokb = BFB
        s_qk = load_w(0, 512)
        pg = pjn()
        for k in range(KC):
            MM(pg.t[0:16, :], wsm.t[:, k, 0:16], uT.t[:, k, 3:515], k == 0, k == KC - 1, [wsm, uT], [pg])
        ACT(grt.t[:], pg.t[0:16, :], AF.Copy, [pg], [grt])
        if STOP < 1:
            return
        for blk in range(2):
            kf = fbn()
            pk = proj_fm(s_qk, 256 + blk * 128, 128)
            ACT(kf.t[:, 0:512], pk.t[:, :], AF.Copy, [pk], [kf])
            px = pjn()
            MM(px.t[:, :], gwb.t[:, blk * 128:(blk + 1) * 128], grt.t[:], True, True, [gwb, grt], [px])
            sp = fbn()
            ACT(sp.t[:, 0:512], px.t[:, :], AF.Exp, [px, sml], [sp], scale=-1.0, bias=negb[:, blk:blk + 1])
            ACT(sp.t[:, 0:512], sp.t[:, 0:512], AF.Ln, [sp], [sp], bias=1.0)
            csb = fbn()
            if STOP < 2:
                continue
            scan(csb, sp)
            if STOP < 3:
                continue
            vd_front(blk, kf, csb, -1.0 / 16, s_qk, blk * 128, 0.125, qTb, kTb, kdTb)
        if STOP < 4 or 30 < STOP < 40:
            return
        vz_and_kdtok(2, (512, 1024), (1040, 1552), vTb, zGb, kdTb, kdtokb)
        if STOP < 5 or 40 < STOP < 50:
            return
        recur_vd(s_idx, "A", 2, 64, qTb, kTb, kdtokb, vTb, zGb, OFF_A, DEC_A, PR_NWA, 0)

    def mixer_C(s_idx):
        qTb, kTb, kdTb, vTb, zGb, kdtokb = BFB
        s_f = load_w(4632, 5144)
        s_q = load_w(4120, 4632) if P2 else None
        for blk in range(4):
            pf_ = proj_fm(s_f, blk * 128, 128)
            ff = fbn()
            ACT(ff.t[:, 0:512], pf_.t[:, :], AF.Sigmoid, [pf_], [ff])
            TS(ff.t[:, 0:512], ff.t[:, 0:512], omlb[:, blk:blk + 1], lbv[:, blk:blk + 1], ALU.mult, ALU.add,
               [ff, sml], [ff])
            kf = fbn()
            TS(kf.t[:, 0:512], ff.t[:, 0:512], -1.0, 1.0, ALU.mult, ALU.add, [ff], [kf])
            lf = fbn()
            ACT(lf.t[:, 0:512], ff.t[:, 0:512], AF.Ln, [ff], [lf])
            csb = fbn()
            scan(csb, lf)
            vd_front(blk, kf, csb, 1.0, s_q, blk * 128, 128 ** -0.5, qTb, kTb, kdTb)
        vz_and_kdtok(4, (5144, 5656), (5656, 6168), vTb, zGb, kdTb, kdtokb)
        recur_vd(s_idx, "C", 1, 128, qTb, kTb, kdtokb, vTb, zGb, OFF_C, DEC_C, PR_NWC, 2)

    def mixer_B(s_idx):
        qTb, kTb, ktokb, vTb, oGb, zGb = BFB
        if P2:
            s_q = load_w(1552, 2064)
            for blk in range(4):
                conv_block(s_idx, s_q, blk * 128, blk, PP_MCW + blk * 4, PP_MCB + blk, qTb.t[:, blk, 0:512], qTb)
        s_k = load_w(2064, 2576)
        for blk in range(4):
            conv_block(s_idx, s_k, blk * 128, 4 + blk, PP_MCW + (4 + blk) * 4, PP_MCB + 4 + blk,
                       kTb.t[:, blk, 0:512], kTb)
        s_v = load_w(2576, 3088)
        for t in range(TPS):
            bg, pgi = proj_small(16, 24, t)
            gi = gates_t[:, t * 16:t * 16 + 4]
            TT(gi, pgi[:, 0:4], ib2, ALU.add, [bg, sml], [GATES])
            ACT(gi, gi, AF.Exp, [GATES], [GATES])
            gf = gates_t[:, t * 16 + 4:t * 16 + 8]
            TT(gf, pgi[:, 4:8], pr.t[:, PR_FB:PR_FB + 4], ALU.add, [bg, pr], [GATES])
            ACT(gf, gf, AF.Exp, [GATES], [GATES], scale=-1.0)
            ACT(gf, gf, AF.Ln, [GATES], [GATES], bias=1.0)
            TS(gf, gf, -1.0, None, ALU.mult, None, [GATES], [GATES])
            pv = proj_tm(s_v, 512, t)
            vt4 = vTb.t[:, t, :].rearrange("p (h c) -> p h c", c=132)
            for h in range(4):
                TS(vt4[:, h, 0:128], pv.t[:, h * 128:(h + 1) * 128], gi[:, h:h + 1], None, ALU.mult, None,
                   [pv, GATES], [vTb])
            CP(vt4[:, :, 128], gi, [GATES], [vTb])
        if P2:
            s_o = load_w(3096, 3608)
            for t in range(TPS):
                po_ = proj_tm(s_o, 512, t)
                ACT(oGb.t[:, t, 0:512], po_.t[:, :], AF.Sigmoid, [po_], [oGb])
            s_z = load_w(3608, 4120)
            for t in range(TPS):
                pz = proj_tm(s_z, 512, t)
                ACT(zGb.t[:, t, 0:512], pz.t[:, :], AF.Silu, [pz], [zGb])
        for t in range(TPS):
            for h in range(4):
                pb, pap = ptsn()
                TR(pap, kTb.t[:, h, t * 128:(t + 1) * 128], identb.t[:], [kTb, identb], [pb])
                CP(ktokb.t[:, t, h * 128:(h + 1) * 128], pap, [pb], [ktokb])
        for t in range(TPS):
            tcs = slice(t * 128, (t + 1) * 128)
            vt4 = vTb.t[:, t, :].rearrange("p (h c) -> p h c", c=132)
            groups = []
            for h in range(4):
                groups.append(dict(qT=qTb.t[:, h, tcs], kT=kTb.t[:, h, tcs], qkbufs=[qTb, kTb],
                                   ktok=ktokb.t[:, t, h * 128:(h + 1) * 128], ktokbufs=[ktokb],
                                   v=vt4[:, h, :], vbuf=vTb, heads=[h], ncg=129,
                                   sf=stf_t[:, OFF_B + h * 132:OFF_B + (h + 1) * 132],
                                   sb=stb_t[:, OFF_B + h * 132:OFF_B + (h + 1) * 132],
                                   stf=STF["B"], stb=STB["B"], decoff=DEC_B + h))
            a_ap = gates_t[:, t * 16 + 4:t * 16 + 8]
            scalar_decay(t, 4, a_ap, GATES, groups, yb, lambda h: yb.t[:, h, 0:129], 129, 132)
            if P2:
                TS(rr4, yb.t[:, :, 128], -1.0, None, ALU.mult, None, [yb], [sml])
                TT(rr4, rr4, yb.t[:, :, 128], ALU.max, [yb, sml], [sml])
                TS(rr4, rr4, 1.0, None, ALU.max, None, [sml], [sml])
                S.op("vector", lambda e: e.reciprocal(out=rr4, in_=rr4), [sml], [sml])
                hb = nxt(TTB, "ttb")
                for h in range(4):
                    hc = slice(h * 128, (h + 1) * 128)
                    STT(hb.t[:, hc], yb.t[:, h, 0:128], rr4[:, h:h + 1], oGb.t[:, t, hc], ALU.mult, ALU.mult,
                        [yb, sml, oGb], [hb])
                for h in range(4):
                    hc = slice(h * 128, (h + 1) * 128)
                    ACT(junk.t[:, 0:128], hb.t[:, hc], AF.Square, [hb], [junk, sml], accum_out=ss4[:, h:h + 1])
                rstd_of(rs4[:, 0:4], ss4[:, 0:4], 128)
                nwz = nxt(TTB, "ttb")
                TT(nwz.t[:], pr.t[:, PR_NWB:PR_NWB + 512], zGb.t[:, t, 0:512], ALU.mult, [pr, zGb], [nwz])
                mix = nxt(MIXB, "mixb")
                for h in range(4):
                    hc = slice(h * 128, (h + 1) * 128)
                    STT(mix.t[:, hc], hb.t[:, hc], rs4[:, h:h + 1], nwz.t[:, hc], ALU.mult, ALU.mult,
                        [hb, sml, nwz], [mix])
                finish_mixer(s_idx, 1, t, mix)
        if P2:
            out_proj(s_idx, 1)

    def mixer_D(s_idx):
        bcTb, btokb, _, _, _, zGb = BFB
        s_x = load_w(6168, 6680)
        for blk in range(4):
            conv_block(s_idx, s_x, blk * 128, 8 + blk, PP_SCW + blk * 4, PP_SCB + blk, xst.t[:, blk, :], xst)
        s_bc = load_w(6680, 7192)
        for blk in (range(4) if P2 else range(2)):
            conv_block(s_idx, s_bc, blk * 128, 12 + blk, PP_SCW + (4 + blk) * 4, PP_SCB + 4 + blk,
                       bcTb.t[:, blk, 0:512], bcTb)
        if P2:
            s_z = load_w(7200, 7712)
            for t in range(TPS):
                pz = proj_tm(s_z, 512, t)
                ACT(zGb.t[:, t, 0:512], pz.t[:, :], AF.Silu, [pz], [zGb])
        for t in range(TPS):
            for g in range(2):
                pb, pap = ptsn()
                TR(pap, bcTb.t[:, g, t * 128:(t + 1) * 128], identb.t[:], [bcTb, identb], [pb])
                CP(btokb.t[:, t, g * 128:(g + 1) * 128], pap, [pb], [btokb])
        for t in range(TPS):
            tcs = slice(t * 128, (t + 1) * 128)
            bd, pdt = proj_small(24, 32, t)
            dtv = gates_t[:, 256 + t * 32:256 + t * 32 + 8]
            TT(dtv, pdt[:, 0:8], pr.t[:, PR_DTB:PR_DTB + 8], ALU.add, [bd, pr], [GATES])
            ACT(dtv, dtv, AF.Exp, [GATES], [GATES])
            ACT(dtv, dtv, AF.Ln, [GATES], [GATES], bias=1.0)
            av = gates_t[:, 256 + t * 32 + 8:256 + t * 32 + 16]
            TT(av, dtv, aneg, ALU.mult, [GATES, sml], [GATES])
            xs = nxt(TTB, "ttb")
            for blk in range(4):
                bx, pxr = pwn()
                S.op("tensor", lambda e, pxr=pxr, blk=blk, tcs=tcs: e.transpose(pxr[:, 0:128], xst.t[:, blk, tcs], identf.t[:]),
                     [xst, identf], [bx], cost=400)
                CP(xs.t[:, blk * 128:(blk + 1) * 128], pxr[:, 0:128], [bx], [xs])
            vt = nxt(VTD, "vtd")
            dtb = bass.AP(gates_t, 256 + t * 32, [[512, 128], [1, 8], [0, 64]])
            TT(vt.t[:, :].rearrange("p (h c) -> p h c", c=64), xs.t[:, :].rearrange("p (h c) -> p h c", c=64),
               dtb, ALU.mult, [xs, GATES], [vt])
            groups = []
            for g in range(2):
                groups.append(dict(qT=bcTb.t[:, 2 + g, tcs], kT=bcTb.t[:, g, tcs], qkbufs=[bcTb],
                                   ktok=btokb.t[:, t, g * 128:(g + 1) * 128], ktokbufs=[btokb],
                                   v=vt.t[:, g * 256:(g + 1) * 256], vbuf=vt, heads=[4 * g + j for j in range(4)],
                                   ncg=256, sf=stf_t[:, OFF_D + g * 256:OFF_D + (g + 1) * 256],
                                   sb=stb_t[:, OFF_D + g * 256:OFF_D + (g + 1) * 256],
                                   stf=STF["D"], stb=STB["D"], decoff=DEC_D + 4 * g))
            Y = nxt(TTB, "ttb")
            scalar_decay(t, 8, av, GATES, groups, Y, lambda h: Y.t[:, h * 64:(h + 1) * 64], 64, 64)
            if P2:
                for h in range(8):
                    hs = slice(h * 64, (h + 1) * 64)
                    STT(Y.t[:, hs], xs.t[:, hs], pr.t[:, PR_D + h:PR_D + h + 1], Y.t[:, hs], ALU.mult, ALU.add,
                        [xs, pr, Y], [Y])
                TT(Y.t[:], Y.t[:], zGb.t[:, t, 0:512], ALU.mult, [Y, zGb], [Y])
                for g in range(2):
                    ACT(junk.t[:, 0:256], Y.t[:, g * 256:(g + 1) * 256], AF.Square, [Y], [junk, sml],
                        accum_out=ssn[:, g:g + 1])
                rstd_of(rsn[:, 0:2], ssn[:, 0:2], 256)
                mix = nxt(MIXB, "mixb")
                for g in range(2):
                    gs = slice(g * 256, (g + 1) * 256)
                    STT(mix.t[:, gs], Y.t[:, gs], rsn[:, g:g + 1], pr.t[:, PR_NWD + g * 256:PR_NWD + (g + 1) * 256],
                        ALU.mult, ALU.mult, [Y, sml, pr], [mix])
                finish_mixer(s_idx, 3, t, mix)
        if P2:
            out_proj(s_idx, 3)

    finals = []
    for layer in range(NLAYERS):
        win3 = d_win_all[layer].rearrange("(k p) c -> p k c", p=128)
        wout3 = d_wout_all[layer].rearrange("(k p) c -> p k c", p=128)
        load_layer_params(layer)
        for P2 in (False, True):
            last = P2 and layer == NLAYERS - 1
            init_pass()
            for s_idx in range(NS):
                if s_idx == 0:
                    make_uT_tile(halo_h, halo_h.t[:], slice(0, 3), 3)
                else:
                    CP(uT.t[:, :, 0:3], uT.t[:, :, 512:515], [uT], [uT])
                for t in range(TPS):
                    tg = s_idx * TPS + t
                    make_uT_tile(HT[tg], hres_t[:, tg, :], tokc(t), 128)
                if "A" in MIXERS:
                    mixer_A(s_idx)
                if "B" in MIXERS:
                    mixer_B(s_idx)
                if "C" in MIXERS:
                    mixer_C(s_idx)
                if "D" in MIXERS:
                    mixer_D(s_idx)
            if not P2:
                end_p1()
            elif not last:
                exchange_halo()

    DMA("sync", nwbc.t[:], d_fnw.partition_broadcast(128), nwbc, [], [nwbc])
    for tg in range(NT):
        ACT(junk.t[:], hres_t[:, tg, :], AF.Square, [HT[tg]], [junk, sml], accum_out=ss4[:, 0:1])
        rstd_of(rs4[:, 0:1], ss4[:, 0:1], DM)
        for half in range(2):
            ob = nxt(TTB, "ttb")
            hs = slice(half * 512, (half + 1) * 512)
            STT(ob.t[:], hres_t[:, tg, hs], rs4[:, 0:1], nwbc.t[:, hs], ALU.mult, ALU.mult,
                [HT[tg], sml, nwbc], [ob])
            finals.append(DMA("sync", d_out[tg * 128:(tg + 1) * 128, hs], ob.t[:], ob, [ob], []))
    finals.extend(dbg_ids)
    S.wait_all("sync", finals)
    print("n semaphores", len(S.sems), flush=True)
    S.schedule(reorder=REORDER)
    print("ops:", len(S.ops), "model makespan us:", S.makespan / 1e3, flush=True)
    S.emit()
    S.close()
    es.close()
    return nc


def pack_layer(inp, l):
    f = lambda a: np.ascontiguousarray(np.asarray(a, dtype=np.float32))
    pp = np.zeros((128, NPP), np.float32)
    pp[:, PP_GB:PP_GB + 2] = f(inp["gla_gate_b"][l]).reshape(2, 128).T
    pp[:, PP_MCW:PP_MCW + 32] = f(inp["ml_conv_w"][l]).reshape(4, 8, 128).transpose(2, 1, 0).reshape(128, 32)
    pp[:, PP_MCB:PP_MCB + 8] = f(inp["ml_conv_b"][l]).reshape(8, 128).T
    pp[:, PP_SCW:PP_SCW + 32] = f(inp["ssd_conv_w"][l]).reshape(4, 8, 128).transpose(2, 1, 0).reshape(128, 32)
    pp[:, PP_SCB:PP_SCB + 8] = f(inp["ssd_conv_b"][l]).reshape(8, 128).T
    pp[:, PP_LB:PP_LB + 8] = f(inp["hg_lb_logits"]).reshape(2, 4, 128).transpose(2, 1, 0).reshape(128, 8)
    pr = np.concatenate([f(inp["ml_i_b"][l]), f(inp["ml_f_b"][l]), f(inp["ssd_dt_bias"][l]), f(inp["ssd_A_log"][l]),
                         f(inp["ssd_D"][l]), f(inp["gla_norm_w"][l]), f(inp["ml_norm_w"][l]), f(inp["hg_norm_w"][l]),
                         f(inp["ssd_norm_w"][l])]).astype(np.float32)
    assert pr.shape[0] == NPR
    return pp, pr


def core_pm(q):
    pm = np.zeros((128, 9), np.float32)
    for j in range(3):
        pm[:, j] = 1.0 if j < q else 0.0
        pm[:, 3 + j] = 1.0 - pm[:, j]
        pm[:, 6 + j] = 1.0 if j == q - 1 else 0.0
    return pm


_NC_CACHE = {}


def pack_inputs(inputs):
    f = lambda a: np.ascontiguousarray(np.asarray(a, dtype=np.float32))
    x = f(inputs["x"])
    pps, prs = zip(*[pack_layer(inputs, l) for l in range(2)])
    shared = dict(w_in=f(inputs["w_in"]), w_out=f(inputs["w_out"]), nw=f(inputs["norm_w"]),
                  fnw=f(inputs["final_norm_w"]), pp=np.ascontiguousarray(np.stack(pps)),
                  pr=np.ascontiguousarray(np.stack(prs)), gw=f(inputs["gla_gate_w"]))
    in_maps = []
    for c in range(NCORES):
        b, q = c // 4, c % 4
        d = dict(shared)
        d["hin"] = np.ascontiguousarray(x[b, q * T:(q + 1) * T, :])
        d["halo"] = (np.zeros((3, DM), np.float32) if q == 0
                     else np.ascontiguousarray(x[b, q * T - 3:q * T, :]))
        d["pm"] = core_pm(q)
        in_maps.append(d)
    return in_maps


def kernel(**inputs):
    if "nc" not in _NC_CACHE:
        _NC_CACHE["nc"] = build()
    in_maps = pack_inputs(inputs)
    res = run_bass_kernel_spmd(_NC_CACHE["nc"], in_maps, core_ids=list(range(NCORES)))
    B = np.asarray(inputs["x"]).shape[0]
    out = np.zeros((B, 4 * T, DM), np.float32)
    for c in range(NCORES):
        out[c // 4, (c % 4) * T:(c % 4 + 1) * T, :] = np.asarray(res.results[c]["hout"], dtype=np.float32)
    return out
```

```python
import math
import numpy as np
from contextlib import ExitStack
import concourse.bass as bass
import concourse.mybir as mybir
from concourse.bass_utils import run_bass_kernel_spmd

F32 = mybir.dt.float32
BF16 = mybir.dt.bfloat16
AF = mybir.ActivationFunctionType
ALU = mybir.AluOpType

NCORES = 8
SELF_DIST = 3
T = 2048
NT = 16
ST = 512
TPS = 4
NS = 4
DM = 1024
KC = 8
EPS = 1e-6
DPROJ = 7712
NSTATE = 1808
NDEC = 18
NSUM = NSTATE + NDEC
OFF_A, OFF_B, OFF_C, OFF_D = 0, 256, 784, 1296
DEC_A, DEC_B, DEC_C, DEC_D = 0, 2, 6, 10
NPP = 90
NPR = 2080
PP_GB, PP_MCW, PP_MCB, PP_SCW, PP_SCB, PP_LB = 0, 2, 34, 42, 74, 82
PR_IB, PR_FB, PR_DTB, PR_ALOG, PR_D, PR_NWA, PR_NWB, PR_NWC, PR_NWD = 0, 4, 8, 16, 24, 32, 544, 1056, 1568


class Buf:
    __slots__ = ("t", "name", "w", "r", "sem", "excl")

    def __init__(self, t, name, excl=False):
        self.t = t
        self.name = name
        self.w = None
        self.r = []
        self.sem = None
        self.excl = excl


class Sched:
    def __init__(self, nc):
        self.nc = nc
        self.engs = []
        self.sems = {}
        self._ctx = []
        self.nself = {"tensor"}
        self.ops = []

    def add_engine(self, name):
        key = "e_" + name
        self.sems[key] = self._alloc(key)
        self.engs.append(name)

    def _alloc(self, key):
        cm = self.nc.semaphore(key)
        h = cm.__enter__()
        self._ctx.append(cm)
        return h

    def dma_sem(self, name):
        key = "d_" + name
        self.sems[key] = self._alloc(key)
        return key

    def _record(self, eng, fn, kind, sem, reads, writes, cost, lat):
        ex = [r for r in reads if r.excl]
        if ex:
            writes = list(writes) + [r for r in ex if r not in writes]
            reads = [r for r in reads if not r.excl]
        deps = set()
        for r in reads:
            if r.w is not None:
                deps.add(r.w)
        for w in writes:
            if w.w is not None:
                deps.add(w.w)
            deps.update(w.r)
        oid = len(self.ops)
        self.ops.append(dict(id=oid, eng=eng, fn=fn, kind=kind, sem=sem, deps=deps, cost=cost, lat=lat))
        for r in reads:
            r.r.append(oid)
        for w in writes:
            w.w = oid
            w.r = []
        return oid

    def op(self, eng, fn, reads=(), writes=(), cost=300):
        return self._record(eng, fn, "op", "e_" + eng, reads, writes, cost, cost)

    def dma(self, eng, fn, semkey, reads=(), writes=(), cost=100, lat=4000):
        return self._record(eng, fn, "dma", semkey, reads, writes, cost, lat)

    def coll(self, eng, fn, semkey, reads=(), writes=()):
        return self._record(eng, fn, "coll", semkey, reads, writes, 2000, 40000)

    def wait_all(self, eng, ids):
        oid = len(self.ops)
        self.ops.append(dict(id=oid, eng=eng, fn=None, kind="wait", sem=None, deps=set(ids), cost=10, lat=10))
        return oid

    def schedule(self, reorder=True):
        import heapq
        ops = self.ops
        n = len(ops)
        ndeps = [len(o["deps"]) for o in ops]
        users = [[] for _ in range(n)]
        for o in ops:
            for d in o["deps"]:
                users[d].append(o["id"])
        finish = [0.0] * n
        ready_t = [0.0] * n
        efree = {e: 0.0 for e in self.engs}
        pending = {e: [] for e in self.engs}
        avail = {e: [] for e in self.engs}
        queues = {e: [] for e in self.engs}
        for o in ops:
            if ndeps[o["id"]] == 0:
                heapq.heappush(pending[o["eng"]], (0.0, o["id"]))
        done = 0
        if not reorder:
            for o in ops:
                queues[o["eng"]].append(o["id"])
            self.queues = queues
            self.makespan = 0
            return
        while done < n:
            best = None
            for e in self.engs:
                pe, av = pending[e], avail[e]
                while pe and pe[0][0] <= efree[e]:
                    _, i = heapq.heappop(pe)
                    heapq.heappush(av, i)
                if av:
                    cand = (efree[e], av[0], e, True)
                elif pe:
                    cand = (pe[0][0], pe[0][1], e, False)
                else:
                    continue
                if best is None or cand[:2] < best[:2]:
                    best = cand
            assert best is not None, "scheduler deadlock"
            st, i, e, from_av = best
            if from_av:
                heapq.heappop(avail[e])
            else:
                heapq.heappop(pending[e])
            o = ops[i]
            efree[e] = st + o["cost"]
            finish[i] = st + o["lat"]
            queues[e].append(i)
            done += 1
            for u in users[i]:
                ndeps[u] -= 1
                if finish[i] > ready_t[u]:
                    ready_t[u] = finish[i]
                if ndeps[u] == 0:
                    heapq.heappush(pending[ops[u]["eng"]], (ready_t[u], u))
        self.queues = queues
        self.makespan = max(finish)

    def emit(self):
        nc = self.nc
        ops = self.ops
        semval = {}
        cnt = {}
        for e in self.engs:
            for i in self.queues[e]:
                o = ops[i]
                if o["kind"] == "op":
                    cnt[o["sem"]] = cnt.get(o["sem"], 0) + 1
                    semval[i] = (o["sem"], cnt[o["sem"]], 1)
        for o in ops:
            if o["kind"] in ("dma", "coll"):
                inc = 16 if o["kind"] == "dma" else 1
                cnt[o["sem"]] = cnt.get(o["sem"], 0) + inc
                semval[o["id"]] = (o["sem"], cnt[o["sem"]], inc)
        with nc.Block() as block:
            for e in self.engs:
                deco = getattr(block, e)
                q = self.queues[e]
                own = "e_" + e

                def body(h, q=q, e=e, own=own):
                    seen = {}
                    for i in q:
                        o = ops[i]
                        need = {}
                        for d in o["deps"]:
                            k, v, _ = semval[d]
                            if k == own and e in self.nself:
                                continue
                            if k == own and ops[d]["kind"] == "op" and i in semval and semval[i][0] == own \
                                    and semval[i][1] - v >= SELF_DIST:
                                continue
                            if need.get(k, 0) < v:
                                need[k] = v
                        for k, v in need.items():
                            if seen.get(k, 0) >= v:
                                continue
                            seen[k] = v
                            h.wait_ge(self.sems[k], v)
                        if o["fn"] is not None:
                            k, v, inc = semval[i]
                            o["fn"](h).then_inc(self.sems[k], inc)
                deco(body)

    def close(self):
        for cm in reversed(self._ctx):
            cm.__exit__(None, None, None)


def build(debug=False, MIXERS="ABCD", STOP=99, NLAYERS=2, REORDER=True):
    nc = bass.Bass("TRN2", target_bir_lowering=False)
    es = ExitStack()
    S = Sched(nc)
    for n in ("sync", "gpsimd", "tensor", "vector", "scalar"):
        S.add_engine(n)

    def dram(name, shape, dt=F32, kind="ExternalInput"):
        return nc.dram_tensor(name, list(shape), dt, kind=kind).ap()

    layer = 0
    P2 = False
    last = False

    d_hin = dram("hin", [T, DM])
    d_halo = dram("halo", [3, DM])
    d_win_all = dram("w_in", [2, DM, DPROJ])
    d_wout_all = dram("w_out", [2, 2048, DM])
    d_nw_all = dram("nw", [2, DM])
    d_fnw = dram("fnw", [DM])
    d_pp_all = dram("pp", [2, 128, NPP])
    d_pr_all = dram("pr", [2, NPR])
    d_gw_all = dram("gw", [2, 16, 256])
    d_pm = dram("pm", [128, 9])
    d_out = dram("hout", [T, DM], kind="ExternalOutput")
    if debug:
        d_dbg = dram("dbg", [T, 2048], BF16, kind="ExternalOutput")
    d_sloc = [nc.dram_tensor(f"sloc{l}", [128, NSUM], F32).ap() for l in range(2)]
    d_sgat = [nc.dram_tensor(f"sgat{l}", [4 * 128, NSUM], F32).ap() for l in range(2)]
    d_hloc = nc.dram_tensor("hloc", [3, DM], F32).ap()
    d_hgat = nc.dram_tensor("hgat", [12, DM], F32).ap()
    GROUPS = [[0, 1, 2, 3], [4, 5, 6, 7]]

    cnt = [0]
    dbg_ids = []

    def sb(shape, dt, name=None):
        cnt[0] += 1
        name = "s_" + (name or f"sb{cnt[0]}")
        t = es.enter_context(nc.sbuf_tensor(name, list(shape), dt))
        return Buf(t, name)

    def ps(shape, dt, name):
        return es.enter_context(nc.psum_tensor(name, list(shape), dt))

    def withsem(b):
        b.sem = S.dma_sem(b.name)
        return b

    def fsz(ap):
        n = 1
        for (_, c) in list(ap.ap)[1:]:
            n *= c
        return n

    def is_psum(ap):
        return "PSum" in type(ap.tensor).__name__

    def ACT(out, in_, func, R, W, **kw):
        c = 200 + 0.85 * fsz(out)
        return S.op("scalar", lambda e: e.activation(out=out, in_=in_, func=func, **kw), R, W, cost=c)

    def _dve_cost(out, ins):
        c = 70 + 1.05 * fsz(out)
        if any(is_psum(a) for a in ins):
            c += 60
        return c

    def TT(out, in0, in1, op, R, W, eng="vector"):
        return S.op(eng, lambda e: e.tensor_tensor(out=out, in0=in0, in1=in1, op=op), R, W,
                    cost=_dve_cost(out, [in0, in1]))

    def TS(out, in0, s1, s2, op0, op1, R, W, eng="vector"):
        c = _dve_cost(out, [in0])
        if s2 is None:
            return S.op(eng, lambda e: e.tensor_scalar(out=out, in0=in0, scalar1=s1, scalar2=None, op0=op0), R, W, cost=c)
        return S.op(eng, lambda e: e.tensor_scalar(out=out, in0=in0, scalar1=s1, scalar2=s2, op0=op0, op1=op1), R, W, cost=c)

    def STT(out, in0, scalar, in1, op0, op1, R, W, eng="vector"):
        return S.op(eng, lambda e: e.scalar_tensor_tensor(out=out, in0=in0, scalar=scalar, in1=in1, op0=op0, op1=op1), R, W,
                    cost=_dve_cost(out, [in0, in1]))

    def CP(out, in_, R, W, eng="vector"):
        return S.op(eng, lambda e: e.tensor_copy(out=out, in_=in_), R, W, cost=_dve_cost(out, [in_]))

    def MSET(ap, val, W, eng="gpsimd"):
        return S.op(eng, lambda e: e.memset(ap, val), (), W, cost=200 + fsz(ap))

    def MM(out, lhsT, rhs, start, stop, R, W):
        n = fsz(rhs)
        f32 = "float32" in str(rhs.tensor.dtype)
        c = (64 + 1.7 * n) if f32 else (64 + 0.45 * n)
        return S.op("tensor", lambda e: e.matmul(out, lhsT=lhsT, rhs=rhs, start=start, stop=stop), R, W, cost=c)

    def TR(out, in_, ident, R, W):
        return S.op("tensor", lambda e: e.transpose(out, in_, ident), R, W, cost=150)

    def DMA(q, out, in_, buf_sem, R, W):
        nbytes = fsz(out) * 128 * 4
        issue = 1000 if q == "gpsimd" else 80
        return S.dma(q, lambda e: e.dma_start(out=out, in_=in_), buf_sem.sem, R, W, cost=issue,
                     lat=issue + 2000 + nbytes / 120.0)

    identf = sb([128, 128], F32, "identf")
    identb = sb([128, 128], BF16, "identb")
    trif = sb([128, 128], F32, "trif")
    suf = sb([128, 128], F32, "suf")
    negb_ = sb([128, 128], BF16, "negmask")
    onesf = sb([128, 128], F32, "onesf")
    rst = sb([128, 512], F32, "rst")
    MSET(onesf.t[:], 1.0, [onesf])
    MSET(identf.t[:], 1.0, [identf])
    S.op("gpsimd", lambda e: e.affine_select(out=identf.t[:], in_=identf.t[:], pattern=[[-1, 128]],
                                             compare_op=ALU.is_equal, fill=0.0, base=0, channel_multiplier=1),
         [identf], [identf])
    CP(identb.t[:], identf.t[:], [identf], [identb])
    MSET(trif.t[:], 1.0, [trif])
    S.op("gpsimd", lambda e: e.affine_select(out=trif.t[:], in_=trif.t[:], pattern=[[1, 128]],
                                             compare_op=ALU.is_ge, fill=0.0, base=0, channel_multiplier=-1),
         [trif], [trif])
    TS(suf.t[:], trif.t[:], -1.0, 1.0, ALU.mult, ALU.add, [trif], [suf])
    TS(negb_.t[:], suf.t[:], -30000.0, None, ALU.mult, None, [suf], [negb_])
    MSET(rst.t[:], 1.0, [rst])
    rst3 = rst.t[:, :].rearrange("p (c k) -> p c k", k=64)
    MSET(rst3[:, :, 0:1], 0.0, [rst])

    hres_t = es.enter_context(nc.sbuf_tensor("hres", [128, NT, DM], F32))
    HT = [withsem(Buf(hres_t, f"ht{t}")) for t in range(NT)]
    nwbc = withsem(sb([128, DM], F32, "nwbc"))
    uT = sb([128, KC, 515], BF16, "uT")
    ubf = sb([128, DM], BF16, "ubf")
    junk = sb([128, DM], BF16, "junk")
    NSLOT = 2
    slots = [withsem(sb([128, 4096], BF16, f"wslot{i}")) for i in range(NSLOT)]
    wsm = withsem(sb([128, KC, 32], BF16, "wsm"))
    pp = withsem(sb([128, NPP], F32, "pp"))
    pr = withsem(sb([128, NPR], F32, "pr"))
    gwf = withsem(sb([16, 256], F32, "gwf"))
    gwb = sb([16, 256], BF16, "gwb")
    grt = sb([16, 512], BF16, "grt")
    pm = withsem(sb([128, 9], F32, "pm"))
    stf_t = es.enter_context(nc.sbuf_tensor("stf", [128, NSUM], F32))
    stb_t = es.enter_context(nc.sbuf_tensor("stb", [128, NSTATE], BF16))
    STF = {m: Buf(stf_t, "stf" + m) for m in "ABCD"}
    STB = {m: Buf(stb_t, "stb" + m) for m in "ABCD"}
    STDEC = Buf(stf_t, "stdec")
    stsem = withsem(Buf(stf_t, "stout"))
    halo_raw = sb([128, 16, 3], F32, "halo_raw")
    BFB = [sb([128, TPS, 528], BF16, f"bfb{i}") for i in range(6)]
    FB = [sb([128, 515], F32, f"fb{i}") for i in range(6)]
    TTB = [sb([128, 512], F32, f"ttb{i}") for i in range(4)]
    SMF = [sb([128, 128], F32, f"smf{i}") for i in range(4)]
    SMB = [sb([128, 128], BF16, f"smb{i}") for i in range(4)]
    VDB = [sb([128, 512], BF16, f"vdb{i}") for i in range(2)]
    MIXB = [sb([128, 512], BF16, f"mix{i}") for i in range(2)]
    mixT = sb([128, 4, ST], BF16, "mixT")
    sml = sb([128, 512], F32, "sml")
    SML = {}
    _smo = [0]

    def small(name, n):
        o = _smo[0]
        _smo[0] += n
        assert _smo[0] <= 512
        SML[name] = (o, n)
        return sml.t[:, o:o + n]

    gates_t = es.enter_context(nc.sbuf_tensor("gates", [128, 512], F32))
    GATES = Buf(gates_t, "gates")
    egl_t = es.enter_context(nc.sbuf_tensor("egl", [128, 4, 8], F32))
    EGL = Buf(egl_t, "egl")
    halo_h = withsem(sb([128, DM], F32, "halo_h"))
    cmb = withsem(sb([128, NSUM], F32, "cmb"))
    for i in range(4):
        withsem(TTB[i])

    pj_t = [ps([128, 512], F32, f"pj{i}") for i in range(2)]
    PJ = [Buf(t, f"pj{i}", excl=True) for i, t in enumerate(pj_t)]
    ptu_t = ps([128, 1024], BF16, "ptu")
    PTU = Buf(ptu_t, "ptu", excl=True)
    pts_t = ps([128, 1024], BF16, "pts")
    PTSB = Buf(pts_t, "pts", excl=True)
    PTS = []
    for i in range(8):
        PTS.append((PTSB, i * 128))
        PTS.append((PTU, i * 128))
    pf_t = [ps([128, 512], F32, f"pf{i}") for i in range(2)]
    PFB = [Buf(pf_t[i], f"pf{i}", excl=True) for i in range(2)]
    PF = [(PFB[i % 2], (i // 2) * 128) for i in range(8)]
    pw_t = [ps([128, 512], F32, f"pw{i}") for i in range(2)]
    PWB = [Buf(pw_t[i], f"pw{i}", excl=True) for i in range(2)]
    PW = [(PWB[i % 2], (i // 2) * 256) for i in range(4)]
    rot = {}

    def nxt(pool, key):
        i = rot.get(key, 0)
        rot[key] = i + 1
        return pool[i % len(pool)]

    def pjn():
        return nxt(PJ, "pj")

    def pfn():
        b, o = nxt(PF, "pf")
        return b, b.t[:, o:o + 128]

    def pwn():
        b, o = nxt(PW, "pw")
        return b, b.t[:, o:o + 256]

    def ptsn():
        b, o = nxt(PTS, "pts")
        return b, b.t[:, o:o + 128]

    def fbn():
        return nxt(FB, "fb")

    def slot3(s):
        return s.t[:, :].rearrange("p (k c) -> p k c", k=KC)

    def slot_wo(s):
        return s.t[:, :].rearrange("p (k c) -> p k c", k=4)

    win3 = None
    wout3 = None

    wcache = {}
    d_wscr = nc.dram_tensor("wscr", [48, 128, 4096], BF16).ap()

    def _load_cached(key, slot_view, src_ap, n):
        s = nxt(slots, "slot")
        dst = slot_view(s)
        if key not in wcache:
            scr = d_wscr[len(wcache)]
            sbuf = withsem(Buf(None, "wscr%d" % len(wcache)))
            wcache[key] = (scr, sbuf)
            DMA("gpsimd", dst, src_ap, s, [], [s])
            DMA("sync", scr, s.t[:, :], sbuf, [s], [sbuf])
        else:
            scr, sbuf = wcache[key]
            DMA("sync", s.t[:, :], scr, s, [sbuf], [s])
        return s

    def load_w(c0, c1):
        n = c1 - c0
        return _load_cached((layer, "i", c0, c1), lambda s: slot3(s)[:, :, 0:n], win3[:, :, c0:c1], n)

    def load_wo(m):
        return _load_cached((layer, "o", m), lambda s: slot_wo(s)[:, :, :], wout3[:, m * 4:(m + 1) * 4, :], 4096)

    DMA("sync", pm.t[:], d_pm, pm, [], [pm])
    for t in range(NT):
        DMA("sync", hres_t[:, t, :], d_hin[t * 128:(t + 1) * 128, :], HT[t], [], [HT[t]])
    MSET(halo_h.t[:], 0.0, [halo_h])
    DMA("sync", halo_h.t[0:3, :], d_halo, halo_h, [], [halo_h])

    negb = small("negb", 2)
    ib2 = small("ib2", 4)
    aneg = small("aneg", 8)
    lbv = small("lbv", 4)
    omlb = small("omlb", 4)
    ss4 = small("ss4", 4)
    rs4 = small("rs4", 4)
    rr4 = small("rr4", 4)
    ssn = small("ssn", 2)
    rsn = small("rsn", 2)
    decp = small("decp", NDEC)
    acs_s = small("acs", 8)
    ee_s = small("ee", 8)
    etot_s = small("etot", 8)
    dec_s = small("dec", 8)
    dd_s = small("dd", 8)
    allst = list(STF.values()) + [STDEC]
    SLOC = [withsem(Buf(None, f"sloc{l}")) for l in range(2)]
    SGAT = [withsem(Buf(None, f"sgat{l}")) for l in range(2)]
    HLOC = withsem(Buf(None, "hloc"))
    HGAT = withsem(Buf(None, "hgat"))

    def load_layer_params(l):
        DMA("sync", nwbc.t[:], d_nw_all[l].partition_broadcast(128), nwbc, [], [nwbc])
        DMA("sync", pp.t[:], d_pp_all[l], pp, [], [pp])
        DMA("sync", pr.t[:], d_pr_all[l].partition_broadcast(128), pr, [], [pr])
        DMA("sync", gwf.t[:], d_gw_all[l], gwf, [], [gwf])
        CP(gwb.t[:], gwf.t[:], [gwf], [gwb])
        DMA("gpsimd", wsm.t[:, :, 0:16], win3[:, :, 1024:1040], wsm, [], [wsm])
        DMA("gpsimd", wsm.t[:, :, 16:24], win3[:, :, 3088:3096], wsm, [], [wsm])
        DMA("gpsimd", wsm.t[:, :, 24:32], win3[:, :, 7192:7200], wsm, [], [wsm])
        TS(negb, pp.t[:, PP_GB:PP_GB + 2], -1.0, None, ALU.mult, None, [pp], [sml])
        TS(ib2, pr.t[:, PR_IB:PR_IB + 4], math.log(128 ** -0.5), None, ALU.add, None, [pr], [sml])
        ACT(aneg, pr.t[:, PR_ALOG:PR_ALOG + 8], AF.Exp, [pr], [sml])
        TS(aneg, aneg, -1.0, None, ALU.mult, None, [sml], [sml])
        lg3 = pp.t[:, PP_LB:PP_LB + 8].rearrange("p (b l) -> p b l", l=2)
        if l == 0:
            MSET(lbv, 0.0, [sml], eng="vector")
        else:
            TT(lbv, lg3[:, :, 1], lg3[:, :, 0], ALU.subtract, [pp], [sml])
            ACT(lbv, lbv, AF.Sigmoid, [sml], [sml])
        TS(omlb, lbv, -1.0, 1.0, ALU.mult, ALU.add, [sml], [sml])

    cgroups = []
    for blk in range(2):
        cgroups.append((OFF_A + blk * 128, 128, DEC_A + blk))
    for h in range(4):
        cgroups.append((OFF_B + h * 132, 132, DEC_B + h))
    for h in range(4):
        cgroups.append((OFF_C + h * 128, 128, DEC_C + h))
    for h in range(8):
        cgroups.append((OFF_D + h * 64, 64, DEC_D + h))

    def init_pass():
        MSET(stf_t[:, 0:NSTATE], 0.0, allst, eng="vector")
        if not P2:
            MSET(stf_t[:, NSTATE:NSUM], 1.0, allst, eng="vector")
        else:
            for j in range(3):
                DMA("sync", cmb.t[:], d_sgat[layer][j * 128:(j + 1) * 128, :], cmb, [SGAT[layer]], [cmb])
                TS(decp, cmb.t[:, NSTATE:NSUM], pm.t[:, j:j + 1], pm.t[:, 3 + j:4 + j], ALU.mult, ALU.add,
                   [cmb, pm], [sml])
                TS(cmb.t[:, 0:NSTATE], cmb.t[:, 0:NSTATE], pm.t[:, j:j + 1], None, ALU.mult, None, [cmb, pm], [cmb])
                for (o, n, g) in cgroups:
                    STT(stf_t[:, o:o + n], stf_t[:, o:o + n], decp[:, g:g + 1], cmb.t[:, o:o + n], ALU.mult, ALU.add,
                        allst + [cmb, sml], allst)
        CP(stb_t[:, :], stf_t[:, 0:NSTATE], allst, list(STB.values()))

    def end_p1():
        l = layer
        DMA("sync", d_sloc[l], stf_t[:, :], SLOC[l], allst, [SLOC[l]])
        S.coll("gpsimd", lambda e: e.collective_compute("AllGather", ALU.bypass, replica_groups=GROUPS,
                                                        ins=[d_sloc[l]], outs=[d_sgat[l]]),
               SGAT[l].sem, [SLOC[l]], [SGAT[l]])

    def exchange_halo():
        DMA("sync", d_hloc, hres_t[125:128, NT - 1, :], HLOC, [HT[NT - 1]], [HLOC])
        S.coll("gpsimd", lambda e: e.collective_compute("AllGather", ALU.bypass, replica_groups=GROUPS,
                                                        ins=[d_hloc], outs=[d_hgat]),
               HGAT.sem, [HLOC], [HGAT])
        MSET(halo_h.t[:], 0.0, [halo_h])
        for j in range(3):
            for half in range(2):
                tb = nxt(TTB, "ttb")
                hs = slice(half * 512, (half + 1) * 512)
                DMA("sync", tb.t[0:3, :], d_hgat[j * 3:(j + 1) * 3, hs], tb, [HGAT], [tb])
                STT(halo_h.t[0:3, hs], tb.t[0:3, :], pm.t[0:3, 6 + j:7 + j], halo_h.t[0:3, hs], ALU.mult, ALU.add,
                    [tb, pm, halo_h], [halo_h])

    def tokc(t):
        return slice(3 + t * 128, 3 + (t + 1) * 128)

    def rstd_of(out_ap, ss_ap, n):
        TS(out_ap, ss_ap, 1.0 / n, EPS, ALU.mult, ALU.add, [sml], [sml])
        ACT(out_ap, out_ap, AF.Ln, [sml], [sml])
        ACT(out_ap, out_ap, AF.Exp, [sml], [sml], scale=-0.5)

    def make_uT_tile(hbuf, h_ap, dst_cols, ncol):
        ACT(junk.t[:], h_ap, AF.Square, [hbuf], [junk, sml], accum_out=ss4[:, 0:1])
        rstd_of(rs4[:, 0:1], ss4[:, 0:1], DM)
        STT(ubf.t[:], h_ap, rs4[:, 0:1], nwbc.t[:], ALU.mult, ALU.mult, [hbuf, sml, nwbc], [ubf])
        for k in range(KC):
            TR(ptu_t[:, k * 128:(k + 1) * 128], ubf.t[:, k * 128:(k + 1) * 128], identb.t[:], [ubf, identb], [PTU])
        src = ptu_t[:, :].rearrange("p (k c) -> p k c", k=KC)[:, :, 0:ncol]
        CP(uT.t[:, :, dst_cols], src, [PTU], [uT])

    def proj_fm(slot, off, M, cols=slice(3, 515), n=512):
        pj = pjn()
        v = slot3(slot)
        for k in range(KC):
            MM(pj.t[0:M, 0:n], v[:, k, off:off + M], uT.t[:, k, cols], k == 0, k == KC - 1, [slot, uT], [pj])
        return pj

    def proj_tm(slot, ncols, t):
        pj = pjn()
        v = slot3(slot)
        for k in range(KC):
            MM(pj.t[:, 0:ncols], uT.t[:, k, tokc(t)], v[:, k, 0:ncols], k == 0, k == KC - 1, [slot, uT], [pj])
        return pj

    def proj_small(c0, c1, t):
        b, ap = pfn()
        n = c1 - c0
        for k in range(KC):
            MM(ap[:, 0:n], uT.t[:, k, tokc(t)], wsm.t[:, k, c0:c1], k == 0, k == KC - 1, [wsm, uT], [b])
        return b, ap

    def conv_block(s_idx, slot, off, hidx, wcol, bcol, out_ap, out_buf):
        pj = proj_fm(slot, off, 128)
        raw = fbn()
        if s_idx == 0:
            b, ap = pfn()
            v = slot3(slot)
            for k in range(KC):
                MM(ap[:, 0:3], v[:, k, off:off + 128], uT.t[:, k, 0:3], k == 0, k == KC - 1, [slot, uT], [b])
            CP(raw.t[:, 0:3], ap[:, 0:3], [b], [raw])
        else:
            CP(raw.t[:, 0:3], halo_raw.t[:, hidx, :], [halo_raw], [raw])
        ACT(raw.t[:, 3:515], pj.t[:, :], AF.Copy, [pj], [raw])
        CP(halo_raw.t[:, hidx, :], raw.t[:, 512:515], [raw], [halo_raw])
        acc = fbn()
        w = lambda k: pp.t[:, wcol + k:wcol + k + 1]
        ACT(acc.t[:, 0:512], raw.t[:, 0:512], AF.Identity, [raw, pp], [acc], scale=w(0), bias=pp.t[:, bcol:bcol + 1])
        for k in (1, 2, 3):
            STT(acc.t[:, 0:512], raw.t[:, k:k + 512], w(k), acc.t[:, 0:512], ALU.mult, ALU.add, [raw, pp, acc], [acc])
        ACT(out_ap, acc.t[:, 0:512], AF.Silu, [acc], [out_buf])

    def finish_mixer(s_idx, m, t, mix):
        if debug:
            tg = s_idx * TPS + t
            dbg_ids.append(S.dma("sync", lambda e: e.dma_start(out=d_dbg[tg * 128:(tg + 1) * 128, m * 512:(m + 1) * 512], in_=mix.t[:]),
                                 mix.sem, [mix], []))
        for b4 in range(4):
            pb, pap = ptsn()
            TR(pap, mix.t[:, b4 * 128:(b4 + 1) * 128], identb.t[:], [mix, identb], [pb])
            ACT(mixT.t[:, b4, t * 128:(t + 1) * 128], pap, AF.Copy, [pb], [mixT])

    def out_proj(s_idx, m):
        s = load_wo(m)
        v = slot_wo(s)
        for t in range(TPS):
            tg = s_idx * TPS + t
            for half in range(2):
                pj = pjn()
                for kc in range(4):
                    MM(pj.t[:, :], mixT.t[:, kc, t * 128:(t + 1) * 128], v[:, kc, half * 512:(half + 1) * 512],
                       kc == 0, kc == 3, [s, mixT], [pj])
                hap = hres_t[:, tg, half * 512:(half + 1) * 512]
                TT(hap, hap, pj.t[:, :], ALU.add, [HT[tg], pj], [HT[tg]])

    def scalar_decay(t, nh, a_ap, a_buf, groups, ybuf, yap, dvh, pad):
        bpa, pa = pfn()
        MM(pa[:, 0:nh], trif.t[:], a_ap, True, True, [trif, a_buf], [bpa])
        MM(pa[:, 64:64 + nh], onesf.t[:], a_ap, True, True, [onesf, a_buf], [bpa])
        CP(acs_s[:, 0:nh], pa[:, 0:nh], [bpa], [sml])
        if P2:
            ACT(ee_s[:, 0:nh], pa[:, 0:nh], AF.Exp, [bpa], [sml])
        ACT(etot_s[:, 0:nh], pa[:, 64:64 + nh], AF.Exp, [bpa], [sml])
        TT(dd_s[:, 0:nh], pa[:, 64:64 + nh], acs_s[:, 0:nh], ALU.subtract, [bpa, sml], [sml])
        ACT(dec_s[:, 0:nh], dd_s[:, 0:nh], AF.Exp, [sml], [sml])
        for g in groups:
            ncg = g["ncg"]
            heads = g["heads"]
            if P2:
                bqk, pqk = pfn()
                MM(pqk, g["kT"], g["qT"], True, True, g["qkbufs"], [bqk])
                byo, pyo = pwn()
                MM(pyo[:, 0:ncg], g["qT"], g["sb"][:, 0:ncg], True, True, g["qkbufs"] + [g["stb"]], [byo])
                byd, pyd = pwn()
            vd = nxt(VDB, "vdb")
            for j, h in enumerate(heads):
                cs = slice(j * pad, j * pad + dvh)
                if P2:
                    lh = nxt(SMF, "smf")
                    ACT(lh.t[:], suf.t[:], AF.Identity, [suf, a_buf], [lh], scale=a_ap[:, h:h + 1])
                    bsg, psg = pfn()
                    MM(psg, lh.t[:], trif.t[:], True, False, [lh, trif], [bsg])
                    MM(psg, identb.t[:], negb_.t[:], False, True, [identb, negb_], [bsg])
                    esg = nxt(SMF, "smf")
                    ACT(esg.t[:], psg, AF.Exp, [bsg], [esg])
                    wb = nxt(SMB, "smb")
                    TT(wb.t[:], pqk, esg.t[:], ALU.mult, [bqk, esg], [wb])
                    MM(pyd[:, cs], wb.t[:], g["v"][:, cs], True, True, [wb, g["vbuf"]], [byd])
                if len(heads) == 1:
                    TS(vd.t[:, cs], g["v"][:, cs], dec_s[:, h:h + 1], None, ALU.mult, None, [g["vbuf"], sml], [vd])
            if len(heads) > 1:
                nhh = len(heads)
                so = SML["dec"][0] + heads[0]
                decb = bass.AP(sml.t, so, [[512, 128], [1, nhh], [0, dvh]])
                TT(vd.t[:, 0:ncg].rearrange("p (h c) -> p h c", c=dvh),
                   g["v"][:, 0:ncg].rearrange("p (h c) -> p h c", c=dvh), decb, ALU.mult, [g["vbuf"], sml], [vd])
            bu, pu = pwn()
            MM(pu[:, 0:ncg], g["ktok"], vd.t[:, 0:ncg], True, True, g["ktokbufs"] + [vd], [bu])
            if P2:
                for j, h in enumerate(heads):
                    cs = slice(j * pad, j * pad + dvh)
                    ACT(yap(h), pyd[:, cs], AF.Copy, [byd], [ybuf])
                    STT(yap(h), pyo[:, cs], ee_s[:, h:h + 1], yap(h), ALU.mult, ALU.add, [byo, sml, ybuf], [ybuf])
            for j, h in enumerate(heads):
                cs = slice(j * pad, j * pad + dvh)
                STT(g["sf"][:, cs], g["sf"][:, cs], etot_s[:, h:h + 1], pu[:, cs], ALU.mult, ALU.add,
                    [g["stf"], sml, bu], [g["stf"]])
                if not P2:
                    dc = stf_t[:, NSTATE + g["decoff"] + j:NSTATE + g["decoff"] + j + 1]
                    TT(dc, dc, etot_s[:, h:h + 1], ALU.mult, [STDEC, sml], [STDEC])
            ACT(g["sb"], g["sf"], AF.Copy, [g["stf"]], [g["stb"]])


    xst = sb([128, 4, 512], F32, "xst")
    yb = sb([128, 4, 132], F32, "yb")
    VTD = [sb([128, 512], BF16, f"vtd{i}") for i in range(2)]
    DBGS = [S.dma_sem(f"dbg{i}") for i in range(2)] if debug else None
    for i, mb in enumerate(MIXB):
        mb.sem = DBGS[i] if debug else None

    def recur_vd(s_idx, m, hpb, dk, qTb, kTb, kdtokb, vTb, zGb, soff, doff, nwoff, midx):
        for t in range(TPS):
            pouts = []
            for h in range(4):
                blk = h // hpb
                r0 = (h % hpb) * dk
                rows = slice(r0, r0 + dk)
                hc = slice(h * 128, (h + 1) * 128)
                scol = soff + blk * 128
                sf = stf_t[rows, scol:scol + 128]
                sbf = stb_t[rows, scol:scol + 128]
                tcs = slice(t * 128, (t + 1) * 128)
                if P2:
                    bsc, psc = pfn()
                    MM(psc, kTb.t[rows, blk, tcs], qTb.t[rows, blk, tcs], True, True, [kTb, qTb], [bsc])
                    sm = nxt(SMB, "smb")
                    TT(sm.t[:], psc, trif.t[:], ALU.mult, [bsc, trif], [sm])
                    bo, po = pwn()
                for c in range(2):
                    tr = slice(c * 64, (c + 1) * 64)
                    if P2:
                        MM(po[tr, 0:128], sm.t[tr, tr], vTb.t[tr, t, hc], True, False, [sm, vTb], [bo])
                        MM(po[tr, 0:128], qTb.t[rows, blk, t * 128 + c * 64:t * 128 + (c + 1) * 64], sbf,
                           False, True, [qTb, STB[m]], [bo])
                    bu, pu = pfn()
                    MM(pu[rows, 0:128], kdtokb.t[tr, t, blk * 128 + r0:blk * 128 + r0 + dk], vTb.t[tr, t, hc],
                       True, True, [kdtokb, vTb], [bu])
                    eg = egl_t[rows, blk, t * 2 + c:t * 2 + c + 1]
                    STT(sf, sf, eg, pu[rows, 0:128], ALU.mult, ALU.add, [STF[m], EGL, bu], [STF[m]])
                    ACT(sbf, sf, AF.Copy, [STF[m]], [STB[m]])
                    if not P2:
                        dc = stf_t[rows, NSTATE + doff + blk:NSTATE + doff + blk + 1]
                        TT(dc, dc, eg, ALU.mult, [STDEC, EGL], [STDEC])
                if P2:
                    ACT(junk.t[:, 0:128], po[:, 0:128], AF.Square, [bo], [junk, sml], accum_out=ss4[:, h:h + 1])
                    pouts.append((bo, po))
            if P2:
                rstd_of(rs4[:, 0:4], ss4[:, 0:4], 128)
                nwz = nxt(TTB, "ttb")
                TT(nwz.t[:], pr.t[:, nwoff:nwoff + 512], zGb.t[:, t, 0:512], ALU.mult, [pr, zGb], [nwz])
                mix = nxt(MIXB, "mixb")
                for h in range(4):
                    hc = slice(h * 128, (h + 1) * 128)
                    STT(mix.t[:, hc], pouts[h][1][:, 0:128], rs4[:, h:h + 1], nwz.t[:, hc], ALU.mult, ALU.mult,
                        [pouts[h][0], sml, nwz], [mix])
                finish_mixer(s_idx, midx, t, mix)
        if P2:
            out_proj(s_idx, midx)

    def vd_front(blk, kf, csb, sG, qslot, qoff, qscale, qTb, kTb, kdTb):
        ACT(egl_t[:, blk, :], csb.t[:, 63:512:64], AF.Exp, [csb], [EGL], scale=sG)
        tmp = fbn()
        if STOP == 31:
            return
        if P2:
            qf = fbn()
            pq = proj_fm(qslot, qoff, 128)
            ACT(qf.t[:, 0:512], pq.t[:, :], AF.Identity, [pq], [qf], scale=qscale)
            if STOP == 32:
                return
            ACT(tmp.t[:, 0:512], csb.t[:, 0:512], AF.Exp, [csb], [tmp], scale=sG)
            TT(qTb.t[:, blk, 0:512], qf.t[:, 0:512], tmp.t[:, 0:512], ALU.mult, [qf, tmp], [qTb])
            if STOP == 33:
                return
            ACT(tmp.t[:, 0:512], csb.t[:, 0:512], AF.Exp, [csb], [tmp], scale=-sG)
            TT(kTb.t[:, blk, 0:512], kf.t[:, 0:512], tmp.t[:, 0:512], ALU.mult, [kf, tmp], [kTb])
        if STOP == 34:
            return
        glb = bass.AP(csb.t, 63, [[515, 128], [64, 8], [0, 64]])
        cs3 = csb.t[:, 0:512].rearrange("p (c k) -> p c k", k=64)
        TT(tmp.t[:, 0:512].rearrange("p (c k) -> p c k", k=64), glb, cs3, ALU.subtract, [csb], [tmp])
        if STOP == 35:
            return
        ACT(tmp.t[:, 0:512], tmp.t[:, 0:512], AF.Exp, [tmp], [tmp], scale=sG)
        if STOP == 36:
            return
        TT(kdTb.t[:, blk, 0:512], kf.t[:, 0:512], tmp.t[:, 0:512], ALU.mult, [kf, tmp], [kdTb])
        if STOP == 37:
            return

    def scan(csb, src):
        S.op("vector", lambda e: e.tensor_tensor_scan(out=csb.t[:, 0:512], data0=rst.t[:, :], data1=src.t[:, 0:512],
                                                      initial=0.0, op0=ALU.mult, op1=ALU.add), [rst, src], [csb],
             cost=1250)

    def vz_and_kdtok(nblk, vslot_cols, zslot_cols, vTb, zGb, kdTb, kdtokb):
        s_v = load_w(*vslot_cols)
        for t in range(TPS):
            pv = proj_tm(s_v, 512, t)
            ACT(vTb.t[:, t, 0:512], pv.t[:, :], AF.Copy, [pv], [vTb])
        if STOP == 41:
            return
        if P2:
            s_z = load_w(*zslot_cols)
            for t in range(TPS):
                pz = proj_tm(s_z, 512, t)
                ACT(zGb.t[:, t, 0:512], pz.t[:, :], AF.Silu, [pz], [zGb])
        if STOP == 42:
            return
        for t in range(TPS):
            for blk in range(nblk):
                pb, pap = ptsn()
                TR(pap, kdTb.t[:, blk, t * 128:(t + 1) * 128], identb.t[:], [kdTb, identb], [pb])
                ACT(kdtokb.t[:, t, blk * 128:(blk + 1) * 128], pap, AF.Copy, [pb], [kdtokb])

    def mixer_A(s_idx):
        qTb, kTb, kdTb, vTb, zGb, kdtokb = BFB
        s_qk = load_w(0, 512)
        pg = pjn()
        for k in range(KC):
            MM(pg.t[0:16, :], wsm.t[:, k, 0:16], uT.t[:, k, 3:515], k == 0, k == KC - 1, [wsm, uT], [pg])
        ACT(grt.t[:], pg.t[0:16, :], AF.Copy, [pg], [grt])
        if STOP < 1:
            return
        for blk in range(2):
            kf = fbn()
            pk = proj_fm(s_qk, 256 + blk * 128, 128)
            ACT(kf.t[:, 0:512], pk.t[:, :], AF.Copy, [pk], [kf])
            px = pjn()
            MM(px.t[:, :], gwb.t[:, blk * 128:(blk + 1) * 128], grt.t[:], True, True, [gwb, grt], [px])
            sp = fbn()
            ACT(sp.t[:, 0:512], px.t[:, :], AF.Exp, [px, sml], [sp], scale=-1.0, bias=negb[:, blk:blk + 1])
            ACT(sp.t[:, 0:512], sp.t[:, 0:512], AF.Ln, [sp], [sp], bias=1.0)
            csb = fbn()
            if STOP < 2:
                continue
            scan(csb, sp)
            if STOP < 3:
                continue
            vd_front(blk, kf, csb, -1.0 / 16, s_qk, blk * 128, 0.125, qTb, kTb, kdTb)
        if STOP < 4 or 30 < STOP < 40:
            return
        vz_and_kdtok(2, (512, 1024), (1040, 1552), vTb, zGb, kdTb, kdtokb)
        if STOP < 5 or 40 < STOP < 50:
            return
        recur_vd(s_idx, "A", 2, 64, qTb, kTb, kdtokb, vTb, zGb, OFF_A, DEC_A, PR_NWA, 0)

    def mixer_C(s_idx):
        qTb, kTb, kdTb, vTb, zGb, kdtokb = BFB
        s_f = load_w(4632, 5144)
        s_q = load_w(4120, 4632) if P2 else None
        for blk in range(4):
            pf_ = proj_fm(s_f, blk * 128, 128)
            ff = fbn()
            ACT(ff.t[:, 0:512], pf_.t[:, :], AF.Sigmoid, [pf_], [ff])
            TS(ff.t[:, 0:512], ff.t[:, 0:512], omlb[:, blk:blk + 1], lbv[:, blk:blk + 1], ALU.mult, ALU.add,
               [ff, sml], [ff])
            kf = fbn()
            TS(kf.t[:, 0:512], ff.t[:, 0:512], -1.0, 1.0, ALU.mult, ALU.add, [ff], [kf])
            lf = fbn()
            ACT(lf.t[:, 0:512], ff.t[:, 0:512], AF.Ln, [ff], [lf])
            csb = fbn()
            scan(csb, lf)
            vd_front(blk, kf, csb, 1.0, s_q, blk * 128, 128 ** -0.5, qTb, kTb, kdTb)
        vz_and_kdtok(4, (5144, 5656), (5656, 6168), vTb, zGb, kdTb, kdtokb)
        recur_vd(s_idx, "C", 1, 128, qTb, kTb, kdtokb, vTb, zGb, OFF_C, DEC_C, PR_NWC, 2)

    def mixer_B(s_idx):
        qTb, kTb, ktokb, vTb, oGb, zGb = BFB
        if P2:
            s_q = load_w(1552, 2064)
            for blk in range(4):
                conv_block(s_idx, s_q, blk * 128, blk, PP_MCW + blk * 4, PP_MCB + blk, qTb.t[:, blk, 0:512], qTb)
        s_k = load_w(2064, 2576)
        for blk in range(4):
            conv_block(s_idx, s_k, blk * 128, 4 + blk, PP_MCW + (4 + blk) * 4, PP_MCB + 4 + blk,
                       kTb.t[:, blk, 0:512], kTb)
        s_v = load_w(2576, 3088)
        for t in range(TPS):
            bg, pgi = proj_small(16, 24, t)
            gi = gates_t[:, t * 16:t * 16 + 4]
            TT(gi, pgi[:, 0:4], ib2, ALU.add, [bg, sml], [GATES])
            ACT(gi, gi, AF.Exp, [GATES], [GATES])
            gf = gates_t[:, t * 16 + 4:t * 16 + 8]
            TT(gf, pgi[:, 4:8], pr.t[:, PR_FB:PR_FB + 4], ALU.add, [bg, pr], [GATES])
            ACT(gf, gf, AF.Exp, [GATES], [GATES], scale=-1.0)
            ACT(gf, gf, AF.Ln, [GATES], [GATES], bias=1.0)
            TS(gf, gf, -1.0, None, ALU.mult, None, [GATES], [GATES])
            pv = proj_tm(s_v, 512, t)
            vt4 = vTb.t[:, t, :].rearrange("p (h c) -> p h c", c=132)
            for h in range(4):
                TS(vt4[:, h, 0:128], pv.t[:, h * 128:(h + 1) * 128], gi[:, h:h + 1], None, ALU.mult, None,
                   [pv, GATES], [vTb])
            CP(vt4[:, :, 128], gi, [GATES], [vTb])
        if P2:
            s_o = load_w(3096, 3608)
            for t in range(TPS):
                po_ = proj_tm(s_o, 512, t)
                ACT(oGb.t[:, t, 0:512], po_.t[:, :], AF.Sigmoid, [po_], [oGb])
            s_z = load_w(3608, 4120)
            for t in range(TPS):
                pz = proj_tm(s_z, 512, t)
                ACT(zGb.t[:, t, 0:512], pz.t[:, :], AF.Silu, [pz], [zGb])
        for t in range(TPS):
            for h in range(4):
                pb, pap = ptsn()
                TR(pap, kTb.t[:, h, t * 128:(t + 1) * 128], identb.t[:], [kTb, identb], [pb])
                ACT(ktokb.t[:, t, h * 128:(h + 1) * 128], pap, AF.Copy, [pb], [ktokb])
        for t in range(TPS):
            tcs = slice(t * 128, (t + 1) * 128)
            vt4 = vTb.t[:, t, :].rearrange("p (h c) -> p h c", c=132)
            groups = []
            for h in range(4):
                groups.append(dict(qT=qTb.t[:, h, tcs], kT=kTb.t[:, h, tcs], qkbufs=[qTb, kTb],
                                   ktok=ktokb.t[:, t, h * 128:(h + 1) * 128], ktokbufs=[ktokb],
                                   v=vt4[:, h, :], vbuf=vTb, heads=[h], ncg=129,
                                   sf=stf_t[:, OFF_B + h * 132:OFF_B + (h + 1) * 132],
                                   sb=stb_t[:, OFF_B + h * 132:OFF_B + (h + 1) * 132],
                                   stf=STF["B"], stb=STB["B"], decoff=DEC_B + h))
            a_ap = gates_t[:, t * 16 + 4:t * 16 + 8]
            scalar_decay(t, 4, a_ap, GATES, groups, yb, lambda h: yb.t[:, h, 0:129], 129, 132)
            if P2:
                TS(rr4, yb.t[:, :, 128], -1.0, None, ALU.mult, None, [yb], [sml])
                TT(rr4, rr4, yb.t[:, :, 128], ALU.max, [yb, sml], [sml])
                TS(rr4, rr4, 1.0, None, ALU.max, None, [sml], [sml])
                S.op("vector", lambda e: e.reciprocal(out=rr4, in_=rr4), [sml], [sml])
                hb = nxt(TTB, "ttb")
                for h in range(4):
                    hc = slice(h * 128, (h + 1) * 128)
                    STT(hb.t[:, hc], yb.t[:, h, 0:128], rr4[:, h:h + 1], oGb.t[:, t, hc], ALU.mult, ALU.mult,
                        [yb, sml, oGb], [hb])
                for h in range(4):
                    hc = slice(h * 128, (h + 1) * 128)
                    ACT(junk.t[:, 0:128], hb.t[:, hc], AF.Square, [hb], [junk, sml], accum_out=ss4[:, h:h + 1])
                rstd_of(rs4[:, 0:4], ss4[:, 0:4], 128)
                nwz = nxt(TTB, "ttb")
                TT(nwz.t[:], pr.t[:, PR_NWB:PR_NWB + 512], zGb.t[:, t, 0:512], ALU.mult, [pr, zGb], [nwz])
                mix = nxt(MIXB, "mixb")
                for h in range(4):
                    hc = slice(h * 128, (h + 1) * 128)
                    STT(mix.t[:, hc], hb.t[:, hc], rs4[:, h:h + 1], nwz.t[:, hc], ALU.mult, ALU.mult,
                        [hb, sml, nwz], [mix])
                finish_mixer(s_idx, 1, t, mix)
        if P2:
            out_proj(s_idx, 1)

    def mixer_D(s_idx):
        bcTb, btokb, _, _, _, zGb = BFB
        s_x = load_w(6168, 6680)
        for blk in range(4):
            conv_block(s_idx, s_x, blk * 128, 8 + blk, PP_SCW + blk * 4, PP_SCB + blk, xst.t[:, blk, :], xst)
        s_bc = load_w(6680, 7192)
        for blk in (range(4) if P2 else range(2)):
            conv_block(s_idx, s_bc, blk * 128, 12 + blk, PP_SCW + (4 + blk) * 4, PP_SCB + 4 + blk,
                       bcTb.t[:, blk, 0:512], bcTb)
        if P2:
            s_z = load_w(7200, 7712)
            for t in range(TPS):
                pz = proj_tm(s_z, 512, t)
                ACT(zGb.t[:, t, 0:512], pz.t[:, :], AF.Silu, [pz], [zGb])
        for t in range(TPS):
            for g in range(2):
                pb, pap = ptsn()
                TR(pap, bcTb.t[:, g, t * 128:(t + 1) * 128], identb.t[:], [bcTb, identb], [pb])
                ACT(btokb.t[:, t, g * 128:(g + 1) * 128], pap, AF.Copy, [pb], [btokb])
        for t in range(TPS):
            tcs = slice(t * 128, (t + 1) * 128)
            bd, pdt = proj_small(24, 32, t)
            dtv = gates_t[:, 256 + t * 32:256 + t * 32 + 8]
            TT(dtv, pdt[:, 0:8], pr.t[:, PR_DTB:PR_DTB + 8], ALU.add, [bd, pr], [GATES])
            ACT(dtv, dtv, AF.Exp, [GATES], [GATES])
            ACT(dtv, dtv, AF.Ln, [GATES], [GATES], bias=1.0)
            av = gates_t[:, 256 + t * 32 + 8:256 + t * 32 + 16]
            TT(av, dtv, aneg, ALU.mult, [GATES, sml], [GATES])
            xs = nxt(TTB, "ttb")
            for blk in range(4):
                bx, pxr = pwn()
                S.op("tensor", lambda e, pxr=pxr, blk=blk, tcs=tcs: e.transpose(pxr[:, 0:128], xst.t[:, blk, tcs], identf.t[:]),
                     [xst, identf], [bx], cost=400)
                CP(xs.t[:, blk * 128:(blk + 1) * 128], pxr[:, 0:128], [bx], [xs])
            vt = nxt(VTD, "vtd")
            dtb = bass.AP(gates_t, 256 + t * 32, [[512, 128], [1, 8], [0, 64]])
            TT(vt.t[:, :].rearrange("p (h c) -> p h c", c=64), xs.t[:, :].rearrange("p (h c) -> p h c", c=64),
               dtb, ALU.mult, [xs, GATES], [vt])
            groups = []
            for g in range(2):
                groups.append(dict(qT=bcTb.t[:, 2 + g, tcs], kT=bcTb.t[:, g, tcs], qkbufs=[bcTb],
                                   ktok=btokb.t[:, t, g * 128:(g + 1) * 128], ktokbufs=[btokb],
                                   v=vt.t[:, g * 256:(g + 1) * 256], vbuf=vt, heads=[4 * g + j for j in range(4)],
                                   ncg=256, sf=stf_t[:, OFF_D + g * 256:OFF_D + (g + 1) * 256],
                                   sb=stb_t[:, OFF_D + g * 256:OFF_D + (g + 1) * 256],
                                   stf=STF["D"], stb=STB["D"], decoff=DEC_D + 4 * g))
            Y = nxt(TTB, "ttb")
            scalar_decay(t, 8, av, GATES, groups, Y, lambda h: Y.t[:, h * 64:(h + 1) * 64], 64, 64)
            if P2:
                for h in range(8):
                    hs = slice(h * 64, (h + 1) * 64)
                    STT(Y.t[:, hs], xs.t[:, hs], pr.t[:, PR_D + h:PR_D + h + 1], Y.t[:, hs], ALU.mult, ALU.add,
                        [xs, pr, Y], [Y])
                TT(Y.t[:], Y.t[:], zGb.t[:, t, 0:512], ALU.mult, [Y, zGb], [Y])
                for g in range(2):
                    ACT(junk.t[:, 0:256], Y.t[:, g * 256:(g + 1) * 256], AF.Square, [Y], [junk, sml],
                        accum_out=ssn[:, g:g + 1])
                rstd_of(rsn[:, 0:2], ssn[:, 0:2], 256)
                mix = nxt(MIXB, "mixb")
                for g in range(2):
                    gs = slice(g * 256, (g + 1) * 256)
                    STT(mix.t[:, gs], Y.t[:, gs], rsn[:, g:g + 1], pr.t[:, PR_NWD + g * 256:PR_NWD + (g + 1) * 256],
                        ALU.mult, ALU.mult, [Y, sml, pr], [mix])
                finish_mixer(s_idx, 3, t, mix)
        if P2:
            out_proj(s_idx, 3)

    finals = []
    for layer in range(NLAYERS):
        win3 = d_win_all[layer].rearrange("(k p) c -> p k c", p=128)
        wout3 = d_wout_all[layer].rearrange("(k p) c -> p k c", p=128)
        load_layer_params(layer)
        for P2 in (False, True):
            last = P2 and layer == NLAYERS - 1
            init_pass()
            for s_idx in range(NS):
                if s_idx == 0:
                    make_uT_tile(halo_h, halo_h.t[:], slice(0, 3), 3)
                else:
                    CP(uT.t[:, :, 0:3], uT.t[:, :, 512:515], [uT], [uT])
                for t in range(TPS):
                    tg = s_idx * TPS + t
                    make_uT_tile(HT[tg], hres_t[:, tg, :], tokc(t), 128)
                if "A" in MIXERS:
                    mixer_A(s_idx)
                if "B" in MIXERS:
                    mixer_B(s_idx)
                if "C" in MIXERS:
                    mixer_C(s_idx)
                if "D" in MIXERS:
                    mixer_D(s_idx)
            if not P2:
                end_p1()
            elif not last:
                exchange_halo()

    DMA("sync", nwbc.t[:], d_fnw.partition_broadcast(128), nwbc, [], [nwbc])
    for tg in range(NT):
        ACT(junk.t[:], hres_t[:, tg, :], AF.Square, [HT[tg]], [junk, sml], accum_out=ss4[:, 0:1])
        rstd_of(rs4[:, 0:1], ss4[:, 0:1], DM)
        for half in range(2):
            ob = nxt(TTB, "ttb")
            hs = slice(half * 512, (half + 1) * 512)
            STT(ob.t[:], hres_t[:, tg, hs], rs4[:, 0:1], nwbc.t[:, hs], ALU.mult, ALU.mult,
                [HT[tg], sml, nwbc], [ob])
            finals.append(DMA("sync", d_out[tg * 128:(tg + 1) * 128, hs], ob.t[:], ob, [ob], []))
    finals.extend(dbg_ids)
    S.wait_all("sync", finals)
    print("n semaphores", len(S.sems), flush=True)
    S.schedule(reorder=REORDER)
    print("ops:", len(S.ops), "model makespan us:", S.makespan / 1e3, flush=True)
    S.emit()
    S.close()
    es.close()
    return nc


def pack_layer(inp, l):
    f = lambda a: np.ascontiguousarray(np.asarray(a, dtype=np.float32))
    pp = np.zeros((128, NPP), np.float32)
    pp[:, PP_GB:PP_GB + 2] = f(inp["gla_gate_b"][l]).reshape(2, 128).T
    pp[:, PP_MCW:PP_MCW + 32] = f(inp["ml_conv_w"][l]).reshape(4, 8, 128).transpose(2, 1, 0).reshape(128, 32)
    pp[:, PP_MCB:PP_MCB + 8] = f(inp["ml_conv_b"][l]).reshape(8, 128).T
    pp[:, PP_SCW:PP_SCW + 32] = f(inp["ssd_conv_w"][l]).reshape(4, 8, 128).transpose(2, 1, 0).reshape(128, 32)
    pp[:, PP_SCB:PP_SCB + 8] = f(inp["ssd_conv_b"][l]).reshape(8, 128).T
    pp[:, PP_LB:PP_LB + 8] = f(inp["hg_lb_logits"]).reshape(2, 4, 128).transpose(2, 1, 0).reshape(128, 8)
    pr = np.concatenate([f(inp["ml_i_b"][l]), f(inp["ml_f_b"][l]), f(inp["ssd_dt_bias"][l]), f(inp["ssd_A_log"][l]),
                         f(inp["ssd_D"][l]), f(inp["gla_norm_w"][l]), f(inp["ml_norm_w"][l]), f(inp["hg_norm_w"][l]),
                         f(inp["ssd_norm_w"][l])]).astype(np.float32)
    assert pr.shape[0] == NPR
    return pp, pr


def core_pm(q):
    pm = np.zeros((128, 9), np.float32)
    for j in range(3):
        pm[:, j] = 1.0 if j < q else 0.0
        pm[:, 3 + j] = 1.0 - pm[:, j]
        pm[:, 6 + j] = 1.0 if j == q - 1 else 0.0
    return pm


_NC_CACHE = {}


def pack_inputs(inputs):
    f = lambda a: np.ascontiguousarray(np.asarray(a, dtype=np.float32))
    x = f(inputs["x"])
    pps, prs = zip(*[pack_layer(inputs, l) for l in range(2)])
    shared = dict(w_in=f(inputs["w_in"]), w_out=f(inputs["w_out"]), nw=f(inputs["norm_w"]),
                  fnw=f(inputs["final_norm_w"]), pp=np.ascontiguousarray(np.stack(pps)),
                  pr=np.ascontiguousarray(np.stack(prs)), gw=f(inputs["gla_gate_w"]))
    in_maps = []
    for c in range(NCORES):
        b, q = c // 4, c % 4
        d = dict(shared)
        d["hin"] = np.ascontiguousarray(x[b, q * T:(q + 1) * T, :])
        d["halo"] = (np.zeros((3, DM), np.float32) if q == 0
                     else np.ascontiguousarray(x[b, q * T - 3:q * T, :]))
        d["pm"] = core_pm(q)
        in_maps.append(d)
    return in_maps


def kernel(**inputs):
    if "nc" not in _NC_CACHE:
        _NC_CACHE["nc"] = build()
    in_maps = pack_inputs(inputs)
    res = run_bass_kernel_spmd(_NC_CACHE["nc"], in_maps, core_ids=list(range(NCORES)))
    B = np.asarray(inputs["x"]).shape[0]
    out = np.zeros((B, 4 * T, DM), np.float32)
    for c in range(NCORES):
        out[c // 4, (c % 4) * T:(c % 4 + 1) * T, :] = np.asarray(res.results[c]["hout"], dtype=np.float32)
    return out
```

```python
import math
import numpy as np
from contextlib import ExitStack
import concourse.bass as bass
import concourse.mybir as mybir
from concourse.bass_utils import run_bass_kernel_spmd

F32 = mybir.dt.float32
BF16 = mybir.dt.bfloat16
AF = mybir.ActivationFunctionType
ALU = mybir.AluOpType

NCORES = 8
SYNC_LAT = 150.0
SELF_DIST = 3
T = 2048
NT = 16
ST = 512
TPS = 4
NS = 4
DM = 1024
KC = 8
EPS = 1e-6
DPROJ = 7712
NSTATE = 1808
NDEC = 18
NSUM = NSTATE + NDEC
OFF_A, OFF_B, OFF_C, OFF_D = 0, 256, 784, 1296
DEC_A, DEC_B, DEC_C, DEC_D = 0, 2, 6, 10
NPP = 90
NPR = 2080
PP_GB, PP_MCW, PP_MCB, PP_SCW, PP_SCB, PP_LB = 0, 2, 34, 42, 74, 82
PR_IB, PR_FB, PR_DTB, PR_ALOG, PR_D, PR_NWA, PR_NWB, PR_NWC, PR_NWD = 0, 4, 8, 16, 24, 32, 544, 1056, 1568


class Buf:
    __slots__ = ("t", "name", "w", "r", "sem", "excl")

    def __init__(self, t, name, excl=False):
        self.t = t
        self.name = name
        self.w = None
        self.r = []
        self.sem = None
        self.excl = excl


class Sched:
    def __init__(self, nc):
        self.nc = nc
        self.engs = []
        self.sems = {}
        self._ctx = []
        self.nself = {"tensor"}
        self.ops = []

    def add_engine(self, name):
        key = "e_" + name
        self.sems[key] = self._alloc(key)
        self.engs.append(name)

    def _alloc(self, key):
        cm = self.nc.semaphore(key)
        h = cm.__enter__()
        self._ctx.append(cm)
        return h

    def dma_sem(self, name):
        key = "d_" + name
        self.sems[key] = self._alloc(key)
        return key

    def _record(self, eng, fn, kind, sem, reads, writes, cost, lat):
        ex = [r for r in reads if r.excl]
        if ex:
            writes = list(writes) + [r for r in ex if r not in writes]
            reads = [r for r in reads if not r.excl]
        deps = set()
        for r in reads:
            if r.w is not None:
                deps.add(r.w)
        for w in writes:
            if w.w is not None:
                deps.add(w.w)
            deps.update(w.r)
        oid = len(self.ops)
        self.ops.append(dict(id=oid, eng=eng, fn=fn, kind=kind, sem=sem, deps=deps, cost=cost, lat=lat))
        for r in reads:
            r.r.append(oid)
        for w in writes:
            w.w = oid
            w.r = []
        return oid

    def op(self, eng, fn, reads=(), writes=(), cost=300):
        return self._record(eng, fn, "op", "e_" + eng, reads, writes, cost, cost)

    def dma(self, eng, fn, semkey, reads=(), writes=(), cost=100, lat=4000):
        return self._record(eng, fn, "dma", semkey, reads, writes, cost, lat)

    def coll(self, eng, fn, semkey, reads=(), writes=()):
        return self._record(eng, fn, "coll", semkey, reads, writes, 2000, 40000)

    def wait_all(self, eng, ids):
        oid = len(self.ops)
        self.ops.append(dict(id=oid, eng=eng, fn=None, kind="wait", sem=None, deps=set(ids), cost=10, lat=10))
        return oid

    def schedule(self, reorder=True):
        import heapq
        ops = self.ops
        n = len(ops)
        ndeps = [len(o["deps"]) for o in ops]
        users = [[] for _ in range(n)]
        for o in ops:
            for d in o["deps"]:
                users[d].append(o["id"])
        prio = [0.0] * n
        for o in reversed(ops):
            i = o["id"]
            m = 0.0
            for u in users[i]:
                if prio[u] > m:
                    m = prio[u]
            prio[i] = m + o["lat"] + SYNC_LAT
        finish = [0.0] * n
        ready_t = [0.0] * n
        efree = {e: 0.0 for e in self.engs}
        pending = {e: [] for e in self.engs}
        avail = {e: [] for e in self.engs}
        queues = {e: [] for e in self.engs}
        for o in ops:
            if ndeps[o["id"]] == 0:
                heapq.heappush(pending[o["eng"]], (0.0, o["id"]))
        done = 0
        if not reorder:
            for o in ops:
                queues[o["eng"]].append(o["id"])
            self.queues = queues
            self.makespan = 0
            return
        while done < n:
            best = None
            for e in self.engs:
                pe, av = pending[e], avail[e]
                while pe and pe[0][0] <= efree[e]:
                    _, i = heapq.heappop(pe)
                    heapq.heappush(av, (-prio[i], i))
                if av:
                    cand = (efree[e], av[0][1], e, True)
                elif pe:
                    cand = (pe[0][0], pe[0][1], e, False)
                else:
                    continue
                if best is None or cand[:2] < best[:2]:
                    best = cand
            assert best is not None, "scheduler deadlock"
            st, i, e, from_av = best
            if from_av:
                heapq.heappop(avail[e])
            else:
                heapq.heappop(pending[e])
            o = ops[i]
            efree[e] = st + o["cost"]
            finish[i] = st + o["lat"] + SYNC_LAT
            queues[e].append(i)
            done += 1
            for u in users[i]:
                ndeps[u] -= 1
                if finish[i] > ready_t[u]:
                    ready_t[u] = finish[i]
                if ndeps[u] == 0:
                    heapq.heappush(pending[ops[u]["eng"]], (ready_t[u], u))
        self.queues = queues
        self.makespan = max(finish)

    def emit(self):
        nc = self.nc
        ops = self.ops
        semval = {}
        cnt = {}
        for e in self.engs:
            for i in self.queues[e]:
                o = ops[i]
                if o["kind"] == "op":
                    cnt[o["sem"]] = cnt.get(o["sem"], 0) + 1
                    semval[i] = (o["sem"], cnt[o["sem"]], 1)
        for o in ops:
            if o["kind"] in ("dma", "coll"):
                inc = 16 if o["kind"] == "dma" else 1
                cnt[o["sem"]] = cnt.get(o["sem"], 0) + inc
                semval[o["id"]] = (o["sem"], cnt[o["sem"]], inc)
        with nc.Block() as block:
            for e in self.engs:
                deco = getattr(block, e)
                q = self.queues[e]
                own = "e_" + e

                def body(h, q=q, e=e, own=own):
                    seen = {}
                    for i in q:
                        o = ops[i]
                        need = {}
                        for d in o["deps"]:
                            k, v, _ = semval[d]
                            if k == own and e in self.nself:
                                continue
                            if k == own and ops[d]["kind"] == "op" and i in semval and semval[i][0] == own \
                                    and semval[i][1] - v >= SELF_DIST:
                                continue
                            if need.get(k, 0) < v:
                                need[k] = v
                        for k, v in need.items():
                            if seen.get(k, 0) >= v:
                                continue
                            seen[k] = v
                            h.wait_ge(self.sems[k], v)
                        if o["fn"] is not None:
                            k, v, inc = semval[i]
                            o["fn"](h).then_inc(self.sems[k], inc)
                deco(body)

    def close(self):
        for cm in reversed(self._ctx):
            cm.__exit__(None, None, None)


def build(debug=False, MIXERS="ABCD", STOP=99, NLAYERS=2, REORDER=True):
    nc = bass.Bass("TRN2", target_bir_lowering=False)
    es = ExitStack()
    S = Sched(nc)
    for n in ("sync", "gpsimd", "tensor", "vector", "scalar"):
        S.add_engine(n)

    def dram(name, shape, dt=F32, kind="ExternalInput"):
        return nc.dram_tensor(name, list(shape), dt, kind=kind).ap()

    layer = 0
    P2 = False
    last = False

    d_hin = dram("hin", [T, DM])
    d_halo = dram("halo", [3, DM])
    d_win_all = dram("w_in", [2, DM, DPROJ])
    d_wout_all = dram("w_out", [2, 2048, DM])
    d_nw_all = dram("nw", [2, DM])
    d_fnw = dram("fnw", [DM])
    d_pp_all = dram("pp", [2, 128, NPP])
    d_pr_all = dram("pr", [2, NPR])
    d_gw_all = dram("gw", [2, 16, 256])
    d_pm = dram("pm", [128, 9])
    d_out = dram("hout", [T, DM], kind="ExternalOutput")
    if debug:
        d_dbg = dram("dbg", [T, 2048], BF16, kind="ExternalOutput")
    d_sloc = [nc.dram_tensor(f"sloc{l}", [128, NSUM], F32).ap() for l in range(2)]
    d_sgat = [nc.dram_tensor(f"sgat{l}", [4 * 128, NSUM], F32).ap() for l in range(2)]
    d_hloc = nc.dram_tensor("hloc", [3, DM], F32).ap()
    d_hgat = nc.dram_tensor("hgat", [12, DM], F32).ap()
    GROUPS = [[0, 1, 2, 3], [4, 5, 6, 7]]

    cnt = [0]
    dbg_ids = []

    def sb(shape, dt, name=None):
        cnt[0] += 1
        name = "s_" + (name or f"sb{cnt[0]}")
        t = es.enter_context(nc.sbuf_tensor(name, list(shape), dt))
        return Buf(t, name)

    def ps(shape, dt, name):
        return es.enter_context(nc.psum_tensor(name, list(shape), dt))

    def withsem(b):
        b.sem = S.dma_sem(b.name)
        return b

    def fsz(ap):
        n = 1
        for (_, c) in list(ap.ap)[1:]:
            n *= c
        return n

    def is_psum(ap):
        return "PSum" in type(ap.tensor).__name__

    def ACT(out, in_, func, R, W, **kw):
        c = 200 + 0.85 * fsz(out)
        return S.op("scalar", lambda e: e.activation(out=out, in_=in_, func=func, **kw), R, W, cost=c)

    def _dve_cost(out, ins):
        c = 70 + 1.05 * fsz(out)
        if any(is_psum(a) for a in ins):
            c += 60
        return c

    def TT(out, in0, in1, op, R, W, eng="vector"):
        return S.op(eng, lambda e: e.tensor_tensor(out=out, in0=in0, in1=in1, op=op), R, W,
                    cost=_dve_cost(out, [in0, in1]))

    def TS(out, in0, s1, s2, op0, op1, R, W, eng="vector"):
        c = _dve_cost(out, [in0])
        if s2 is None:
            return S.op(eng, lambda e: e.tensor_scalar(out=out, in0=in0, scalar1=s1, scalar2=None, op0=op0), R, W, cost=c)
        return S.op(eng, lambda e: e.tensor_scalar(out=out, in0=in0, scalar1=s1, scalar2=s2, op0=op0, op1=op1), R, W, cost=c)

    def STT(out, in0, scalar, in1, op0, op1, R, W, eng="vector"):
        return S.op(eng, lambda e: e.scalar_tensor_tensor(out=out, in0=in0, scalar=scalar, in1=in1, op0=op0, op1=op1), R, W,
                    cost=_dve_cost(out, [in0, in1]))

    def CP(out, in_, R, W, eng="vector"):
        return S.op(eng, lambda e: e.tensor_copy(out=out, in_=in_), R, W, cost=_dve_cost(out, [in_]))

    def MSET(ap, val, W, eng="gpsimd"):
        return S.op(eng, lambda e: e.memset(ap, val), (), W, cost=200 + fsz(ap))

    def MM(out, lhsT, rhs, start, stop, R, W):
        n = fsz(rhs)
        f32 = "float32" in str(rhs.tensor.dtype)
        c = (64 + 1.7 * n) if f32 else (64 + 0.45 * n)
        return S.op("tensor", lambda e: e.matmul(out, lhsT=lhsT, rhs=rhs, start=start, stop=stop), R, W, cost=c)

    def TR(out, in_, ident, R, W):
        return S.op("tensor", lambda e: e.transpose(out, in_, ident), R, W, cost=150)

    def DMA(q, out, in_, buf_sem, R, W):
        nbytes = fsz(out) * 128 * 4
        issue = 1000 if q == "gpsimd" else 80
        return S.dma(q, lambda e: e.dma_start(out=out, in_=in_), buf_sem.sem, R, W, cost=issue,
                     lat=issue + 2000 + nbytes / 120.0)

    identf = sb([128, 128], F32, "identf")
    identb = sb([128, 128], BF16, "identb")
    trif = sb([128, 128], F32, "trif")
    suf = sb([128, 128], F32, "suf")
    negb_ = sb([128, 128], BF16, "negmask")
    onesf = sb([128, 128], F32, "onesf")
    rst = sb([128, 512], F32, "rst")
    MSET(onesf.t[:], 1.0, [onesf])
    MSET(identf.t[:], 1.0, [identf])
    S.op("gpsimd", lambda e: e.affine_select(out=identf.t[:], in_=identf.t[:], pattern=[[-1, 128]],
                                             compare_op=ALU.is_equal, fill=0.0, base=0, channel_multiplier=1),
         [identf], [identf])
    CP(identb.t[:], identf.t[:], [identf], [identb])
    MSET(trif.t[:], 1.0, [trif])
    S.op("gpsimd", lambda e: e.affine_select(out=trif.t[:], in_=trif.t[:], pattern=[[1, 128]],
                                             compare_op=ALU.is_ge, fill=0.0, base=0, channel_multiplier=-1),
         [trif], [trif])
    TS(suf.t[:], trif.t[:], -1.0, 1.0, ALU.mult, ALU.add, [trif], [suf])
    TS(negb_.t[:], suf.t[:], -30000.0, None, ALU.mult, None, [suf], [negb_])
    MSET(rst.t[:], 1.0, [rst])
    rst3 = rst.t[:, :].rearrange("p (c k) -> p c k", k=64)
    MSET(rst3[:, :, 0:1], 0.0, [rst])

    hres_t = es.enter_context(nc.sbuf_tensor("hres", [128, NT, DM], F32))
    HT = [withsem(Buf(hres_t, f"ht{t}")) for t in range(NT)]
    nwbc = withsem(sb([128, DM], F32, "nwbc"))
    uT = sb([128, KC, 515], BF16, "uT")
    ubf = sb([128, DM], BF16, "ubf")
    junk = sb([128, DM], BF16, "junk")
    NSLOT = 2
    slots = [withsem(sb([128, 4096], BF16, f"wslot{i}")) for i in range(NSLOT)]
    wsm = withsem(sb([128, KC, 32], BF16, "wsm"))
    pp = withsem(sb([128, NPP], F32, "pp"))
    pr = withsem(sb([128, NPR], F32, "pr"))
    gwf = withsem(sb([16, 256], F32, "gwf"))
    gwb = sb([16, 256], BF16, "gwb")
    grt = sb([16, 512], BF16, "grt")
    pm = withsem(sb([128, 9], F32, "pm"))
    stf_t = es.enter_context(nc.sbuf_tensor("stf", [128, NSUM], F32))
    stb_t = es.enter_context(nc.sbuf_tensor("stb", [128, NSTATE], BF16))
    STF = {(m, h): Buf(stf_t, "stf%s%d" % (m, h)) for m in "ABCD" for h in range(4)}
    STB = {(m, h): Buf(stb_t, "stb%s%d" % (m, h)) for m in "ABCD" for h in range(4)}
    STDEC = Buf(stf_t, "stdec")
    stsem = withsem(Buf(stf_t, "stout"))
    halo_raw = sb([128, 16, 3], F32, "halo_raw")
    BFB = [sb([128, TPS, 528], BF16, f"bfb{i}") for i in range(6)]
    FB = [sb([128, 515], F32, f"fb{i}") for i in range(6)]
    TTB = [sb([128, 512], F32, f"ttb{i}") for i in range(4)]
    SMF = [sb([128, 128], F32, f"smf{i}") for i in range(4)]
    SMB = [sb([128, 128], BF16, f"smb{i}") for i in range(4)]
    VDB = [sb([128, 512], BF16, f"vdb{i}") for i in range(2)]
    MIXB = [sb([128, 512], BF16, f"mix{i}") for i in range(2)]
    mixT = sb([128, 4, ST], BF16, "mixT")
    sml = sb([128, 512], F32, "sml")
    smp = Buf(sml.t, "smp")
    smn = Buf(sml.t, "smn")
    smd = Buf(sml.t, "smd")
    SML = {}
    _smo = [0]

    def small(name, n):
        o = _smo[0]
        _smo[0] += n
        assert _smo[0] <= 512
        SML[name] = (o, n)
        return sml.t[:, o:o + n]

    gates_t = es.enter_context(nc.sbuf_tensor("gates", [128, 512], F32))
    GATES = Buf(gates_t, "gatesB")
    GATESD = Buf(gates_t, "gatesD")
    egl_t = es.enter_context(nc.sbuf_tensor("egl", [128, 4, 8], F32))
    EGLB = [Buf(egl_t, "egl%d" % i) for i in range(4)]
    halo_h = withsem(sb([128, DM], F32, "halo_h"))
    cmb = withsem(sb([128, NSUM], F32, "cmb"))
    for i in range(4):
        withsem(TTB[i])

    pj_t = [ps([128, 512], F32, f"pj{i}") for i in range(2)]
    PJ = [Buf(t, f"pj{i}", excl=True) for i, t in enumerate(pj_t)]
    ptu_t = ps([128, 1024], BF16, "ptu")
    PTU = Buf(ptu_t, "ptu", excl=True)
    pts_t = ps([128, 1024], BF16, "pts")
    PTSB = Buf(pts_t, "pts", excl=True)
    PTS = []
    for i in range(8):
        PTS.append((PTSB, i * 128))
        PTS.append((PTU, i * 128))
    pf_t = [ps([128, 512], F32, f"pf{i}") for i in range(2)]
    PFB = [Buf(pf_t[i], f"pf{i}", excl=True) for i in range(2)]
    PF = [(PFB[i % 2], (i // 2) * 128) for i in range(8)]
    pw_t = [ps([128, 512], F32, f"pw{i}") for i in range(2)]
    PWB = [Buf(pw_t[i], f"pw{i}", excl=True) for i in range(2)]
    PW = [(PWB[i % 2], (i // 2) * 256) for i in range(4)]
    rot = {}

    def nxt(pool, key):
        i = rot.get(key, 0)
        rot[key] = i + 1
        return pool[i % len(pool)]

    def pjn():
        return nxt(PJ, "pj")

    def pfn():
        b, o = nxt(PF, "pf")
        return b, b.t[:, o:o + 128]

    def pwn():
        b, o = nxt(PW, "pw")
        return b, b.t[:, o:o + 256]

    def ptsn():
        b, o = nxt(PTS, "pts")
        return b, b.t[:, o:o + 128]

    def fbn():
        return nxt(FB, "fb")

    def slot3(s):
        return s.t[:, :].rearrange("p (k c) -> p k c", k=KC)

    def slot_wo(s):
        return s.t[:, :].rearrange("p (k c) -> p k c", k=4)

    win3 = None
    wout3 = None

    wcache = {}
    slot_hw = {sl.name: withsem(Buf(None, sl.name + "_hw")) for sl in slots}
    d_wscr = nc.dram_tensor("wscr", [48, 128, 4096], BF16).ap()

    def _load_cached(key, slot_view, src_ap, n):
        s = nxt(slots, "slot")
        dst = slot_view(s)
        if key not in wcache:
            scr = d_wscr[len(wcache)]
            sbuf = withsem(Buf(None, "wscr%d" % len(wcache)))
            wcache[key] = (scr, sbuf)
            DMA("gpsimd", dst, src_ap, s, [], [s])
            DMA("sync", scr, s.t[:, :], sbuf, [s], [sbuf])
        else:
            scr, sbuf = wcache[key]
            DMA("sync", s.t[:, :], scr, slot_hw[s.name], [sbuf], [s])
        return s

    def load_w(c0, c1):
        n = c1 - c0
        return _load_cached((layer, "i", c0, c1), lambda s: slot3(s)[:, :, 0:n], win3[:, :, c0:c1], n)

    def load_wo(m):
        return _load_cached((layer, "o", m), lambda s: slot_wo(s)[:, :, :], wout3[:, m * 4:(m + 1) * 4, :], 4096)

    DMA("sync", pm.t[:], d_pm, pm, [], [pm])
    for t in range(NT):
        DMA("sync", hres_t[:, t, :], d_hin[t * 128:(t + 1) * 128, :], HT[t], [], [HT[t]])
    MSET(halo_h.t[:], 0.0, [halo_h])
    DMA("sync", halo_h.t[0:3, :], d_halo, halo_h, [], [halo_h])

    negb = small("negb", 2)
    ib2 = small("ib2", 4)
    aneg = small("aneg", 8)
    lbv = small("lbv", 4)
    omlb = small("omlb", 4)
    ss4 = small("ss4", 4)
    rs4 = small("rs4", 4)
    rr4 = small("rr4", 4)
    ssn = small("ssn", 2)
    rsn = small("rsn", 2)
    decp = small("decp", NDEC)
    acs_s = small("acs", 8)
    ee_s = small("ee", 8)
    etot_s = small("etot", 8)
    dec_s = small("dec", 8)
    dd_s = small("dd", 8)
    allst = list(STF.values()) + [STDEC]
    SLOC = [withsem(Buf(None, f"sloc{l}")) for l in range(2)]
    SGAT = [withsem(Buf(None, f"sgat{l}")) for l in range(2)]
    HLOC = withsem(Buf(None, "hloc"))
    HGAT = withsem(Buf(None, "hgat"))

    def load_layer_params(l):
        DMA("sync", nwbc.t[:], d_nw_all[l].partition_broadcast(128), nwbc, [], [nwbc])
        DMA("sync", pp.t[:], d_pp_all[l], pp, [], [pp])
        DMA("sync", pr.t[:], d_pr_all[l].partition_broadcast(128), pr, [], [pr])
        DMA("sync", gwf.t[:], d_gw_all[l], gwf, [], [gwf])
        CP(gwb.t[:], gwf.t[:], [gwf], [gwb])
        DMA("gpsimd", wsm.t[:, :, 0:16], win3[:, :, 1024:1040], wsm, [], [wsm])
        DMA("gpsimd", wsm.t[:, :, 16:24], win3[:, :, 3088:3096], wsm, [], [wsm])
        DMA("gpsimd", wsm.t[:, :, 24:32], win3[:, :, 7192:7200], wsm, [], [wsm])
        TS(negb, pp.t[:, PP_GB:PP_GB + 2], -1.0, None, ALU.mult, None, [pp], [smp])
        TS(ib2, pr.t[:, PR_IB:PR_IB + 4], math.log(128 ** -0.5), None, ALU.add, None, [pr], [smp])
        ACT(aneg, pr.t[:, PR_ALOG:PR_ALOG + 8], AF.Exp, [pr], [smp])
        TS(aneg, aneg, -1.0, None, ALU.mult, None, [smp], [smp])
        lg3 = pp.t[:, PP_LB:PP_LB + 8].rearrange("p (b l) -> p b l", l=2)
        if l == 0:
            MSET(lbv, 0.0, [smp], eng="vector")
        else:
            TT(lbv, lg3[:, :, 1], lg3[:, :, 0], ALU.subtract, [pp], [smp])
            ACT(lbv, lbv, AF.Sigmoid, [smp], [smp])
        TS(omlb, lbv, -1.0, 1.0, ALU.mult, ALU.add, [smp], [smp])

    cgroups = []
    for blk in range(2):
        cgroups.append((OFF_A + blk * 128, 128, DEC_A + blk))
    for h in range(4):
        cgroups.append((OFF_B + h * 132, 132, DEC_B + h))
    for h in range(4):
        cgroups.append((OFF_C + h * 128, 128, DEC_C + h))
    for h in range(8):
        cgroups.append((OFF_D + h * 64, 64, DEC_D + h))

    def init_pass():
        MSET(stf_t[:, 0:NSTATE], 0.0, allst, eng="vector")
        if not P2:
            MSET(stf_t[:, NSTATE:NSUM], 1.0, allst, eng="vector")
        else:
            for j in range(3):
                DMA("sync", cmb.t[:], d_sgat[layer][j * 128:(j + 1) * 128, :], cmb, [SGAT[layer]], [cmb])
                TS(decp, cmb.t[:, NSTATE:NSUM], pm.t[:, j:j + 1], pm.t[:, 3 + j:4 + j], ALU.mult, ALU.add,
                   [cmb, pm], [smd])
                TS(cmb.t[:, 0:NSTATE], cmb.t[:, 0:NSTATE], pm.t[:, j:j + 1], None, ALU.mult, None, [cmb, pm], [cmb])
                for (o, n, g) in cgroups:
                    STT(stf_t[:, o:o + n], stf_t[:, o:o + n], decp[:, g:g + 1], cmb.t[:, o:o + n], ALU.mult, ALU.add,
                        allst + [cmb, smd], allst)
        CP(stb_t[:, :], stf_t[:, 0:NSTATE], allst, list(STB.values()))

    def end_p1():
        l = layer
        DMA("sync", d_sloc[l], stf_t[:, :], SLOC[l], allst, [SLOC[l]])
        S.coll("gpsimd", lambda e: e.collective_compute("AllGather", ALU.bypass, replica_groups=GROUPS,
                                                        ins=[d_sloc[l]], outs=[d_sgat[l]]),
               SGAT[l].sem, [SLOC[l]], [SGAT[l]])

    def exchange_halo():
        DMA("sync", d_hloc, hres_t[125:128, NT - 1, :], HLOC, [HT[NT - 1]], [HLOC])
        S.coll("gpsimd", lambda e: e.collective_compute("AllGather", ALU.bypass, replica_groups=GROUPS,
                                                        ins=[d_hloc], outs=[d_hgat]),
               HGAT.sem, [HLOC], [HGAT])
        MSET(halo_h.t[:], 0.0, [halo_h])
        for j in range(3):
            for half in range(2):
                tb = nxt(TTB, "ttb")
                hs = slice(half * 512, (half + 1) * 512)
                DMA("sync", tb.t[0:3, :], d_hgat[j * 3:(j + 1) * 3, hs], tb, [HGAT], [tb])
                STT(halo_h.t[0:3, hs], tb.t[0:3, :], pm.t[0:3, 6 + j:7 + j], halo_h.t[0:3, hs], ALU.mult, ALU.add,
                    [tb, pm, halo_h], [halo_h])

    def tokc(t):
        return slice(3 + t * 128, 3 + (t + 1) * 128)

    def rstd_of(out_ap, ss_ap, n):
        TS(out_ap, ss_ap, 1.0 / n, EPS, ALU.mult, ALU.add, [smn], [smn])
        ACT(out_ap, out_ap, AF.Ln, [smn], [smn])
        ACT(out_ap, out_ap, AF.Exp, [smn], [smn], scale=-0.5)

    def make_uT_tile(hbuf, h_ap, dst_cols, ncol):
        ACT(junk.t[:], h_ap, AF.Square, [hbuf], [smn], accum_out=ss4[:, 0:1])
        rstd_of(rs4[:, 0:1], ss4[:, 0:1], DM)
        STT(ubf.t[:], h_ap, rs4[:, 0:1], nwbc.t[:], ALU.mult, ALU.mult, [hbuf, smn, nwbc], [ubf])
        for k in range(KC):
            TR(ptu_t[:, k * 128:(k + 1) * 128], ubf.t[:, k * 128:(k + 1) * 128], identb.t[:], [ubf, identb], [PTU])
        src = ptu_t[:, :].rearrange("p (k c) -> p k c", k=KC)[:, :, 0:ncol]
        CP(uT.t[:, :, dst_cols], src, [PTU], [uT])

    def proj_fm(slot, off, M, cols=slice(3, 515), n=512):
        pj = pjn()
        v = slot3(slot)
        for k in range(KC):
            MM(pj.t[0:M, 0:n], v[:, k, off:off + M], uT.t[:, k, cols], k == 0, k == KC - 1, [slot, uT], [pj])
        return pj

    def proj_tm(slot, ncols, t):
        pj = pjn()
        v = slot3(slot)
        for k in range(KC):
            MM(pj.t[:, 0:ncols], uT.t[:, k, tokc(t)], v[:, k, 0:ncols], k == 0, k == KC - 1, [slot, uT], [pj])
        return pj

    def proj_small(c0, c1, t):
        b, ap = pfn()
        n = c1 - c0
        for k in range(KC):
            MM(ap[:, 0:n], uT.t[:, k, tokc(t)], wsm.t[:, k, c0:c1], k == 0, k == KC - 1, [wsm, uT], [b])
        return b, ap

    def conv_block(s_idx, slot, off, hidx, wcol, bcol, out_ap, out_buf):
        pj = proj_fm(slot, off, 128)
        raw = fbn()
        if s_idx == 0:
            b, ap = pfn()
            v = slot3(slot)
            for k in range(KC):
                MM(ap[:, 0:3], v[:, k, off:off + 128], uT.t[:, k, 0:3], k == 0, k == KC - 1, [slot, uT], [b])
            CP(raw.t[:, 0:3], ap[:, 0:3], [b], [raw])
        else:
            CP(raw.t[:, 0:3], halo_raw.t[:, hidx, :], [halo_raw], [raw])
        ACT(raw.t[:, 3:515], pj.t[:, :], AF.Copy, [pj], [raw])
        CP(halo_raw.t[:, hidx, :], raw.t[:, 512:515], [raw], [halo_raw])
        acc = fbn()
        w = lambda k: pp.t[:, wcol + k:wcol + k + 1]
        ACT(acc.t[:, 0:512], raw.t[:, 0:512], AF.Identity, [raw, pp], [acc], scale=w(0), bias=pp.t[:, bcol:bcol + 1])
        for k in (1, 2, 3):
            STT(acc.t[:, 0:512], raw.t[:, k:k + 512], w(k), acc.t[:, 0:512], ALU.mult, ALU.add, [raw, pp, acc], [acc])
        ACT(out_ap, acc.t[:, 0:512], AF.Silu, [acc], [out_buf])

    def finish_mixer(s_idx, m, t, mix):
        if debug:
            tg = s_idx * TPS + t
            dbg_ids.append(S.dma("sync", lambda e: e.dma_start(out=d_dbg[tg * 128:(tg + 1) * 128, m * 512:(m + 1) * 512], in_=mix.t[:]),
                                 mix.sem, [mix], []))
        for b4 in range(4):
            pb, pap = ptsn()
            TR(pap, mix.t[:, b4 * 128:(b4 + 1) * 128], identb.t[:], [mix, identb], [pb])
            ACT(mixT.t[:, b4, t * 128:(t + 1) * 128], pap, AF.Copy, [pb], [mixT])

    def out_proj(s_idx, m):
        s = load_wo(m)
        v = slot_wo(s)
        for t in range(TPS):
            tg = s_idx * TPS + t
            for half in range(2):
                pj = pjn()
                for kc in range(4):
                    MM(pj.t[:, :], mixT.t[:, kc, t * 128:(t + 1) * 128], v[:, kc, half * 512:(half + 1) * 512],
                       kc == 0, kc == 3, [s, mixT], [pj])
                hap = hres_t[:, tg, half * 512:(half + 1) * 512]
                TT(hap, hap, pj.t[:, :], ALU.add, [HT[tg], pj], [HT[tg]])

    def scalar_decay(t, nh, a_ap, a_buf, groups, ybuf, yap, dvh, pad):
        bpa, pa = pfn()
        MM(pa[:, 0:nh], trif.t[:], a_ap, True, True, [trif, a_buf], [bpa])
        MM(pa[:, 64:64 + nh], onesf.t[:], a_ap, True, True, [onesf, a_buf], [bpa])
        CP(acs_s[:, 0:nh], pa[:, 0:nh], [bpa], [smd])
        if P2:
            ACT(ee_s[:, 0:nh], pa[:, 0:nh], AF.Exp, [bpa], [smd])
        ACT(etot_s[:, 0:nh], pa[:, 64:64 + nh], AF.Exp, [bpa], [smd])
        TT(dd_s[:, 0:nh], pa[:, 64:64 + nh], acs_s[:, 0:nh], ALU.subtract, [bpa, smd], [smd])
        ACT(dec_s[:, 0:nh], dd_s[:, 0:nh], AF.Exp, [smd], [smd])
        for g in groups:
            ncg = g["ncg"]
            heads = g["heads"]
            if P2:
                bqk, pqk = pfn()
                MM(pqk, g["kT"], g["qT"], True, True, g["qkbufs"], [bqk])
                byo, pyo = pwn()
                MM(pyo[:, 0:ncg], g["qT"], g["sb"][:, 0:ncg], True, True, g["qkbufs"] + [g["stb"]], [byo])
                byd, pyd = pwn()
            vd = nxt(VDB, "vdb")
            for j, h in enumerate(heads):
                cs = slice(j * pad, j * pad + dvh)
                if P2:
                    lh = nxt(SMF, "smf")
                    ACT(lh.t[:], suf.t[:], AF.Identity, [suf, a_buf], [lh], scale=a_ap[:, h:h + 1])
                    bsg, psg = pfn()
                    MM(psg, lh.t[:], trif.t[:], True, False, [lh, trif], [bsg])
                    MM(psg, identb.t[:], negb_.t[:], False, True, [identb, negb_], [bsg])
                    esg = nxt(SMF, "smf")
                    ACT(esg.t[:], psg, AF.Exp, [bsg], [esg])
                    wb = nxt(SMB, "smb")
                    TT(wb.t[:], pqk, esg.t[:], ALU.mult, [bqk, esg], [wb])
                    MM(pyd[:, cs], wb.t[:], g["v"][:, cs], True, True, [wb, g["vbuf"]], [byd])
                if len(heads) == 1:
                    TS(vd.t[:, cs], g["v"][:, cs], dec_s[:, h:h + 1], None, ALU.mult, None, [g["vbuf"], smd], [vd])
            if len(heads) > 1:
                nhh = len(heads)
                so = SML["dec"][0] + heads[0]
                decb = bass.AP(sml.t, so, [[512, 128], [1, nhh], [0, dvh]])
                TT(vd.t[:, 0:ncg].rearrange("p (h c) -> p h c", c=dvh),
                   g["v"][:, 0:ncg].rearrange("p (h c) -> p h c", c=dvh), decb, ALU.mult, [g["vbuf"], smd], [vd])
            bu, pu = pwn()
            MM(pu[:, 0:ncg], g["ktok"], vd.t[:, 0:ncg], True, True, g["ktokbufs"] + [vd], [bu])
            if P2:
                for j, h in enumerate(heads):
                    cs = slice(j * pad, j * pad + dvh)
                    ACT(yap(h), pyd[:, cs], AF.Copy, [byd], [ybuf])
                    STT(yap(h), pyo[:, cs], ee_s[:, h:h + 1], yap(h), ALU.mult, ALU.add, [byo, smd, ybuf], [ybuf])
            for j, h in enumerate(heads):
                cs = slice(j * pad, j * pad + dvh)
                STT(g["sf"][:, cs], g["sf"][:, cs], etot_s[:, h:h + 1], pu[:, cs], ALU.mult, ALU.add,
                    [g["stf"], smd, bu], [g["stf"]])
                if not P2:
                    dc = stf_t[:, NSTATE + g["decoff"] + j:NSTATE + g["decoff"] + j + 1]
                    TT(dc, dc, etot_s[:, h:h + 1], ALU.mult, [STDEC, smd], [STDEC])
            ACT(g["sb"], g["sf"], AF.Copy, [g["stf"]], [g["stb"]])


    xst = sb([128, 4, 512], F32, "xst")
    yb = sb([128, 4, 132], F32, "yb")
    VTD = [sb([128, 512], BF16, f"vtd{i}") for i in range(2)]
    DBGS = [S.dma_sem(f"dbg{i}") for i in range(2)] if debug else None
    for i, mb in enumerate(MIXB):
        mb.sem = DBGS[i] if debug else None

    def recur_vd(s_idx, m, hpb, dk, qTb, kTb, kdtokb, vTb, zGb, soff, doff, nwoff, midx):
        for t in range(TPS):
            pouts = []
            for h in range(4):
                blk = h // hpb
                r0 = (h % hpb) * dk
                rows = slice(r0, r0 + dk)
                hc = slice(h * 128, (h + 1) * 128)
                scol = soff + blk * 128
                sf = stf_t[rows, scol:scol + 128]
                sbf = stb_t[rows, scol:scol + 128]
                tcs = slice(t * 128, (t + 1) * 128)
                if P2:
                    bsc, psc = pfn()
                    MM(psc, kTb.t[rows, blk, tcs], qTb.t[rows, blk, tcs], True, True, [kTb, qTb], [bsc])
                    sm = nxt(SMB, "smb")
                    TT(sm.t[:], psc, trif.t[:], ALU.mult, [bsc, trif], [sm])
                    bo, po = pwn()
                for c in range(2):
                    tr = slice(c * 64, (c + 1) * 64)
                    if P2:
                        MM(po[tr, 0:128], sm.t[tr, tr], vTb.t[tr, t, hc], True, False, [sm, vTb], [bo])
                        MM(po[tr, 0:128], qTb.t[rows, blk, t * 128 + c * 64:t * 128 + (c + 1) * 64], sbf,
                           False, True, [qTb, STB[(m, h)]], [bo])
                    bu, pu = pfn()
                    MM(pu[rows, 0:128], kdtokb.t[tr, t, blk * 128 + r0:blk * 128 + r0 + dk], vTb.t[tr, t, hc],
                       True, True, [kdtokb, vTb], [bu])
                    eg = egl_t[rows, blk, t * 2 + c:t * 2 + c + 1]
                    STT(sf, sf, eg, pu[rows, 0:128], ALU.mult, ALU.add, [STF[(m, h)], EGLB[blk], bu], [STF[(m, h)]])
                    ACT(sbf, sf, AF.Copy, [STF[(m, h)]], [STB[(m, h)]])
                    if not P2:
                        dc = stf_t[rows, NSTATE + doff + blk:NSTATE + doff + blk + 1]
                        TT(dc, dc, eg, ALU.mult, [STDEC, EGLB[blk]], [STDEC])
                if P2:
                    ACT(junk.t[:, 0:128], po[:, 0:128], AF.Square, [bo], [smn], accum_out=ss4[:, h:h + 1])
                    pouts.append((bo, po))
            if P2:
                rstd_of(rs4[:, 0:4], ss4[:, 0:4], 128)
                nwz = nxt(TTB, "ttb")
                TT(nwz.t[:], pr.t[:, nwoff:nwoff + 512], zGb.t[:, t, 0:512], ALU.mult, [pr, zGb], [nwz])
                mix = nxt(MIXB, "mixb")
                for h in range(4):
                    hc = slice(h * 128, (h + 1) * 128)
                    STT(mix.t[:, hc], pouts[h][1][:, 0:128], rs4[:, h:h + 1], nwz.t[:, hc], ALU.mult, ALU.mult,
                        [pouts[h][0], smn, nwz], [mix])
                finish_mixer(s_idx, midx, t, mix)
        if P2:
            out_proj(s_idx, midx)

    def vd_front(blk, kf, csb, sG, qslot, qoff, qscale, qTb, kTb, kdTb):
        ACT(egl_t[:, blk, :], csb.t[:, 63:512:64], AF.Exp, [csb], [EGLB[blk]], scale=sG)
        tmp = fbn()
        if STOP == 31:
            return
        if P2:
            qf = fbn()
            pq = proj_fm(qslot, qoff, 128)
            ACT(qf.t[:, 0:512], pq.t[:, :], AF.Identity, [pq], [qf], scale=qscale)
            if STOP == 32:
                return
            ACT(tmp.t[:, 0:512], csb.t[:, 0:512], AF.Exp, [csb], [tmp], scale=sG)
            TT(qTb.t[:, blk, 0:512], qf.t[:, 0:512], tmp.t[:, 0:512], ALU.mult, [qf, tmp], [qTb])
            if STOP == 33:
                return
            ACT(tmp.t[:, 0:512], csb.t[:, 0:512], AF.Exp, [csb], [tmp], scale=-sG)
            TT(kTb.t[:, blk, 0:512], kf.t[:, 0:512], tmp.t[:, 0:512], ALU.mult, [kf, tmp], [kTb])
        if STOP == 34:
            return
        glb = bass.AP(csb.t, 63, [[515, 128], [64, 8], [0, 64]])
        cs3 = csb.t[:, 0:512].rearrange("p (c k) -> p c k", k=64)
        TT(tmp.t[:, 0:512].rearrange("p (c k) -> p c k", k=64), glb, cs3, ALU.subtract, [csb], [tmp])
        if STOP == 35:
            return
        ACT(tmp.t[:, 0:512], tmp.t[:, 0:512], AF.Exp, [tmp], [tmp], scale=sG)
        if STOP == 36:
            return
        TT(kdTb.t[:, blk, 0:512], kf.t[:, 0:512], tmp.t[:, 0:512], ALU.mult, [kf, tmp], [kdTb])
        if STOP == 37:
            return

    def scan(csb, src):
        S.op("vector", lambda e: e.tensor_tensor_scan(out=csb.t[:, 0:512], data0=rst.t[:, :], data1=src.t[:, 0:512],
                                                      initial=0.0, op0=ALU.mult, op1=ALU.add), [rst, src], [csb],
             cost=1250)

    def vz_and_kdtok(nblk, vslot_cols, zslot_cols, vTb, zGb, kdTb, kdtokb):
        s_v = load_w(*vslot_cols)
        for t in range(TPS):
            pv = proj_tm(s_v, 512, t)
            ACT(vTb.t[:, t, 0:512], pv.t[:, :], AF.Copy, [pv], [vTb])
        if STOP == 41:
            return
        if P2:
            s_z = load_w(*zslot_cols)
            for t in range(TPS):
                pz = proj_tm(s_z, 512, t)
                ACT(zGb.t[:, t, 0:512], pz.t[:, :], AF.Silu, [pz], [zGb])
        if STOP == 42:
            return
        for t in range(TPS):
            for blk in range(nblk):
                pb, pap = ptsn()
                TR(pap, kdTb.t[:, blk, t * 128:(t + 1) * 128], identb.t[:], [kdTb, identb], [pb])
                ACT(kdtokb.t[:, t, blk * 128:(blk + 1) * 128], pap, AF.Copy, [pb], [kdtokb])

    def mixer_A(s_idx):
        qTb, kTb, kdTb, vTb, zGb, kdtokb = BFB
        s_qk = load_w(0, 512)
        pg = pjn()
        for k in range(KC):
            MM(pg.t[0:16, :], wsm.t[:, k, 0:16], uT.t[:, k, 3:515], k == 0, k == KC - 1, [wsm, uT], [pg])
        ACT(grt.t[:], pg.t[0:16, :], AF.Copy, [pg], [grt])
        if STOP < 1:
            return
        for blk in range(2):
            kf = fbn()
            pk = proj_fm(s_qk, 256 + blk * 128, 128)
            ACT(kf.t[:, 0:512], pk.t[:, :], AF.Copy, [pk], [kf])
            px = pjn()
            MM(px.t[:, :], gwb.t[:, blk * 128:(blk + 1) * 128], grt.t[:], True, True, [gwb, grt], [px])
            sp = fbn()
            ACT(sp.t[:, 0:512], px.t[:, :], AF.Exp, [px, smp], [sp], scale=-1.0, bias=negb[:, blk:blk + 1])
            ACT(sp.t[:, 0:512], sp.t[:, 0:512], AF.Ln, [sp], [sp], bias=1.0)
            csb = fbn()
            if STOP < 2:
                continue
            scan(csb, sp)
            if STOP < 3:
                continue
            vd_front(blk, kf, csb, -1.0 / 16, s_qk, blk * 128, 0.125, qTb, kTb, kdTb)
        if STOP < 4 or 30 < STOP < 40:
            return
        vz_and_kdtok(2, (512, 1024), (1040, 1552), vTb, zGb, kdTb, kdtokb)
        if STOP < 5 or 40 < STOP < 50:
            return
        recur_vd(s_idx, "A", 2, 64, qTb, kTb, kdtokb, vTb, zGb, OFF_A, DEC_A, PR_NWA, 0)

    def mixer_C(s_idx):
        qTb, kTb, kdTb, vTb, zGb, kdtokb = BFB
        s_f = load_w(4632, 5144)
        s_q = load_w(4120, 4632) if P2 else None
        for blk in range(4):
            pf_ = proj_fm(s_f, blk * 128, 128)
            ff = fbn()
            ACT(ff.t[:, 0:512], pf_.t[:, :], AF.Sigmoid, [pf_], [ff])
            TS(ff.t[:, 0:512], ff.t[:, 0:512], omlb[:, blk:blk + 1], lbv[:, blk:blk + 1], ALU.mult, ALU.add,
               [ff, smp], [ff])
            kf = fbn()
            TS(kf.t[:, 0:512], ff.t[:, 0:512], -1.0, 1.0, ALU.mult, ALU.add, [ff], [kf])
            lf = fbn()
            ACT(lf.t[:, 0:512], ff.t[:, 0:512], AF.Ln, [ff], [lf])
            csb = fbn()
            scan(csb, lf)
            vd_front(blk, kf, csb, 1.0, s_q, blk * 128, 128 ** -0.5, qTb, kTb, kdTb)
        vz_and_kdtok(4, (5144, 5656), (5656, 6168), vTb, zGb, kdTb, kdtokb)
        recur_vd(s_idx, "C", 1, 128, qTb, kTb, kdtokb, vTb, zGb, OFF_C, DEC_C, PR_NWC, 2)

    def mixer_B(s_idx):
        qTb, kTb, ktokb, vTb, oGb, zGb = BFB
        if P2:
            s_q = load_w(1552, 2064)
            for blk in range(4):
                conv_block(s_idx, s_q, blk * 128, blk, PP_MCW + blk * 4, PP_MCB + blk, qTb.t[:, blk, 0:512], qTb)
        s_k = load_w(2064, 2576)
        for blk in range(4):
            conv_block(s_idx, s_k, blk * 128, 4 + blk, PP_MCW + (4 + blk) * 4, PP_MCB + 4 + blk,
                       kTb.t[:, blk, 0:512], kTb)
        s_v = load_w(2576, 3088)
        for t in range(TPS):
            bg, pgi = proj_small(16, 24, t)
            gi = gates_t[:, t * 16:t * 16 + 4]
            TT(gi, pgi[:, 0:4], ib2, ALU.add, [bg, smp], [GATES])
            ACT(gi, gi, AF.Exp, [GATES], [GATES])
            gf = gates_t[:, t * 16 + 4:t * 16 + 8]
            TT(gf, pgi[:, 4:8], pr.t[:, PR_FB:PR_FB + 4], ALU.add, [bg, pr], [GATES])
            ACT(gf, gf, AF.Exp, [GATES], [GATES], scale=-1.0)
            ACT(gf, gf, AF.Ln, [GATES], [GATES], bias=1.0)
            TS(gf, gf, -1.0, None, ALU.mult, None, [GATES], [GATES])
            pv = proj_tm(s_v, 512, t)
            vt4 = vTb.t[:, t, :].rearrange("p (h c) -> p h c", c=132)
            for h in range(4):
                TS(vt4[:, h, 0:128], pv.t[:, h * 128:(h + 1) * 128], gi[:, h:h + 1], None, ALU.mult, None,
                   [pv, GATES], [vTb])
            CP(vt4[:, :, 128], gi, [GATES], [vTb])
        if P2:
            s_o = load_w(3096, 3608)
            for t in range(TPS):
                po_ = proj_tm(s_o, 512, t)
                ACT(oGb.t[:, t, 0:512], po_.t[:, :], AF.Sigmoid, [po_], [oGb])
            s_z = load_w(3608, 4120)
            for t in range(TPS):
                pz = proj_tm(s_z, 512, t)
                ACT(zGb.t[:, t, 0:512], pz.t[:, :], AF.Silu, [pz], [zGb])
        for t in range(TPS):
            for h in range(4):
                pb, pap = ptsn()
                TR(pap, kTb.t[:, h, t * 128:(t + 1) * 128], identb.t[:], [kTb, identb], [pb])
                ACT(ktokb.t[:, t, h * 128:(h + 1) * 128], pap, AF.Copy, [pb], [ktokb])
        for t in range(TPS):
            tcs = slice(t * 128, (t + 1) * 128)
            vt4 = vTb.t[:, t, :].rearrange("p (h c) -> p h c", c=132)
            groups = []
            for h in range(4):
                groups.append(dict(qT=qTb.t[:, h, tcs], kT=kTb.t[:, h, tcs], qkbufs=[qTb, kTb],
                                   ktok=ktokb.t[:, t, h * 128:(h + 1) * 128], ktokbufs=[ktokb],
                                   v=vt4[:, h, :], vbuf=vTb, heads=[h], ncg=129,
                                   sf=stf_t[:, OFF_B + h * 132:OFF_B + (h + 1) * 132],
                                   sb=stb_t[:, OFF_B + h * 132:OFF_B + (h + 1) * 132],
                                   stf=STF[("B", h)], stb=STB[("B", h)], decoff=DEC_B + h))
            a_ap = gates_t[:, t * 16 + 4:t * 16 + 8]
            scalar_decay(t, 4, a_ap, GATES, groups, yb, lambda h: yb.t[:, h, 0:129], 129, 132)
            if P2:
                TS(rr4, yb.t[:, :, 128], -1.0, None, ALU.mult, None, [yb], [smn])
                TT(rr4, rr4, yb.t[:, :, 128], ALU.max, [yb, smn], [smn])
                TS(rr4, rr4, 1.0, None, ALU.max, None, [smn], [smn])
                S.op("vector", lambda e: e.reciprocal(out=rr4, in_=rr4), [smn], [smn])
                hb = nxt(TTB, "ttb")
                for h in range(4):
                    hc = slice(h * 128, (h + 1) * 128)
                    STT(hb.t[:, hc], yb.t[:, h, 0:128], rr4[:, h:h + 1], oGb.t[:, t, hc], ALU.mult, ALU.mult,
                        [yb, smn, oGb], [hb])
                for h in range(4):
                    hc = slice(h * 128, (h + 1) * 128)
                    ACT(junk.t[:, 0:128], hb.t[:, hc], AF.Square, [hb], [smn], accum_out=ss4[:, h:h + 1])
                rstd_of(rs4[:, 0:4], ss4[:, 0:4], 128)
                nwz = nxt(TTB, "ttb")
                TT(nwz.t[:], pr.t[:, PR_NWB:PR_NWB + 512], zGb.t[:, t, 0:512], ALU.mult, [pr, zGb], [nwz])
                mix = nxt(MIXB, "mixb")
                for h in range(4):
                    hc = slice(h * 128, (h + 1) * 128)
                    STT(mix.t[:, hc], hb.t[:, hc], rs4[:, h:h + 1], nwz.t[:, hc], ALU.mult, ALU.mult,
                        [hb, smn, nwz], [mix])
                finish_mixer(s_idx, 1, t, mix)
        if P2:
            out_proj(s_idx, 1)

    def mixer_D(s_idx):
        bcTb, btokb, _, _, _, zGb = BFB
        s_x = load_w(6168, 6680)
        for blk in range(4):
            conv_block(s_idx, s_x, blk * 128, 8 + blk, PP_SCW + blk * 4, PP_SCB + blk, xst.t[:, blk, :], xst)
        s_bc = load_w(6680, 7192)
        for blk in (range(4) if P2 else range(2)):
            conv_block(s_idx, s_bc, blk * 128, 12 + blk, PP_SCW + (4 + blk) * 4, PP_SCB + 4 + blk,
                       bcTb.t[:, blk, 0:512], bcTb)
        if P2:
            s_z = load_w(7200, 7712)
            for t in range(TPS):
                pz = proj_tm(s_z, 512, t)
                ACT(zGb.t[:, t, 0:512], pz.t[:, :], AF.Silu, [pz], [zGb])
        for t in range(TPS):
            for g in range(2):
                pb, pap = ptsn()
                TR(pap, bcTb.t[:, g, t * 128:(t + 1) * 128], identb.t[:], [bcTb, identb], [pb])
                ACT(btokb.t[:, t, g * 128:(g + 1) * 128], pap, AF.Copy, [pb], [btokb])
        for t in range(TPS):
            tcs = slice(t * 128, (t + 1) * 128)
            bd, pdt = proj_small(24, 32, t)
            dtv = gates_t[:, 256 + t * 32:256 + t * 32 + 8]
            TT(dtv, pdt[:, 0:8], pr.t[:, PR_DTB:PR_DTB + 8], ALU.add, [bd, pr], [GATESD])
            ACT(dtv, dtv, AF.Exp, [GATESD], [GATESD])
            ACT(dtv, dtv, AF.Ln, [GATESD], [GATESD], bias=1.0)
            av = gates_t[:, 256 + t * 32 + 8:256 + t * 32 + 16]
            TT(av, dtv, aneg, ALU.mult, [GATESD, smp], [GATESD])
            xs = nxt(TTB, "ttb")
            for blk in range(4):
                bx, pxr = pwn()
                S.op("tensor", lambda e, pxr=pxr, blk=blk, tcs=tcs: e.transpose(pxr[:, 0:128], xst.t[:, blk, tcs], identf.t[:]),
                     [xst, identf], [bx], cost=400)
                CP(xs.t[:, blk * 128:(blk + 1) * 128], pxr[:, 0:128], [bx], [xs])
            vt = nxt(VTD, "vtd")
            dtb = bass.AP(gates_t, 256 + t * 32, [[512, 128], [1, 8], [0, 64]])
            TT(vt.t[:, :].rearrange("p (h c) -> p h c", c=64), xs.t[:, :].rearrange("p (h c) -> p h c", c=64),
               dtb, ALU.mult, [xs, GATESD], [vt])
            groups = []
            for g in range(2):
                groups.append(dict(qT=bcTb.t[:, 2 + g, tcs], kT=bcTb.t[:, g, tcs], qkbufs=[bcTb],
                                   ktok=btokb.t[:, t, g * 128:(g + 1) * 128], ktokbufs=[btokb],
                                   v=vt.t[:, g * 256:(g + 1) * 256], vbuf=vt, heads=[4 * g + j for j in range(4)],
                                   ncg=256, sf=stf_t[:, OFF_D + g * 256:OFF_D + (g + 1) * 256],
                                   sb=stb_t[:, OFF_D + g * 256:OFF_D + (g + 1) * 256],
                                   stf=STF[("D", g)], stb=STB[("D", g)], decoff=DEC_D + 4 * g))
            Y = nxt(TTB, "ttb")
            scalar_decay(t, 8, av, GATESD, groups, Y, lambda h: Y.t[:, h * 64:(h + 1) * 64], 64, 64)
            if P2:
                for h in range(8):
                    hs = slice(h * 64, (h + 1) * 64)
                    STT(Y.t[:, hs], xs.t[:, hs], pr.t[:, PR_D + h:PR_D + h + 1], Y.t[:, hs], ALU.mult, ALU.add,
                        [xs, pr, Y], [Y])
                TT(Y.t[:], Y.t[:], zGb.t[:, t, 0:512], ALU.mult, [Y, zGb], [Y])
                for g in range(2):
                    ACT(junk.t[:, 0:256], Y.t[:, g * 256:(g + 1) * 256], AF.Square, [Y], [smn],
                        accum_out=ssn[:, g:g + 1])
                rstd_of(rsn[:, 0:2], ssn[:, 0:2], 256)
                mix = nxt(MIXB, "mixb")
                for g in range(2):
                    gs = slice(g * 256, (g + 1) * 256)
                    STT(mix.t[:, gs], Y.t[:, gs], rsn[:, g:g + 1], pr.t[:, PR_NWD + g * 256:PR_NWD + (g + 1) * 256],
                        ALU.mult, ALU.mult, [Y, smn, pr], [mix])
                finish_mixer(s_idx, 3, t, mix)
        if P2:
            out_proj(s_idx, 3)

    finals = []
    for layer in range(NLAYERS):
        win3 = d_win_all[layer].rearrange("(k p) c -> p k c", p=128)
        wout3 = d_wout_all[layer].rearrange("(k p) c -> p k c", p=128)
        load_layer_params(layer)
        for P2 in (False, True):
            last = P2 and layer == NLAYERS - 1
            init_pass()
            for s_idx in range(NS):
                if s_idx == 0:
                    make_uT_tile(halo_h, halo_h.t[:], slice(0, 3), 3)
                else:
                    CP(uT.t[:, :, 0:3], uT.t[:, :, 512:515], [uT], [uT])
                for t in range(TPS):
                    tg = s_idx * TPS + t
                    make_uT_tile(HT[tg], hres_t[:, tg, :], tokc(t), 128)
                if "A" in MIXERS:
                    mixer_A(s_idx)
                if "B" in MIXERS:
                    mixer_B(s_idx)
                if "C" in MIXERS:
                    mixer_C(s_idx)
                if "D" in MIXERS:
                    mixer_D(s_idx)
            if not P2:
                end_p1()
            elif not last:
                exchange_halo()

    DMA("sync", nwbc.t[:], d_fnw.partition_broadcast(128), nwbc, [], [nwbc])
    for tg in range(NT):
        ACT(junk.t[:], hres_t[:, tg, :], AF.Square, [HT[tg]], [smn], accum_out=ss4[:, 0:1])
        rstd_of(rs4[:, 0:1], ss4[:, 0:1], DM)
        for half in range(2):
            ob = nxt(TTB, "ttb")
            hs = slice(half * 512, (half + 1) * 512)
            STT(ob.t[:], hres_t[:, tg, hs], rs4[:, 0:1], nwbc.t[:, hs], ALU.mult, ALU.mult,
                [HT[tg], smn, nwbc], [ob])
            finals.append(DMA("sync", d_out[tg * 128:(tg + 1) * 128, hs], ob.t[:], ob, [ob], []))
    finals.extend(dbg_ids)
    S.wait_all("sync", finals)
    print("n semaphores", len(S.sems), flush=True)
    S.schedule(reorder=REORDER)
    print("ops:", len(S.ops), "model makespan us:", S.makespan / 1e3, flush=True)
    S.emit()
    S.close()
    es.close()
    return nc


def pack_layer(inp, l):
    f = lambda a: np.ascontiguousarray(np.asarray(a, dtype=np.float32))
    pp = np.zeros((128, NPP), np.float32)
    pp[:, PP_GB:PP_GB + 2] = f(inp["gla_gate_b"][l]).reshape(2, 128).T
    pp[:, PP_MCW:PP_MCW + 32] = f(inp["ml_conv_w"][l]).reshape(4, 8, 128).transpose(2, 1, 0).reshape(128, 32)
    pp[:, PP_MCB:PP_MCB + 8] = f(inp["ml_conv_b"][l]).reshape(8, 128).T
    pp[:, PP_SCW:PP_SCW + 32] = f(inp["ssd_conv_w"][l]).reshape(4, 8, 128).transpose(2, 1, 0).reshape(128, 32)
    pp[:, PP_SCB:PP_SCB + 8] = f(inp["ssd_conv_b"][l]).reshape(8, 128).T
    pp[:, PP_LB:PP_LB + 8] = f(inp["hg_lb_logits"]).reshape(2, 4, 128).transpose(2, 1, 0).reshape(128, 8)
    pr = np.concatenate([f(inp["ml_i_b"][l]), f(inp["ml_f_b"][l]), f(inp["ssd_dt_bias"][l]), f(inp["ssd_A_log"][l]),
                         f(inp["ssd_D"][l]), f(inp["gla_norm_w"][l]), f(inp["ml_norm_w"][l]), f(inp["hg_norm_w"][l]),
                         f(inp["ssd_norm_w"][l])]).astype(np.float32)
    assert pr.shape[0] == NPR
    return pp, pr


def core_pm(q):
    pm = np.zeros((128, 9), np.float32)
    for j in range(3):
        pm[:, j] = 1.0 if j < q else 0.0
        pm[:, 3 + j] = 1.0 - pm[:, j]
        pm[:, 6 + j] = 1.0 if j == q - 1 else 0.0
    return pm


_NC_CACHE = {}


def pack_inputs(inputs):
    f = lambda a: np.ascontiguousarray(np.asarray(a, dtype=np.float32))
    x = f(inputs["x"])
    pps, prs = zip(*[pack_layer(inputs, l) for l in range(2)])
    shared = dict(w_in=f(inputs["w_in"]), w_out=f(inputs["w_out"]), nw=f(inputs["norm_w"]),
                  fnw=f(inputs["final_norm_w"]), pp=np.ascontiguousarray(np.stack(pps)),
                  pr=np.ascontiguousarray(np.stack(prs)), gw=f(inputs["gla_gate_w"]))
    in_maps = []
    for c in range(NCORES):
        b, q = c // 4, c % 4
        d = dict(shared)
        d["hin"] = np.ascontiguousarray(x[b, q * T:(q + 1) * T, :])
        d["halo"] = (np.zeros((3, DM), np.float32) if q == 0
                     else np.ascontiguousarray(x[b, q * T - 3:q * T, :]))
        d["pm"] = core_pm(q)
        in_maps.append(d)
    return in_maps


def kernel(**inputs):
    if "nc" not in _NC_CACHE:
        _NC_CACHE["nc"] = build()
    in_maps = pack_inputs(inputs)
    res = run_bass_kernel_spmd(_NC_CACHE["nc"], in_maps, core_ids=list(range(NCORES)))
    B = np.asarray(inputs["x"]).shape[0]
    out = np.zeros((B, 4 * T, DM), np.float32)
    for c in range(NCORES):
        out[c // 4, (c % 4) * T:(c % 4 + 1) * T, :] = np.asarray(res.results[c]["hout"], dtype=np.float32)
    return out
```

```python
import math
import numpy as np
from contextlib import ExitStack
import concourse.bass as bass
import concourse.mybir as mybir
from concourse.bass_utils import run_bass_kernel_spmd

F32 = mybir.dt.float32
BF16 = mybir.dt.bfloat16
AF = mybir.ActivationFunctionType
ALU = mybir.AluOpType

NCORES = 8
SYNC_LAT = 150.0
SELF_DIST = 3
T = 2048
NT = 16
ST = 512
TPS = 4
NS = 4
DM = 1024
KC = 8
EPS = 1e-6
DPROJ = 7712
NSTATE = 1808
NDEC = 18
NSUM = NSTATE + NDEC
OFF_A, OFF_B, OFF_C, OFF_D = 0, 256, 784, 1296
DEC_A, DEC_B, DEC_C, DEC_D = 0, 2, 6, 10
NPP = 90
NPR = 2080
PP_GB, PP_MCW, PP_MCB, PP_SCW, PP_SCB, PP_LB = 0, 2, 34, 42, 74, 82
PR_IB, PR_FB, PR_DTB, PR_ALOG, PR_D, PR_NWA, PR_NWB, PR_NWC, PR_NWD = 0, 4, 8, 16, 24, 32, 544, 1056, 1568


class Buf:
    __slots__ = ("t", "name", "w", "r", "sem", "excl")

    def __init__(self, t, name, excl=False):
        self.t = t
        self.name = name
        self.w = None
        self.r = []
        self.sem = None
        self.excl = excl


class Sched:
    def __init__(self, nc):
        self.nc = nc
        self.engs = []
        self.sems = {}
        self._ctx = []
        self.nself = {"tensor"}
        self.ops = []

    def add_engine(self, name):
        key = "e_" + name
        self.sems[key] = self._alloc(key)
        self.engs.append(name)

    def _alloc(self, key):
        cm = self.nc.semaphore(key)
        h = cm.__enter__()
        self._ctx.append(cm)
        return h

    def dma_sem(self, name):
        key = "d_" + name
        self.sems[key] = self._alloc(key)
        return key

    def _record(self, eng, fn, kind, sem, reads, writes, cost, lat):
        ex = [r for r in reads if r.excl]
        if ex:
            writes = list(writes) + [r for r in ex if r not in writes]
            reads = [r for r in reads if not r.excl]
        deps = set()
        for r in reads:
            if r.w is not None:
                deps.add(r.w)
        for w in writes:
            if w.w is not None:
                deps.add(w.w)
            deps.update(w.r)
        oid = len(self.ops)
        self.ops.append(dict(id=oid, eng=eng, fn=fn, kind=kind, sem=sem, deps=deps, cost=cost, lat=lat))
        for r in reads:
            r.r.append(oid)
        for w in writes:
            w.w = oid
            w.r = []
        return oid

    def op(self, eng, fn, reads=(), writes=(), cost=300):
        return self._record(eng, fn, "op", "e_" + eng, reads, writes, cost, cost)

    def dma(self, eng, fn, semkey, reads=(), writes=(), cost=100, lat=4000):
        return self._record(eng, fn, "dma", semkey, reads, writes, cost, lat)

    def coll(self, eng, fn, semkey, reads=(), writes=()):
        return self._record(eng, fn, "coll", semkey, reads, writes, 2000, 40000)

    def wait_all(self, eng, ids):
        oid = len(self.ops)
        self.ops.append(dict(id=oid, eng=eng, fn=None, kind="wait", sem=None, deps=set(ids), cost=10, lat=10))
        return oid

    def schedule(self, reorder=True):
        import heapq
        ops = self.ops
        n = len(ops)
        ndeps = [len(o["deps"]) for o in ops]
        users = [[] for _ in range(n)]
        for o in ops:
            for d in o["deps"]:
                users[d].append(o["id"])
        prio = [0.0] * n
        for o in reversed(ops):
            i = o["id"]
            m = 0.0
            for u in users[i]:
                if prio[u] > m:
                    m = prio[u]
            prio[i] = m + o["lat"] + SYNC_LAT
        finish = [0.0] * n
        ready_t = [0.0] * n
        efree = {e: 0.0 for e in self.engs}
        pending = {e: [] for e in self.engs}
        avail = {e: [] for e in self.engs}
        queues = {e: [] for e in self.engs}
        for o in ops:
            if ndeps[o["id"]] == 0:
                heapq.heappush(pending[o["eng"]], (0.0, o["id"]))
        done = 0
        if not reorder:
            for o in ops:
                queues[o["eng"]].append(o["id"])
            self.queues = queues
            self.makespan = 0
            return
        while done < n:
            best = None
            for e in self.engs:
                pe, av = pending[e], avail[e]
                while pe and pe[0][0] <= efree[e]:
                    _, i = heapq.heappop(pe)
                    heapq.heappush(av, (-prio[i], i))
                if av:
                    cand = (efree[e], av[0][1], e, True)
                elif pe:
                    cand = (pe[0][0], pe[0][1], e, False)
                else:
                    continue
                if best is None or cand[:2] < best[:2]:
                    best = cand
            assert best is not None, "scheduler deadlock"
            st, i, e, from_av = best
            if from_av:
                heapq.heappop(avail[e])
            else:
                heapq.heappop(pending[e])
            o = ops[i]
            efree[e] = st + o["cost"]
            finish[i] = st + o["lat"] + SYNC_LAT
            queues[e].append(i)
            done += 1
            for u in users[i]:
                ndeps[u] -= 1
                if finish[i] > ready_t[u]:
                    ready_t[u] = finish[i]
                if ndeps[u] == 0:
                    heapq.heappush(pending[ops[u]["eng"]], (ready_t[u], u))
        self.queues = queues
        self.makespan = max(finish)

    def emit(self):
        nc = self.nc
        ops = self.ops
        semval = {}
        cnt = {}
        for e in self.engs:
            for i in self.queues[e]:
                o = ops[i]
                if o["kind"] == "op":
                    cnt[o["sem"]] = cnt.get(o["sem"], 0) + 1
                    semval[i] = (o["sem"], cnt[o["sem"]], 1)
        for o in ops:
            if o["kind"] in ("dma", "coll"):
                inc = 16 if o["kind"] == "dma" else 1
                cnt[o["sem"]] = cnt.get(o["sem"], 0) + inc
                semval[o["id"]] = (o["sem"], cnt[o["sem"]], inc)
        with nc.Block() as block:
            for e in self.engs:
                deco = getattr(block, e)
                q = self.queues[e]
                own = "e_" + e

                def body(h, q=q, e=e, own=own):
                    seen = {}
                    for i in q:
                        o = ops[i]
                        need = {}
                        for d in o["deps"]:
                            k, v, _ = semval[d]
                            if k == own and e in self.nself:
                                continue
                            if k == own and ops[d]["kind"] == "op" and i in semval and semval[i][0] == own \
                                    and semval[i][1] - v >= SELF_DIST:
                                continue
                            if need.get(k, 0) < v:
                                need[k] = v
                        for k, v in need.items():
                            if seen.get(k, 0) >= v:
                                continue
                            seen[k] = v
                            h.wait_ge(self.sems[k], v)
                        if o["fn"] is not None:
                            k, v, inc = semval[i]
                            o["fn"](h).then_inc(self.sems[k], inc)
                deco(body)

    def close(self):
        for cm in reversed(self._ctx):
            cm.__exit__(None, None, None)


def build(debug=False, MIXERS="ABCD", STOP=99, NLAYERS=2, REORDER=True):
    nc = bass.Bass("TRN2", target_bir_lowering=False)
    es = ExitStack()
    S = Sched(nc)
    for n in ("sync", "gpsimd", "tensor", "vector", "scalar"):
        S.add_engine(n)

    def dram(name, shape, dt=F32, kind="ExternalInput"):
        return nc.dram_tensor(name, list(shape), dt, kind=kind).ap()

    layer = 0
    P2 = False
    last = False

    d_hin = dram("hin", [T, DM])
    d_halo = dram("halo", [3, DM])
    d_win_all = dram("w_in", [2, DM, DPROJ])
    d_wout_all = dram("w_out", [2, 2048, DM])
    d_nw_all = dram("nw", [2, DM])
    d_fnw = dram("fnw", [DM])
    d_pp_all = dram("pp", [2, 128, NPP])
    d_pr_all = dram("pr", [2, NPR])
    d_gw_all = dram("gw", [2, 16, 256])
    d_pm = dram("pm", [128, 9])
    d_out = dram("hout", [T, DM], kind="ExternalOutput")
    if debug:
        d_dbg = dram("dbg", [T, 2048], BF16, kind="ExternalOutput")
    d_sloc = [nc.dram_tensor(f"sloc{l}", [128, NSUM], F32).ap() for l in range(2)]
    d_sgat = [nc.dram_tensor(f"sgat{l}", [4 * 128, NSUM], F32).ap() for l in range(2)]
    d_hloc = nc.dram_tensor("hloc", [3, DM], F32).ap()
    d_hgat = nc.dram_tensor("hgat", [12, DM], F32).ap()
    GROUPS = [[0, 1, 2, 3], [4, 5, 6, 7]]

    cnt = [0]
    dbg_ids = []

    def sb(shape, dt, name=None):
        cnt[0] += 1
        name = "s_" + (name or f"sb{cnt[0]}")
        t = es.enter_context(nc.sbuf_tensor(name, list(shape), dt))
        return Buf(t, name)

    def ps(shape, dt, name):
        return es.enter_context(nc.psum_tensor(name, list(shape), dt))

    def withsem(b):
        b.sem = S.dma_sem(b.name)
        return b

    def fsz(ap):
        n = 1
        for (_, c) in list(ap.ap)[1:]:
            n *= c
        return n

    def is_psum(ap):
        return "PSum" in type(ap.tensor).__name__

    def ACT(out, in_, func, R, W, **kw):
        c = 200 + 0.85 * fsz(out)
        return S.op("scalar", lambda e: e.activation(out=out, in_=in_, func=func, **kw), R, W, cost=c)

    def _dve_cost(out, ins):
        c = 70 + 1.05 * fsz(out)
        if any(is_psum(a) for a in ins):
            c += 60
        return c

    def TT(out, in0, in1, op, R, W, eng="vector"):
        return S.op(eng, lambda e: e.tensor_tensor(out=out, in0=in0, in1=in1, op=op), R, W,
                    cost=_dve_cost(out, [in0, in1]))

    def TS(out, in0, s1, s2, op0, op1, R, W, eng="vector"):
        c = _dve_cost(out, [in0])
        if s2 is None:
            return S.op(eng, lambda e: e.tensor_scalar(out=out, in0=in0, scalar1=s1, scalar2=None, op0=op0), R, W, cost=c)
        return S.op(eng, lambda e: e.tensor_scalar(out=out, in0=in0, scalar1=s1, scalar2=s2, op0=op0, op1=op1), R, W, cost=c)

    def STT(out, in0, scalar, in1, op0, op1, R, W, eng="vector"):
        return S.op(eng, lambda e: e.scalar_tensor_tensor(out=out, in0=in0, scalar=scalar, in1=in1, op0=op0, op1=op1), R, W,
                    cost=_dve_cost(out, [in0, in1]))

    def CP(out, in_, R, W, eng="vector"):
        return S.op(eng, lambda e: e.tensor_copy(out=out, in_=in_), R, W, cost=_dve_cost(out, [in_]))

    def MSET(ap, val, W, eng="gpsimd"):
        return S.op(eng, lambda e: e.memset(ap, val), (), W, cost=200 + fsz(ap))

    def MM(out, lhsT, rhs, start, stop, R, W):
        n = fsz(rhs)
        f32 = "float32" in str(rhs.tensor.dtype)
        c = (64 + 1.7 * n) if f32 else (64 + 0.45 * n)
        return S.op("tensor", lambda e: e.matmul(out, lhsT=lhsT, rhs=rhs, start=start, stop=stop), R, W, cost=c)

    def TR(out, in_, ident, R, W):
        return S.op("tensor", lambda e: e.transpose(out, in_, ident), R, W, cost=150)

    def DMA(q, out, in_, buf_sem, R, W):
        nbytes = fsz(out) * 128 * 4
        issue = 1000 if q == "gpsimd" else 80
        return S.dma(q, lambda e: e.dma_start(out=out, in_=in_), buf_sem.sem, R, W, cost=issue,
                     lat=issue + 2000 + nbytes / 120.0)

    identf = sb([128, 128], F32, "identf")
    identb = sb([128, 128], BF16, "identb")
    trif = sb([128, 128], F32, "trif")
    suf = sb([128, 128], F32, "suf")
    negb_ = sb([128, 128], BF16, "negmask")
    onesf = sb([128, 128], F32, "onesf")
    rst = sb([128, 512], F32, "rst")
    MSET(onesf.t[:], 1.0, [onesf])
    MSET(identf.t[:], 1.0, [identf])
    S.op("gpsimd", lambda e: e.affine_select(out=identf.t[:], in_=identf.t[:], pattern=[[-1, 128]],
                                             compare_op=ALU.is_equal, fill=0.0, base=0, channel_multiplier=1),
         [identf], [identf])
    CP(identb.t[:], identf.t[:], [identf], [identb])
    MSET(trif.t[:], 1.0, [trif])
    S.op("gpsimd", lambda e: e.affine_select(out=trif.t[:], in_=trif.t[:], pattern=[[1, 128]],
                                             compare_op=ALU.is_ge, fill=0.0, base=0, channel_multiplier=-1),
         [trif], [trif])
    TS(suf.t[:], trif.t[:], -1.0, 1.0, ALU.mult, ALU.add, [trif], [suf])
    TS(negb_.t[:], suf.t[:], -30000.0, None, ALU.mult, None, [suf], [negb_])
    MSET(rst.t[:], 1.0, [rst])
    rst3 = rst.t[:, :].rearrange("p (c k) -> p c k", k=64)
    MSET(rst3[:, :, 0:1], 0.0, [rst])

    hres_t = es.enter_context(nc.sbuf_tensor("hres", [128, NT, DM], F32))
    HT = [withsem(Buf(hres_t, f"ht{t}")) for t in range(NT)]
    nwbc = withsem(sb([128, DM], F32, "nwbc"))
    uT = sb([128, KC, 515], BF16, "uT")
    ubf = sb([128, DM], BF16, "ubf")
    junk = sb([128, DM], BF16, "junk")
    NSLOT = 3
    slots = [withsem(sb([128, 4096], BF16, f"wslot{i}")) for i in range(NSLOT)]
    wsm = withsem(sb([128, KC, 32], BF16, "wsm"))
    pp = withsem(sb([128, NPP], F32, "pp"))
    pr = withsem(sb([128, NPR], F32, "pr"))
    gwf = withsem(sb([16, 256], F32, "gwf"))
    gwb = sb([16, 256], BF16, "gwb")
    grt = sb([16, 512], BF16, "grt")
    pm = withsem(sb([128, 9], F32, "pm"))
    stf_t = es.enter_context(nc.sbuf_tensor("stf", [128, NSUM], F32))
    stb_t = es.enter_context(nc.sbuf_tensor("stb", [128, NSTATE], BF16))
    STF = {(m, h): Buf(stf_t, "stf%s%d" % (m, h)) for m in "ABCD" for h in range(4)}
    STB = {(m, h): Buf(stb_t, "stb%s%d" % (m, h)) for m in "ABCD" for h in range(4)}
    STDEC = Buf(stf_t, "stdec")
    stsem = withsem(Buf(stf_t, "stout"))
    halo_raw = sb([128, 16, 3], F32, "halo_raw")
    BFB = [sb([128, TPS, 528], BF16, f"bfb{i}") for i in range(6)]
    FB = [sb([128, 515], F32, f"fb{i}") for i in range(6)]
    TTB = [sb([128, 512], F32, f"ttb{i}") for i in range(4)]
    SMF = [sb([128, 128], F32, f"smf{i}") for i in range(4)]
    SMB = [sb([128, 128], BF16, f"smb{i}") for i in range(4)]
    VDB = [sb([128, 512], BF16, f"vdb{i}") for i in range(2)]
    MIXB = [sb([128, 512], BF16, f"mix{i}") for i in range(2)]
    mixT = sb([128, 4, ST], BF16, "mixT")
    sml = sb([128, 512], F32, "sml")
    smp = Buf(sml.t, "smp")
    smn = Buf(sml.t, "smn")
    smd = Buf(sml.t, "smd")
    SML = {}
    _smo = [0]

    def small(name, n):
        o = _smo[0]
        _smo[0] += n
        assert _smo[0] <= 512
        SML[name] = (o, n)
        return sml.t[:, o:o + n]

    gates_t = es.enter_context(nc.sbuf_tensor("gates", [128, 512], F32))
    GATES = Buf(gates_t, "gatesB")
    GATESD = Buf(gates_t, "gatesD")
    egl_t = es.enter_context(nc.sbuf_tensor("egl", [128, 4, 8], F32))
    EGLB = [Buf(egl_t, "egl%d" % i) for i in range(4)]
    halo_h = withsem(sb([128, DM], F32, "halo_h"))
    for i in range(4):
        withsem(TTB[i])

    pj_t = [ps([128, 512], F32, f"pj{i}") for i in range(2)]
    PJ = [Buf(t, f"pj{i}", excl=True) for i, t in enumerate(pj_t)]
    ptu_t = ps([128, 1024], BF16, "ptu")
    PTU = Buf(ptu_t, "ptu", excl=True)
    pts_t = ps([128, 1024], BF16, "pts")
    PTSB = Buf(pts_t, "pts", excl=True)
    PTS = []
    for i in range(8):
        PTS.append((PTSB, i * 128))
        PTS.append((PTU, i * 128))
    pf_t = [ps([128, 512], F32, f"pf{i}") for i in range(2)]
    PFB = [Buf(pf_t[i], f"pf{i}", excl=True) for i in range(2)]
    PF = [(PFB[i % 2], (i // 2) * 128) for i in range(8)]
    pw_t = [ps([128, 512], F32, f"pw{i}") for i in range(2)]
    PWB = [Buf(pw_t[i], f"pw{i}", excl=True) for i in range(2)]
    PW = [(PWB[i % 2], (i // 2) * 256) for i in range(4)]
    rot = {}

    def nxt(pool, key):
        i = rot.get(key, 0)
        rot[key] = i + 1
        return pool[i % len(pool)]

    def pjn():
        return nxt(PJ, "pj")

    def pfn():
        b, o = nxt(PF, "pf")
        return b, b.t[:, o:o + 128]

    def pwn():
        b, o = nxt(PW, "pw")
        return b, b.t[:, o:o + 256]

    def ptsn():
        b, o = nxt(PTS, "pts")
        return b, b.t[:, o:o + 128]

    def fbn():
        return nxt(FB, "fb")

    def slot3(s):
        return s.t[:, :].rearrange("p (k c) -> p k c", k=KC)

    def slot_wo(s):
        return s.t[:, :].rearrange("p (k c) -> p k c", k=4)

    win3 = None
    wout3 = None

    wcache = {}
    slot_hw = {sl.name: withsem(Buf(None, sl.name + "_hw")) for sl in slots}
    d_wscr = nc.dram_tensor("wscr", [48, 128, 4096], BF16).ap()

    def _load_cached(key, slot_view, src_ap, n):
        s = nxt(slots, "slot")
        dst = slot_view(s)
        if key not in wcache:
            scr = d_wscr[len(wcache)]
            sbuf = withsem(Buf(None, "wscr%d" % len(wcache)))
            wcache[key] = (scr, sbuf)
            DMA("gpsimd", dst, src_ap, s, [], [s])
            DMA("sync", scr, s.t[:, :], sbuf, [s], [sbuf])
        else:
            scr, sbuf = wcache[key]
            DMA("sync", s.t[:, :], scr, slot_hw[s.name], [sbuf], [s])
        return s

    def load_w(c0, c1):
        n = c1 - c0
        return _load_cached((layer, "i", c0, c1), lambda s: slot3(s)[:, :, 0:n], win3[:, :, c0:c1], n)

    def load_wo(m):
        return _load_cached((layer, "o", m), lambda s: slot_wo(s)[:, :, :], wout3[:, m * 4:(m + 1) * 4, :], 4096)

    DMA("sync", pm.t[:], d_pm, pm, [], [pm])
    for t in range(NT):
        DMA("sync", hres_t[:, t, :], d_hin[t * 128:(t + 1) * 128, :], HT[t], [], [HT[t]])
    MSET(halo_h.t[:], 0.0, [halo_h])
    DMA("sync", halo_h.t[0:3, :], d_halo, halo_h, [], [halo_h])

    negb = small("negb", 2)
    ib2 = small("ib2", 4)
    aneg = small("aneg", 8)
    lbv = small("lbv", 4)
    omlb = small("omlb", 4)
    ss4 = small("ss4", 4)
    rs4 = small("rs4", 4)
    rr4 = small("rr4", 4)
    ssn = small("ssn", 2)
    rsn = small("rsn", 2)
    decp = small("decp", NDEC)
    acs_s = small("acs", 8)
    ee_s = small("ee", 8)
    etot_s = small("etot", 8)
    dec_s = small("dec", 8)
    dd_s = small("dd", 8)
    allst = list(STF.values()) + [STDEC]
    SLOC = [withsem(Buf(None, f"sloc{l}")) for l in range(2)]
    SGAT = [withsem(Buf(None, f"sgat{l}")) for l in range(2)]
    HLOC = withsem(Buf(None, "hloc"))
    HGAT = withsem(Buf(None, "hgat"))

    def load_layer_params(l):
        DMA("sync", nwbc.t[:], d_nw_all[l].partition_broadcast(128), nwbc, [], [nwbc])
        DMA("sync", pp.t[:], d_pp_all[l], pp, [], [pp])
        DMA("sync", pr.t[:], d_pr_all[l].partition_broadcast(128), pr, [], [pr])
        DMA("sync", gwf.t[:], d_gw_all[l], gwf, [], [gwf])
        CP(gwb.t[:], gwf.t[:], [gwf], [gwb])
        DMA("gpsimd", wsm.t[:, :, 0:16], win3[:, :, 1024:1040], wsm, [], [wsm])
        DMA("gpsimd", wsm.t[:, :, 16:24], win3[:, :, 3088:3096], wsm, [], [wsm])
        DMA("gpsimd", wsm.t[:, :, 24:32], win3[:, :, 7192:7200], wsm, [], [wsm])
        TS(negb, pp.t[:, PP_GB:PP_GB + 2], -1.0, None, ALU.mult, None, [pp], [smp])
        TS(ib2, pr.t[:, PR_IB:PR_IB + 4], math.log(128 ** -0.5), None, ALU.add, None, [pr], [smp])
        ACT(aneg, pr.t[:, PR_ALOG:PR_ALOG + 8], AF.Exp, [pr], [smp])
        TS(aneg, aneg, -1.0, None, ALU.mult, None, [smp], [smp])
        lg3 = pp.t[:, PP_LB:PP_LB + 8].rearrange("p (b l) -> p b l", l=2)
        if l == 0:
            MSET(lbv, 0.0, [smp], eng="vector")
        else:
            TT(lbv, lg3[:, :, 1], lg3[:, :, 0], ALU.subtract, [pp], [smp])
            ACT(lbv, lbv, AF.Sigmoid, [smp], [smp])
        TS(omlb, lbv, -1.0, 1.0, ALU.mult, ALU.add, [smp], [smp])

    cgroups = []
    for blk in range(2):
        cgroups.append((OFF_A + blk * 128, 128, DEC_A + blk))
    for h in range(4):
        cgroups.append((OFF_B + h * 132, 132, DEC_B + h))
    for h in range(4):
        cgroups.append((OFF_C + h * 128, 128, DEC_C + h))
    for h in range(8):
        cgroups.append((OFF_D + h * 64, 64, DEC_D + h))

    def init_pass():
        MSET(stf_t[:, 0:NSTATE], 0.0, allst, eng="vector")
        if not P2:
            MSET(stf_t[:, NSTATE:NSUM], 1.0, allst, eng="vector")
        else:
            for j in range(3):
                DMA("sync", cmbv[:, 0:NSUM], d_sgat[layer][j * 128:(j + 1) * 128, :], cmb, [SGAT[layer]], [cmb])
                TS(decp, cmbv[:, NSTATE:NSUM], pm.t[:, j:j + 1], pm.t[:, 3 + j:4 + j], ALU.mult, ALU.add,
                   [cmb, pm], [smd])
                TS(cmbv[:, 0:NSTATE], cmbv[:, 0:NSTATE], pm.t[:, j:j + 1], None, ALU.mult, None, [cmb, pm], [cmb])
                for (o, n, g) in cgroups:
                    STT(stf_t[:, o:o + n], stf_t[:, o:o + n], decp[:, g:g + 1], cmbv[:, o:o + n], ALU.mult, ALU.add,
                        allst + [cmb, smd], allst)
        CP(stb_t[:, :], stf_t[:, 0:NSTATE], allst, list(STB.values()))

    def end_p1():
        l = layer
        DMA("sync", d_sloc[l], stf_t[:, :], SLOC[l], allst, [SLOC[l]])
        S.coll("gpsimd", lambda e: e.collective_compute("AllGather", ALU.bypass, replica_groups=GROUPS,
                                                        ins=[d_sloc[l]], outs=[d_sgat[l]]),
               SGAT[l].sem, [SLOC[l]], [SGAT[l]])

    def exchange_halo():
        DMA("sync", d_hloc, hres_t[125:128, NT - 1, :], HLOC, [HT[NT - 1]], [HLOC])
        S.coll("gpsimd", lambda e: e.collective_compute("AllGather", ALU.bypass, replica_groups=GROUPS,
                                                        ins=[d_hloc], outs=[d_hgat]),
               HGAT.sem, [HLOC], [HGAT])
        MSET(halo_h.t[:], 0.0, [halo_h])
        for j in range(3):
            for half in range(2):
                tb = nxt(TTB, "ttb")
                hs = slice(half * 512, (half + 1) * 512)
                DMA("sync", tb.t[0:3, :], d_hgat[j * 3:(j + 1) * 3, hs], tb, [HGAT], [tb])
                STT(halo_h.t[0:3, hs], tb.t[0:3, :], pm.t[0:3, 6 + j:7 + j], halo_h.t[0:3, hs], ALU.mult, ALU.add,
                    [tb, pm, halo_h], [halo_h])

    def tokc(t):
        return slice(3 + t * 128, 3 + (t + 1) * 128)

    def rstd_of(out_ap, ss_ap, n):
        TS(out_ap, ss_ap, 1.0 / n, EPS, ALU.mult, ALU.add, [smn], [smn])
        ACT(out_ap, out_ap, AF.Ln, [smn], [smn])
        ACT(out_ap, out_ap, AF.Exp, [smn], [smn], scale=-0.5)

    def make_uT_tile(hbuf, h_ap, dst_cols, ncol):
        ACT(junk.t[:], h_ap, AF.Square, [hbuf], [smn], accum_out=ss4[:, 0:1])
        rstd_of(rs4[:, 0:1], ss4[:, 0:1], DM)
        STT(ubf.t[:], h_ap, rs4[:, 0:1], nwbc.t[:], ALU.mult, ALU.mult, [hbuf, smn, nwbc], [ubf])
        for k in range(KC):
            TR(ptu_t[:, k * 128:(k + 1) * 128], ubf.t[:, k * 128:(k + 1) * 128], identb.t[:], [ubf, identb], [PTU])
        src = ptu_t[:, :].rearrange("p (k c) -> p k c", k=KC)[:, :, 0:ncol]
        CP(uT.t[:, :, dst_cols], src, [PTU], [uT])

    def proj_fm(slot, off, M, cols=slice(3, 515), n=512):
        pj = pjn()
        v = slot3(slot)
        for k in range(KC):
            MM(pj.t[0:M, 0:n], v[:, k, off:off + M], uT.t[:, k, cols], k == 0, k == KC - 1, [slot, uT], [pj])
        return pj

    def proj_tm(slot, ncols, t):
        pj = pjn()
        v = slot3(slot)
        for k in range(KC):
            MM(pj.t[:, 0:ncols], uT.t[:, k, tokc(t)], v[:, k, 0:ncols], k == 0, k == KC - 1, [slot, uT], [pj])
        return pj

    def proj_small(c0, c1, t):
        b, ap = pfn()
        n = c1 - c0
        for k in range(KC):
            MM(ap[:, 0:n], uT.t[:, k, tokc(t)], wsm.t[:, k, c0:c1], k == 0, k == KC - 1, [wsm, uT], [b])
        return b, ap

    def conv_block(s_idx, slot, off, hidx, wcol, bcol, out_ap, out_buf):
        pj = proj_fm(slot, off, 128)
        raw = fbn()
        if s_idx == 0:
            b, ap = pfn()
            v = slot3(slot)
            for k in range(KC):
                MM(ap[:, 0:3], v[:, k, off:off + 128], uT.t[:, k, 0:3], k == 0, k == KC - 1, [slot, uT], [b])
            CP(raw.t[:, 0:3], ap[:, 0:3], [b], [raw])
        else:
            CP(raw.t[:, 0:3], halo_raw.t[:, hidx, :], [halo_raw], [raw])
        ACT(raw.t[:, 3:515], pj.t[:, :], AF.Copy, [pj], [raw])
        CP(halo_raw.t[:, hidx, :], raw.t[:, 512:515], [raw], [halo_raw])
        acc = fbn()
        w = lambda k: pp.t[:, wcol + k:wcol + k + 1]
        ACT(acc.t[:, 0:512], raw.t[:, 0:512], AF.Identity, [raw, pp], [acc], scale=w(0), bias=pp.t[:, bcol:bcol + 1])
        for k in (1, 2, 3):
            STT(acc.t[:, 0:512], raw.t[:, k:k + 512], w(k), acc.t[:, 0:512], ALU.mult, ALU.add, [raw, pp, acc], [acc])
        ACT(out_ap, acc.t[:, 0:512], AF.Silu, [acc], [out_buf])

    def finish_mixer(s_idx, m, t, mix):
        if debug:
            tg = s_idx * TPS + t
            dbg_ids.append(S.dma("sync", lambda e: e.dma_start(out=d_dbg[tg * 128:(tg + 1) * 128, m * 512:(m + 1) * 512], in_=mix.t[:]),
                                 mix.sem, [mix], []))
        for b4 in range(4):
            pb, pap = ptsn()
            TR(pap, mix.t[:, b4 * 128:(b4 + 1) * 128], identb.t[:], [mix, identb], [pb])
            ACT(mixT.t[:, b4, t * 128:(t + 1) * 128], pap, AF.Copy, [pb], [mixT])

    def out_proj(s_idx, m):
        s = load_wo(m)
        v = slot_wo(s)
        for t in range(TPS):
            tg = s_idx * TPS + t
            for half in range(2):
                pj = pjn()
                for kc in range(4):
                    MM(pj.t[:, :], mixT.t[:, kc, t * 128:(t + 1) * 128], v[:, kc, half * 512:(half + 1) * 512],
                       kc == 0, kc == 3, [s, mixT], [pj])
                hap = hres_t[:, tg, half * 512:(half + 1) * 512]
                TT(hap, hap, pj.t[:, :], ALU.add, [HT[tg], pj], [HT[tg]])

    def scalar_decay(t, nh, a_ap, a_buf, groups, ybuf, yap, dvh, pad):
        bpa, pa = pfn()
        MM(pa[:, 0:nh], trif.t[:], a_ap, True, True, [trif, a_buf], [bpa])
        MM(pa[:, 64:64 + nh], onesf.t[:], a_ap, True, True, [onesf, a_buf], [bpa])
        CP(acs_s[:, 0:nh], pa[:, 0:nh], [bpa], [smd])
        if P2:
            ACT(ee_s[:, 0:nh], pa[:, 0:nh], AF.Exp, [bpa], [smd])
        ACT(etot_s[:, 0:nh], pa[:, 64:64 + nh], AF.Exp, [bpa], [smd])
        TT(dd_s[:, 0:nh], pa[:, 64:64 + nh], acs_s[:, 0:nh], ALU.subtract, [bpa, smd], [smd])
        ACT(dec_s[:, 0:nh], dd_s[:, 0:nh], AF.Exp, [smd], [smd])
        for g in groups:
            ncg = g["ncg"]
            heads = g["heads"]
            if P2:
                bqk, pqk = pfn()
                MM(pqk, g["kT"], g["qT"], True, True, g["qkbufs"], [bqk])
                byo, pyo = pwn()
                MM(pyo[:, 0:ncg], g["qT"], g["sb"][:, 0:ncg], True, True, g["qkbufs"] + [g["stb"]], [byo])
                byd, pyd = pwn()
            vd = nxt(VDB, "vdb")
            for j, h in enumerate(heads):
                cs = slice(j * pad, j * pad + dvh)
                if P2:
                    lh = nxt(SMF, "smf")
                    ACT(lh.t[:], suf.t[:], AF.Identity, [suf, a_buf], [lh], scale=a_ap[:, h:h + 1])
                    bsg, psg = pfn()
                    MM(psg, lh.t[:], trif.t[:], True, False, [lh, trif], [bsg])
                    MM(psg, identb.t[:], negb_.t[:], False, True, [identb, negb_], [bsg])
                    esg = nxt(SMF, "smf")
                    ACT(esg.t[:], psg, AF.Exp, [bsg], [esg])
                    wb = nxt(SMB, "smb")
                    TT(wb.t[:], pqk, esg.t[:], ALU.mult, [bqk, esg], [wb])
                    MM(pyd[:, cs], wb.t[:], g["v"][:, cs], True, True, [wb, g["vbuf"]], [byd])
                if len(heads) == 1:
                    TS(vd.t[:, cs], g["v"][:, cs], dec_s[:, h:h + 1], None, ALU.mult, None, [g["vbuf"], smd], [vd])
            if len(heads) > 1:
                nhh = len(heads)
                so = SML["dec"][0] + heads[0]
                decb = bass.AP(sml.t, so, [[512, 128], [1, nhh], [0, dvh]])
                TT(vd.t[:, 0:ncg].rearrange("p (h c) -> p h c", c=dvh),
                   g["v"][:, 0:ncg].rearrange("p (h c) -> p h c", c=dvh), decb, ALU.mult, [g["vbuf"], smd], [vd])
            bu, pu = pwn()
            MM(pu[:, 0:ncg], g["ktok"], vd.t[:, 0:ncg], True, True, g["ktokbufs"] + [vd], [bu])
            if P2:
                for j, h in enumerate(heads):
                    cs = slice(j * pad, j * pad + dvh)
                    ACT(yap(h), pyd[:, cs], AF.Copy, [byd], [ybuf])
                    STT(yap(h), pyo[:, cs], ee_s[:, h:h + 1], yap(h), ALU.mult, ALU.add, [byo, smd, ybuf], [ybuf])
            for j, h in enumerate(heads):
                cs = slice(j * pad, j * pad + dvh)
                STT(g["sf"][:, cs], g["sf"][:, cs], etot_s[:, h:h + 1], pu[:, cs], ALU.mult, ALU.add,
                    [g["stf"], smd, bu], [g["stf"]])
                if not P2:
                    dc = stf_t[:, NSTATE + g["decoff"] + j:NSTATE + g["decoff"] + j + 1]
                    TT(dc, dc, etot_s[:, h:h + 1], ALU.mult, [STDEC, smd], [STDEC])
            ACT(g["sb"], g["sf"], AF.Copy, [g["stf"]], [g["stb"]])


    xst = withsem(sb([128, 4, 512], F32, "xst"))
    cmb = xst
    cmbv = xst.t[:, :, :].rearrange("p a b -> p (a b)")
    yb = sb([128, 4, 132], F32, "yb")
    VTD = [sb([128, 512], BF16, f"vtd{i}") for i in range(2)]
    DBGS = [S.dma_sem(f"dbg{i}") for i in range(2)] if debug else None
    for i, mb in enumerate(MIXB):
        mb.sem = DBGS[i] if debug else None

    def recur_vd(s_idx, m, hpb, dk, qTb, kTb, kdtokb, vTb, zGb, soff, doff, nwoff, midx):
        for t in range(TPS):
            pouts = []
            for h in range(4):
                blk = h // hpb
                r0 = (h % hpb) * dk
                rows = slice(r0, r0 + dk)
                hc = slice(h * 128, (h + 1) * 128)
                scol = soff + blk * 128
                sf = stf_t[rows, scol:scol + 128]
                sbf = stb_t[rows, scol:scol + 128]
                tcs = slice(t * 128, (t + 1) * 128)
                if P2:
                    bsc, psc = pfn()
                    MM(psc, kTb.t[rows, blk, tcs], qTb.t[rows, blk, tcs], True, True, [kTb, qTb], [bsc])
                    sm = nxt(SMB, "smb")
                    TT(sm.t[:], psc, trif.t[:], ALU.mult, [bsc, trif], [sm])
                    bo, po = pwn()
                for c in range(2):
                    tr = slice(c * 64, (c + 1) * 64)
                    if P2:
                        MM(po[tr, 0:128], sm.t[tr, tr], vTb.t[tr, t, hc], True, False, [sm, vTb], [bo])
                        MM(po[tr, 0:128], qTb.t[rows, blk, t * 128 + c * 64:t * 128 + (c + 1) * 64], sbf,
                           False, True, [qTb, STB[(m, h)]], [bo])
                    bu, pu = pfn()
                    MM(pu[rows, 0:128], kdtokb.t[tr, t, blk * 128 + r0:blk * 128 + r0 + dk], vTb.t[tr, t, hc],
                       True, True, [kdtokb, vTb], [bu])
                    eg = egl_t[rows, blk, t * 2 + c:t * 2 + c + 1]
                    STT(sf, sf, eg, pu[rows, 0:128], ALU.mult, ALU.add, [STF[(m, h)], EGLB[blk], bu], [STF[(m, h)]])
                    ACT(sbf, sf, AF.Copy, [STF[(m, h)]], [STB[(m, h)]])
                    if not P2:
                        dc = stf_t[rows, NSTATE + doff + blk:NSTATE + doff + blk + 1]
                        TT(dc, dc, eg, ALU.mult, [STDEC, EGLB[blk]], [STDEC])
                if P2:
                    ACT(junk.t[:, 0:128], po[:, 0:128], AF.Square, [bo], [smn], accum_out=ss4[:, h:h + 1])
                    pouts.append((bo, po))
            if P2:
                rstd_of(rs4[:, 0:4], ss4[:, 0:4], 128)
                nwz = nxt(TTB, "ttb")
                TT(nwz.t[:], pr.t[:, nwoff:nwoff + 512], zGb.t[:, t, 0:512], ALU.mult, [pr, zGb], [nwz])
                mix = nxt(MIXB, "mixb")
                for h in range(4):
                    hc = slice(h * 128, (h + 1) * 128)
                    STT(mix.t[:, hc], pouts[h][1][:, 0:128], rs4[:, h:h + 1], nwz.t[:, hc], ALU.mult, ALU.mult,
                        [pouts[h][0], smn, nwz], [mix])
                finish_mixer(s_idx, midx, t, mix)
        if P2:
            out_proj(s_idx, midx)

    def vd_front(blk, kf, csb, sG, qslot, qoff, qscale, qTb, kTb, kdTb):
        ACT(egl_t[:, blk, :], csb.t[:, 63:512:64], AF.Exp, [csb], [EGLB[blk]], scale=sG)
        tmp = fbn()
        if STOP == 31:
            return
        if P2:
            qf = fbn()
            pq = proj_fm(qslot, qoff, 128)
            ACT(qf.t[:, 0:512], pq.t[:, :], AF.Identity, [pq], [qf], scale=qscale)
            if STOP == 32:
                return
            ACT(tmp.t[:, 0:512], csb.t[:, 0:512], AF.Exp, [csb], [tmp], scale=sG)
            TT(qTb.t[:, blk, 0:512], qf.t[:, 0:512], tmp.t[:, 0:512], ALU.mult, [qf, tmp], [qTb])
            if STOP == 33:
                return
            ACT(tmp.t[:, 0:512], csb.t[:, 0:512], AF.Exp, [csb], [tmp], scale=-sG)
            TT(kTb.t[:, blk, 0:512], kf.t[:, 0:512], tmp.t[:, 0:512], ALU.mult, [kf, tmp], [kTb])
        if STOP == 34:
            return
        glb = bass.AP(csb.t, 63, [[515, 128], [64, 8], [0, 64]])
        cs3 = csb.t[:, 0:512].rearrange("p (c k) -> p c k", k=64)
        TT(tmp.t[:, 0:512].rearrange("p (c k) -> p c k", k=64), glb, cs3, ALU.subtract, [csb], [tmp])
        if STOP == 35:
            return
        ACT(tmp.t[:, 0:512], tmp.t[:, 0:512], AF.Exp, [tmp], [tmp], scale=sG)
        if STOP == 36:
            return
        TT(kdTb.t[:, blk, 0:512], kf.t[:, 0:512], tmp.t[:, 0:512], ALU.mult, [kf, tmp], [kdTb])
        if STOP == 37:
            return

    def scan(csb, src):
        S.op("vector", lambda e: e.tensor_tensor_scan(out=csb.t[:, 0:512], data0=rst.t[:, :], data1=src.t[:, 0:512],
                                                      initial=0.0, op0=ALU.mult, op1=ALU.add), [rst, src], [csb],
             cost=1250)

    def vz_and_kdtok(nblk, vslot_cols, zslot_cols, vTb, zGb, kdTb, kdtokb):
        s_v = load_w(*vslot_cols)
        for t in range(TPS):
            pv = proj_tm(s_v, 512, t)
            ACT(vTb.t[:, t, 0:512], pv.t[:, :], AF.Copy, [pv], [vTb])
        if STOP == 41:
            return
        if P2:
            s_z = load_w(*zslot_cols)
            for t in range(TPS):
                pz = proj_tm(s_z, 512, t)
                ACT(zGb.t[:, t, 0:512], pz.t[:, :], AF.Silu, [pz], [zGb])
        if STOP == 42:
            return
        for t in range(TPS):
            for blk in range(nblk):
                pb, pap = ptsn()
                TR(pap, kdTb.t[:, blk, t * 128:(t + 1) * 128], identb.t[:], [kdTb, identb], [pb])
                ACT(kdtokb.t[:, t, blk * 128:(blk + 1) * 128], pap, AF.Copy, [pb], [kdtokb])

    def mixer_A(s_idx):
        qTb, kTb, kdTb, vTb, zGb, kdtokb = BFB
        s_qk = load_w(0, 512)
        pg = pjn()
        for k in range(KC):
            MM(pg.t[0:16, :], wsm.t[:, k, 0:16], uT.t[:, k, 3:515], k == 0, k == KC - 1, [wsm, uT], [pg])
        ACT(grt.t[:], pg.t[0:16, :], AF.Copy, [pg], [grt])
        if STOP < 1:
            return
        for blk in range(2):
            kf = fbn()
            pk = proj_fm(s_qk, 256 + blk * 128, 128)
            ACT(kf.t[:, 0:512], pk.t[:, :], AF.Copy, [pk], [kf])
            px = pjn()
            MM(px.t[:, :], gwb.t[:, blk * 128:(blk + 1) * 128], grt.t[:], True, True, [gwb, grt], [px])
            sp = fbn()
            ACT(sp.t[:, 0:512], px.t[:, :], AF.Exp, [px, smp], [sp], scale=-1.0, bias=negb[:, blk:blk + 1])
            ACT(sp.t[:, 0:512], sp.t[:, 0:512], AF.Ln, [sp], [sp], bias=1.0)
            csb = fbn()
            if STOP < 2:
                continue
            scan(csb, sp)
            if STOP < 3:
                continue
            vd_front(blk, kf, csb, -1.0 / 16, s_qk, blk * 128, 0.125, qTb, kTb, kdTb)
        if STOP < 4 or 30 < STOP < 40:
            return
        vz_and_kdtok(2, (512, 1024), (1040, 1552), vTb, zGb, kdTb, kdtokb)
        if STOP < 5 or 40 < STOP < 50:
            return
        recur_vd(s_idx, "A", 2, 64, qTb, kTb, kdtokb, vTb, zGb, OFF_A, DEC_A, PR_NWA, 0)

    def mixer_C(s_idx):
        qTb, kTb, kdTb, vTb, zGb, kdtokb = BFB
        s_f = load_w(4632, 5144)
        s_q = load_w(4120, 4632) if P2 else None
        for blk in range(4):
            pf_ = proj_fm(s_f, blk * 128, 128)
            ff = fbn()
            ACT(ff.t[:, 0:512], pf_.t[:, :], AF.Sigmoid, [pf_], [ff])
            TS(ff.t[:, 0:512], ff.t[:, 0:512], omlb[:, blk:blk + 1], lbv[:, blk:blk + 1], ALU.mult, ALU.add,
               [ff, smp], [ff])
            kf = fbn()
            TS(kf.t[:, 0:512], ff.t[:, 0:512], -1.0, 1.0, ALU.mult, ALU.add, [ff], [kf])
            lf = fbn()
            ACT(lf.t[:, 0:512], ff.t[:, 0:512], AF.Ln, [ff], [lf])
            csb = fbn()
            scan(csb, lf)
            vd_front(blk, kf, csb, 1.0, s_q, blk * 128, 128 ** -0.5, qTb, kTb, kdTb)
        vz_and_kdtok(4, (5144, 5656), (5656, 6168), vTb, zGb, kdTb, kdtokb)
        recur_vd(s_idx, "C", 1, 128, qTb, kTb, kdtokb, vTb, zGb, OFF_C, DEC_C, PR_NWC, 2)

    def mixer_B(s_idx):
        qTb, kTb, ktokb, vTb, oGb, zGb = BFB
        if P2:
            s_q = load_w(1552, 2064)
            for blk in range(4):
                conv_block(s_idx, s_q, blk * 128, blk, PP_MCW + blk * 4, PP_MCB + blk, qTb.t[:, blk, 0:512], qTb)
        s_k = load_w(2064, 2576)
        for blk in range(4):
            conv_block(s_idx, s_k, blk * 128, 4 + blk, PP_MCW + (4 + blk) * 4, PP_MCB + 4 + blk,
                       kTb.t[:, blk, 0:512], kTb)
        s_v = load_w(2576, 3088)
        for t in range(TPS):
            bg, pgi = proj_small(16, 24, t)
            gi = gates_t[:, t * 16:t * 16 + 4]
            TT(gi, pgi[:, 0:4], ib2, ALU.add, [bg, smp], [GATES])
            ACT(gi, gi, AF.Exp, [GATES], [GATES])
            gf = gates_t[:, t * 16 + 4:t * 16 + 8]
            TT(gf, pgi[:, 4:8], pr.t[:, PR_FB:PR_FB + 4], ALU.add, [bg, pr], [GATES])
            ACT(gf, gf, AF.Exp, [GATES], [GATES], scale=-1.0)
            ACT(gf, gf, AF.Ln, [GATES], [GATES], bias=1.0)
            TS(gf, gf, -1.0, None, ALU.mult, None, [GATES], [GATES])
            pv = proj_tm(s_v, 512, t)
            vt4 = vTb.t[:, t, :].rearrange("p (h c) -> p h c", c=132)
            for h in range(4):
                TS(vt4[:, h, 0:128], pv.t[:, h * 128:(h + 1) * 128], gi[:, h:h + 1], None, ALU.mult, None,
                   [pv, GATES], [vTb])
            CP(vt4[:, :, 128], gi, [GATES], [vTb])
        if P2:
            s_o = load_w(3096, 3608)
            for t in range(TPS):
                po_ = proj_tm(s_o, 512, t)
                ACT(oGb.t[:, t, 0:512], po_.t[:, :], AF.Sigmoid, [po_], [oGb])
            s_z = load_w(3608, 4120)
            for t in range(TPS):
                pz = proj_tm(s_z, 512, t)
                ACT(zGb.t[:, t, 0:512], pz.t[:, :], AF.Silu, [pz], [zGb])
        for t in range(TPS):
            for h in range(4):
                pb, pap = ptsn()
                TR(pap, kTb.t[:, h, t * 128:(t + 1) * 128], identb.t[:], [kTb, identb], [pb])
                ACT(ktokb.t[:, t, h * 128:(h + 1) * 128], pap, AF.Copy, [pb], [ktokb])
        for t in range(TPS):
            tcs = slice(t * 128, (t + 1) * 128)
            vt4 = vTb.t[:, t, :].rearrange("p (h c) -> p h c", c=132)
            groups = []
            for h in range(4):
                groups.append(dict(qT=qTb.t[:, h, tcs], kT=kTb.t[:, h, tcs], qkbufs=[qTb, kTb],
                                   ktok=ktokb.t[:, t, h * 128:(h + 1) * 128], ktokbufs=[ktokb],
                                   v=vt4[:, h, :], vbuf=vTb, heads=[h], ncg=129,
                                   sf=stf_t[:, OFF_B + h * 132:OFF_B + (h + 1) * 132],
                                   sb=stb_t[:, OFF_B + h * 132:OFF_B + (h + 1) * 132],
                                   stf=STF[("B", h)], stb=STB[("B", h)], decoff=DEC_B + h))
            a_ap = gates_t[:, t * 16 + 4:t * 16 + 8]
            scalar_decay(t, 4, a_ap, GATES, groups, yb, lambda h: yb.t[:, h, 0:129], 129, 132)
            if P2:
                TS(rr4, yb.t[:, :, 128], -1.0, None, ALU.mult, None, [yb], [smn])
                TT(rr4, rr4, yb.t[:, :, 128], ALU.max, [yb, smn], [smn])
                TS(rr4, rr4, 1.0, None, ALU.max, None, [smn], [smn])
                S.op("vector", lambda e: e.reciprocal(out=rr4, in_=rr4), [smn], [smn])
                hb = nxt(TTB, "ttb")
                for h in range(4):
                    hc = slice(h * 128, (h + 1) * 128)
                    STT(hb.t[:, hc], yb.t[:, h, 0:128], rr4[:, h:h + 1], oGb.t[:, t, hc], ALU.mult, ALU.mult,
                        [yb, smn, oGb], [hb])
                for h in range(4):
                    hc = slice(h * 128, (h + 1) * 128)
                    ACT(junk.t[:, 0:128], hb.t[:, hc], AF.Square, [hb], [smn], accum_out=ss4[:, h:h + 1])
                rstd_of(rs4[:, 0:4], ss4[:, 0:4], 128)
                nwz = nxt(TTB, "ttb")
                TT(nwz.t[:], pr.t[:, PR_NWB:PR_NWB + 512], zGb.t[:, t, 0:512], ALU.mult, [pr, zGb], [nwz])
                mix = nxt(MIXB, "mixb")
                for h in range(4):
                    hc = slice(h * 128, (h + 1) * 128)
                    STT(mix.t[:, hc], hb.t[:, hc], rs4[:, h:h + 1], nwz.t[:, hc], ALU.mult, ALU.mult,
                        [hb, smn, nwz], [mix])
                finish_mixer(s_idx, 1, t, mix)
        if P2:
            out_proj(s_idx, 1)

    def mixer_D(s_idx):
        bcTb, btokb, _, _, _, zGb = BFB
        s_x = load_w(6168, 6680)
        for blk in range(4):
            conv_block(s_idx, s_x, blk * 128, 8 + blk, PP_SCW + blk * 4, PP_SCB + blk, xst.t[:, blk, :], xst)
        s_bc = load_w(6680, 7192)
        for blk in (range(4) if P2 else range(2)):
            conv_block(s_idx, s_bc, blk * 128, 12 + blk, PP_SCW + (4 + blk) * 4, PP_SCB + 4 + blk,
                       bcTb.t[:, blk, 0:512], bcTb)
        if P2:
            s_z = load_w(7200, 7712)
            for t in range(TPS):
                pz = proj_tm(s_z, 512, t)
                ACT(zGb.t[:, t, 0:512], pz.t[:, :], AF.Silu, [pz], [zGb])
        for t in range(TPS):
            for g in range(2):
                pb, pap = ptsn()
                TR(pap, bcTb.t[:, g, t * 128:(t + 1) * 128], identb.t[:], [bcTb, identb], [pb])
                ACT(btokb.t[:, t, g * 128:(g + 1) * 128], pap, AF.Copy, [pb], [btokb])
        for t in range(TPS):
            tcs = slice(t * 128, (t + 1) * 128)
            bd, pdt = proj_small(24, 32, t)
            dtv = gates_t[:, 256 + t * 32:256 + t * 32 + 8]
            TT(dtv, pdt[:, 0:8], pr.t[:, PR_DTB:PR_DTB + 8], ALU.add, [bd, pr], [GATESD])
            ACT(dtv, dtv, AF.Exp, [GATESD], [GATESD])
            ACT(dtv, dtv, AF.Ln, [GATESD], [GATESD], bias=1.0)
            av = gates_t[:, 256 + t * 32 + 8:256 + t * 32 + 16]
            TT(av, dtv, aneg, ALU.mult, [GATESD, smp], [GATESD])
            xs = nxt(TTB, "ttb")
            for blk in range(4):
                bx, pxr = pwn()
                S.op("tensor", lambda e, pxr=pxr, blk=blk, tcs=tcs: e.transpose(pxr[:, 0:128], xst.t[:, blk, tcs], identf.t[:]),
                     [xst, identf], [bx], cost=400)
                CP(xs.t[:, blk * 128:(blk + 1) * 128], pxr[:, 0:128], [bx], [xs])
            vt = nxt(VTD, "vtd")
            dtb = bass.AP(gates_t, 256 + t * 32, [[512, 128], [1, 8], [0, 64]])
            TT(vt.t[:, :].rearrange("p (h c) -> p h c", c=64), xs.t[:, :].rearrange("p (h c) -> p h c", c=64),
               dtb, ALU.mult, [xs, GATESD], [vt])
            groups = []
            for g in range(2):
                groups.append(dict(qT=bcTb.t[:, 2 + g, tcs], kT=bcTb.t[:, g, tcs], qkbufs=[bcTb],
                                   ktok=btokb.t[:, t, g * 128:(g + 1) * 128], ktokbufs=[btokb],
                                   v=vt.t[:, g * 256:(g + 1) * 256], vbuf=vt, heads=[4 * g + j for j in range(4)],
                                   ncg=256, sf=stf_t[:, OFF_D + g * 256:OFF_D + (g + 1) * 256],
                                   sb=stb_t[:, OFF_D + g * 256:OFF_D + (g + 1) * 256],
                                   stf=STF[("D", g)], stb=STB[("D", g)], decoff=DEC_D + 4 * g))
            Y = nxt(TTB, "ttb")
            scalar_decay(t, 8, av, GATESD, groups, Y, lambda h: Y.t[:, h * 64:(h + 1) * 64], 64, 64)
            if P2:
                for h in range(8):
                    hs = slice(h * 64, (h + 1) * 64)
                    STT(Y.t[:, hs], xs.t[:, hs], pr.t[:, PR_D + h:PR_D + h + 1], Y.t[:, hs], ALU.mult, ALU.add,
                        [xs, pr, Y], [Y])
                TT(Y.t[:], Y.t[:], zGb.t[:, t, 0:512], ALU.mult, [Y, zGb], [Y])
                for g in range(2):
                    ACT(junk.t[:, 0:256], Y.t[:, g * 256:(g + 1) * 256], AF.Square, [Y], [smn],
                        accum_out=ssn[:, g:g + 1])
                rstd_of(rsn[:, 0:2], ssn[:, 0:2], 256)
                mix = nxt(MIXB, "mixb")
                for g in range(2):
                    gs = slice(g * 256, (g + 1) * 256)
                    STT(mix.t[:, gs], Y.t[:, gs], rsn[:, g:g + 1], pr.t[:, PR_NWD + g * 256:PR_NWD + (g + 1) * 256],
                        ALU.mult, ALU.mult, [Y, smn, pr], [mix])
                finish_mixer(s_idx, 3, t, mix)
        if P2:
            out_proj(s_idx, 3)

    finals = []
    for layer in range(NLAYERS):
        win3 = d_win_all[layer].rearrange("(k p) c -> p k c", p=128)
        wout3 = d_wout_all[layer].rearrange("(k p) c -> p k c", p=128)
        load_layer_params(layer)
        for P2 in (False, True):
            last = P2 and layer == NLAYERS - 1
            init_pass()
            for s_idx in range(NS):
                if s_idx == 0:
                    make_uT_tile(halo_h, halo_h.t[:], slice(0, 3), 3)
                else:
                    CP(uT.t[:, :, 0:3], uT.t[:, :, 512:515], [uT], [uT])
                for t in range(TPS):
                    tg = s_idx * TPS + t
                    make_uT_tile(HT[tg], hres_t[:, tg, :], tokc(t), 128)
                if "A" in MIXERS:
                    mixer_A(s_idx)
                if "B" in MIXERS:
                    mixer_B(s_idx)
                if "C" in MIXERS:
                    mixer_C(s_idx)
                if "D" in MIXERS:
                    mixer_D(s_idx)
            if not P2:
                end_p1()
            elif not last:
                exchange_halo()

    DMA("sync", nwbc.t[:], d_fnw.partition_broadcast(128), nwbc, [], [nwbc])
    for tg in range(NT):
        ACT(junk.t[:], hres_t[:, tg, :], AF.Square, [HT[tg]], [smn], accum_out=ss4[:, 0:1])
        rstd_of(rs4[:, 0:1], ss4[:, 0:1], DM)
        for half in range(2):
            ob = nxt(TTB, "ttb")
            hs = slice(half * 512, (half + 1) * 512)
            STT(ob.t[:], hres_t[:, tg, hs], rs4[:, 0:1], nwbc.t[:, hs], ALU.mult, ALU.mult,
                [HT[tg], smn, nwbc], [ob])
            finals.append(DMA("sync", d_out[tg * 128:(tg + 1) * 128, hs], ob.t[:], ob, [ob], []))
    finals.extend(dbg_ids)
    S.wait_all("sync", finals)
    print("n semaphores", len(S.sems), flush=True)
    S.schedule(reorder=REORDER)
    print("ops:", len(S.ops), "model makespan us:", S.makespan / 1e3, flush=True)
    S.emit()
    S.close()
    es.close()
    return nc


def pack_layer(inp, l):
    f = lambda a: np.ascontiguousarray(np.asarray(a, dtype=np.float32))
    pp = np.zeros((128, NPP), np.float32)
    pp[:, PP_GB:PP_GB + 2] = f(inp["gla_gate_b"][l]).reshape(2, 128).T
    pp[:, PP_MCW:PP_MCW + 32] = f(inp["ml_conv_w"][l]).reshape(4, 8, 128).transpose(2, 1, 0).reshape(128, 32)
    pp[:, PP_MCB:PP_MCB + 8] = f(inp["ml_conv_b"][l]).reshape(8, 128).T
    pp[:, PP_SCW:PP_SCW + 32] = f(inp["ssd_conv_w"][l]).reshape(4, 8, 128).transpose(2, 1, 0).reshape(128, 32)
    pp[:, PP_SCB:PP_SCB + 8] = f(inp["ssd_conv_b"][l]).reshape(8, 128).T
    pp[:, PP_LB:PP_LB + 8] = f(inp["hg_lb_logits"]).reshape(2, 4, 128).transpose(2, 1, 0).reshape(128, 8)
    pr = np.concatenate([f(inp["ml_i_b"][l]), f(inp["ml_f_b"][l]), f(inp["ssd_dt_bias"][l]), f(inp["ssd_A_log"][l]),
                         f(inp["ssd_D"][l]), f(inp["gla_norm_w"][l]), f(inp["ml_norm_w"][l]), f(inp["hg_norm_w"][l]),
                         f(inp["ssd_norm_w"][l])]).astype(np.float32)
    assert pr.shape[0] == NPR
    return pp, pr


def core_pm(q):
    pm = np.zeros((128, 9), np.float32)
    for j in range(3):
        pm[:, j] = 1.0 if j < q else 0.0
        pm[:, 3 + j] = 1.0 - pm[:, j]
        pm[:, 6 + j] = 1.0 if j == q - 1 else 0.0
    return pm


_NC_CACHE = {}


def pack_inputs(inputs):
    f = lambda a: np.ascontiguousarray(np.asarray(a, dtype=np.float32))
    x = f(inputs["x"])
    pps, prs = zip(*[pack_layer(inputs, l) for l in range(2)])
    shared = dict(w_in=f(inputs["w_in"]), w_out=f(inputs["w_out"]), nw=f(inputs["norm_w"]),
                  fnw=f(inputs["final_norm_w"]), pp=np.ascontiguousarray(np.stack(pps)),
                  pr=np.ascontiguousarray(np.stack(prs)), gw=f(inputs["gla_gate_w"]))
    in_maps = []
    for c in range(NCORES):
        b, q = c // 4, c % 4
        d = dict(shared)
        d["hin"] = np.ascontiguousarray(x[b, q * T:(q + 1) * T, :])
        d["halo"] = (np.zeros((3, DM), np.float32) if q == 0
                     else np.ascontiguousarray(x[b, q * T - 3:q * T, :]))
        d["pm"] = core_pm(q)
        in_maps.append(d)
    return in_maps


def kernel(**inputs):
    if "nc" not in _NC_CACHE:
        _NC_CACHE["nc"] = build()
    in_maps = pack_inputs(inputs)
    res = run_bass_kernel_spmd(_NC_CACHE["nc"], in_maps, core_ids=list(range(NCORES)))
    B = np.asarray(inputs["x"]).shape[0]
    out = np.zeros((B, 4 * T, DM), np.float32)
    for c in range(NCORES):
        out[c // 4, (c % 4) * T:(c % 4 + 1) * T, :] = np.asarray(res.results[c]["hout"], dtype=np.float32)
    return out
```

```python
import math
import numpy as np
from contextlib import ExitStack
import concourse.bass as bass
import concourse.mybir as mybir
from concourse.bass_utils import run_bass_kernel_spmd

F32 = mybir.dt.float32
BF16 = mybir.dt.bfloat16
AF = mybir.ActivationFunctionType
ALU = mybir.AluOpType

NCORES = 8
SYNC_LAT = 150.0
SELF_DIST = 3
T = 2048
NT = 16
ST = 512
TPS = 4
NS = 4
DM = 1024
KC = 8
EPS = 1e-6
DPROJ = 7712
NSTATE = 1808
NDEC = 18
NSUM = NSTATE + NDEC
OFF_A, OFF_B, OFF_C, OFF_D = 0, 256, 784, 1296
DEC_A, DEC_B, DEC_C, DEC_D = 0, 2, 6, 10
NPP = 90
NPR = 2080
PP_GB, PP_MCW, PP_MCB, PP_SCW, PP_SCB, PP_LB = 0, 2, 34, 42, 74, 82
PR_IB, PR_FB, PR_DTB, PR_ALOG, PR_D, PR_NWA, PR_NWB, PR_NWC, PR_NWD = 0, 4, 8, 16, 24, 32, 544, 1056, 1568


class Buf:
    __slots__ = ("t", "name", "w", "r", "sem", "excl")

    def __init__(self, t, name, excl=False):
        self.t = t
        self.name = name
        self.w = None
        self.r = []
        self.sem = None
        self.excl = excl


class Sched:
    def __init__(self, nc):
        self.nc = nc
        self.engs = []
        self.sems = {}
        self._ctx = []
        self.nself = {"tensor"}
        self.ops = []

    def add_engine(self, name):
        key = "e_" + name
        self.sems[key] = self._alloc(key)
        self.engs.append(name)

    def _alloc(self, key):
        cm = self.nc.semaphore(key)
        h = cm.__enter__()
        self._ctx.append(cm)
        return h

    def dma_sem(self, name):
        key = "d_" + name
        self.sems[key] = self._alloc(key)
        return key

    def _record(self, eng, fn, kind, sem, reads, writes, cost, lat):
        ex = [r for r in reads if r.excl]
        if ex:
            writes = list(writes) + [r for r in ex if r not in writes]
            reads = [r for r in reads if not r.excl]
        deps = set()
        for r in reads:
            if r.w is not None:
                deps.add(r.w)
        for w in writes:
            if w.w is not None:
                deps.add(w.w)
            deps.update(w.r)
        oid = len(self.ops)
        self.ops.append(dict(id=oid, eng=eng, fn=fn, kind=kind, sem=sem, deps=deps, cost=cost, lat=lat))
        for r in reads:
            r.r.append(oid)
        for w in writes:
            w.w = oid
            w.r = []
        return oid

    def op(self, eng, fn, reads=(), writes=(), cost=300):
        return self._record(eng, fn, "op", "e_" + eng, reads, writes, cost, cost)

    def dma(self, eng, fn, semkey, reads=(), writes=(), cost=100, lat=4000):
        return self._record(eng, fn, "dma", semkey, reads, writes, cost, lat)

    def coll(self, eng, fn, semkey, reads=(), writes=()):
        return self._record(eng, fn, "coll", semkey, reads, writes, 2000, 40000)

    def wait_all(self, eng, ids):
        oid = len(self.ops)
        self.ops.append(dict(id=oid, eng=eng, fn=None, kind="wait", sem=None, deps=set(ids), cost=10, lat=10))
        return oid

    def schedule(self, reorder=True):
        import heapq
        ops = self.ops
        n = len(ops)
        ndeps = [len(o["deps"]) for o in ops]
        users = [[] for _ in range(n)]
        for o in ops:
            for d in o["deps"]:
                users[d].append(o["id"])
        prio = [0.0] * n
        for o in reversed(ops):
            i = o["id"]
            m = 0.0
            for u in users[i]:
                if prio[u] > m:
                    m = prio[u]
            prio[i] = m + o["lat"] + SYNC_LAT
        finish = [0.0] * n
        ready_t = [0.0] * n
        efree = {e: 0.0 for e in self.engs}
        pending = {e: [] for e in self.engs}
        avail = {e: [] for e in self.engs}
        queues = {e: [] for e in self.engs}
        for o in ops:
            if ndeps[o["id"]] == 0:
                heapq.heappush(pending[o["eng"]], (0.0, o["id"]))
        done = 0
        if not reorder:
            for o in ops:
                queues[o["eng"]].append(o["id"])
            self.queues = queues
            self.makespan = 0
            return
        while done < n:
            best = None
            for e in self.engs:
                pe, av = pending[e], avail[e]
                while pe and pe[0][0] <= efree[e]:
                    _, i = heapq.heappop(pe)
                    heapq.heappush(av, (-prio[i], i))
                if av:
                    cand = (efree[e], av[0][1], e, True)
                elif pe:
                    cand = (pe[0][0], pe[0][1], e, False)
                else:
                    continue
                if best is None or cand[:2] < best[:2]:
                    best = cand
            assert best is not None, "scheduler deadlock"
            st, i, e, from_av = best
            if from_av:
                heapq.heappop(avail[e])
            else:
                heapq.heappop(pending[e])
            o = ops[i]
            efree[e] = st + o["cost"]
            finish[i] = st + o["lat"] + SYNC_LAT
            queues[e].append(i)
            done += 1
            for u in users[i]:
                ndeps[u] -= 1
                if finish[i] > ready_t[u]:
                    ready_t[u] = finish[i]
                if ndeps[u] == 0:
                    heapq.heappush(pending[ops[u]["eng"]], (ready_t[u], u))
        self.queues = queues
        self.makespan = max(finish)

    def emit(self):
        nc = self.nc
        ops = self.ops
        semval = {}
        cnt = {}
        for e in self.engs:
            for i in self.queues[e]:
                o = ops[i]
                if o["kind"] == "op":
                    cnt[o["sem"]] = cnt.get(o["sem"], 0) + 1
                    semval[i] = (o["sem"], cnt[o["sem"]], 1)
        for o in ops:
            if o["kind"] in ("dma", "coll"):
                inc = 16 if o["kind"] == "dma" else 1
                cnt[o["sem"]] = cnt.get(o["sem"], 0) + inc
                semval[o["id"]] = (o["sem"], cnt[o["sem"]], inc)
        with nc.Block() as block:
            for e in self.engs:
                deco = getattr(block, e)
                q = self.queues[e]
                own = "e_" + e

                def body(h, q=q, e=e, own=own):
                    seen = {}
                    for i in q:
                        o = ops[i]
                        need = {}
                        for d in o["deps"]:
                            k, v, _ = semval[d]
                            if k == own and e in self.nself:
                                continue
                            if k == own and ops[d]["kind"] == "op" and i in semval and semval[i][0] == own \
                                    and semval[i][1] - v >= SELF_DIST:
                                continue
                            if need.get(k, 0) < v:
                                need[k] = v
                        for k, v in need.items():
                            if seen.get(k, 0) >= v:
                                continue
                            seen[k] = v
                            h.wait_ge(self.sems[k], v)
                        if o["fn"] is not None:
                            k, v, inc = semval[i]
                            o["fn"](h).then_inc(self.sems[k], inc)
                deco(body)

    def close(self):
        for cm in reversed(self._ctx):
            cm.__exit__(None, None, None)


def build(debug=False, MIXERS="ABCD", STOP=99, NLAYERS=2, REORDER=True):
    nc = bass.Bass("TRN2", target_bir_lowering=False)
    es = ExitStack()
    S = Sched(nc)
    for n in ("sync", "gpsimd", "tensor", "vector", "scalar"):
        S.add_engine(n)

    def dram(name, shape, dt=F32, kind="ExternalInput"):
        return nc.dram_tensor(name, list(shape), dt, kind=kind).ap()

    layer = 0
    P2 = False
    last = False

    d_hin = dram("hin", [T, DM])
    d_halo = dram("halo", [3, DM])
    d_win_all = dram("w_in", [2, DM, DPROJ])
    d_wout_all = dram("w_out", [2, 2048, DM])
    d_nw_all = dram("nw", [2, DM])
    d_fnw = dram("fnw", [DM])
    d_pp_all = dram("pp", [2, 128, NPP])
    d_pr_all = dram("pr", [2, NPR])
    d_gw_all = dram("gw", [2, 16, 256])
    d_pm = dram("pm", [128, 9])
    d_out = dram("hout", [T, DM], kind="ExternalOutput")
    if debug:
        d_dbg = dram("dbg", [T, 2048], BF16, kind="ExternalOutput")
    d_sloc = [nc.dram_tensor(f"sloc{l}", [128, NSUM], F32).ap() for l in range(2)]
    d_sgat = [nc.dram_tensor(f"sgat{l}", [4 * 128, NSUM], F32).ap() for l in range(2)]
    d_hloc = nc.dram_tensor("hloc", [3, DM], F32).ap()
    d_hgat = nc.dram_tensor("hgat", [12, DM], F32).ap()
    GROUPS = [[0, 1, 2, 3], [4, 5, 6, 7]]

    cnt = [0]
    dbg_ids = []

    def sb(shape, dt, name=None):
        cnt[0] += 1
        name = "s_" + (name or f"sb{cnt[0]}")
        t = es.enter_context(nc.sbuf_tensor(name, list(shape), dt))
        return Buf(t, name)

    def ps(shape, dt, name):
        return es.enter_context(nc.psum_tensor(name, list(shape), dt))

    def withsem(b):
        b.sem = S.dma_sem(b.name)
        return b

    def fsz(ap):
        n = 1
        for (_, c) in list(ap.ap)[1:]:
            n *= c
        return n

    def is_psum(ap):
        return "PSum" in type(ap.tensor).__name__

    def ACT(out, in_, func, R, W, **kw):
        c = 200 + 0.85 * fsz(out)
        return S.op("scalar", lambda e: e.activation(out=out, in_=in_, func=func, **kw), R, W, cost=c)

    def _dve_cost(out, ins):
        c = 70 + 1.05 * fsz(out)
        if any(is_psum(a) for a in ins):
            c += 60
        return c

    def TT(out, in0, in1, op, R, W, eng="vector"):
        return S.op(eng, lambda e: e.tensor_tensor(out=out, in0=in0, in1=in1, op=op), R, W,
                    cost=_dve_cost(out, [in0, in1]))

    def TS(out, in0, s1, s2, op0, op1, R, W, eng="vector"):
        c = _dve_cost(out, [in0])
        if s2 is None:
            return S.op(eng, lambda e: e.tensor_scalar(out=out, in0=in0, scalar1=s1, scalar2=None, op0=op0), R, W, cost=c)
        return S.op(eng, lambda e: e.tensor_scalar(out=out, in0=in0, scalar1=s1, scalar2=s2, op0=op0, op1=op1), R, W, cost=c)

    def STT(out, in0, scalar, in1, op0, op1, R, W, eng="vector"):
        return S.op(eng, lambda e: e.scalar_tensor_tensor(out=out, in0=in0, scalar=scalar, in1=in1, op0=op0, op1=op1), R, W,
                    cost=_dve_cost(out, [in0, in1]))

    def CP(out, in_, R, W, eng="vector"):
        return S.op(eng, lambda e: e.tensor_copy(out=out, in_=in_), R, W, cost=_dve_cost(out, [in_]))

    def MSET(ap, val, W, eng="gpsimd"):
        return S.op(eng, lambda e: e.memset(ap, val), (), W, cost=200 + fsz(ap))

    def MM(out, lhsT, rhs, start, stop, R, W):
        n = fsz(rhs)
        f32 = "float32" in str(rhs.tensor.dtype)
        c = (64 + 1.7 * n) if f32 else (64 + 0.45 * n)
        return S.op("tensor", lambda e: e.matmul(out, lhsT=lhsT, rhs=rhs, start=start, stop=stop), R, W, cost=c)

    def TR(out, in_, ident, R, W):
        return S.op("tensor", lambda e: e.transpose(out, in_, ident), R, W, cost=150)

    def DMA(q, out, in_, buf_sem, R, W):
        nbytes = fsz(out) * 128 * 4
        issue = 1000 if q == "gpsimd" else 80
        return S.dma(q, lambda e: e.dma_start(out=out, in_=in_), buf_sem.sem, R, W, cost=issue,
                     lat=issue + 2000 + nbytes / 120.0)

    identf = sb([128, 128], F32, "identf")
    identb = sb([128, 128], BF16, "identb")
    trif = sb([128, 128], F32, "trif")
    suf = sb([128, 128], F32, "suf")
    negb_ = sb([128, 128], BF16, "negmask")
    onesf = sb([128, 128], F32, "onesf")
    rst = sb([128, 512], F32, "rst")
    MSET(onesf.t[:], 1.0, [onesf])
    MSET(identf.t[:], 1.0, [identf])
    S.op("gpsimd", lambda e: e.affine_select(out=identf.t[:], in_=identf.t[:], pattern=[[-1, 128]],
                                             compare_op=ALU.is_equal, fill=0.0, base=0, channel_multiplier=1),
         [identf], [identf])
    CP(identb.t[:], identf.t[:], [identf], [identb])
    MSET(trif.t[:], 1.0, [trif])
    S.op("gpsimd", lambda e: e.affine_select(out=trif.t[:], in_=trif.t[:], pattern=[[1, 128]],
                                             compare_op=ALU.is_ge, fill=0.0, base=0, channel_multiplier=-1),
         [trif], [trif])
    TS(suf.t[:], trif.t[:], -1.0, 1.0, ALU.mult, ALU.add, [trif], [suf])
    TS(negb_.t[:], suf.t[:], -30000.0, None, ALU.mult, None, [suf], [negb_])
    MSET(rst.t[:], 1.0, [rst])
    rst3 = rst.t[:, :].rearrange("p (c k) -> p c k", k=64)
    MSET(rst3[:, :, 0:1], 0.0, [rst])

    hres_t = es.enter_context(nc.sbuf_tensor("hres", [128, NT, DM], F32))
    HT = [withsem(Buf(hres_t, f"ht{t}")) for t in range(NT)]
    nwbc = withsem(sb([128, DM], F32, "nwbc"))
    uT = sb([128, KC, 515], BF16, "uT")
    ubf = sb([128, DM], BF16, "ubf")
    junk = sb([128, DM], BF16, "junk")
    NSLOT = 3
    slots = [withsem(sb([128, 4096], BF16, f"wslot{i}")) for i in range(NSLOT)]
    wsm = withsem(sb([128, KC, 32], BF16, "wsm"))
    pp = withsem(sb([128, NPP], F32, "pp"))
    pr = withsem(sb([128, NPR], F32, "pr"))
    gwf = withsem(sb([16, 256], F32, "gwf"))
    gwb = sb([16, 256], BF16, "gwb")
    grt = sb([16, 512], BF16, "grt")
    pm = withsem(sb([128, 9], F32, "pm"))
    stf_t = es.enter_context(nc.sbuf_tensor("stf", [128, NSUM], F32))
    stb_t = es.enter_context(nc.sbuf_tensor("stb", [128, NSTATE], BF16))
    STF = {(m, h): Buf(stf_t, "stf%s%d" % (m, h)) for m in "ABCD" for h in range(4)}
    STB = {(m, h): Buf(stb_t, "stb%s%d" % (m, h)) for m in "ABCD" for h in range(4)}
    STDEC = Buf(stf_t, "stdec")
    stsem = withsem(Buf(stf_t, "stout"))
    halo_raw = sb([128, 16, 3], F32, "halo_raw")
    BFB = [sb([128, TPS, 528], BF16, f"bfb{i}") for i in range(6)]
    FB = [sb([128, 515], F32, f"fb{i}") for i in range(6)]
    TTB = [sb([128, 512], F32, f"ttb{i}") for i in range(4)]
    SMF = [sb([128, 128], F32, f"smf{i}") for i in range(4)]
    SMB = [sb([128, 128], BF16, f"smb{i}") for i in range(4)]
    VDB = [sb([128, 512], BF16, f"vdb{i}") for i in range(2)]
    MIXB = [sb([128, 512], BF16, f"mix{i}") for i in range(2)]
    mixT = sb([128, 4, ST], BF16, "mixT")
    sml = sb([128, 512], F32, "sml")
    smp = Buf(sml.t, "smp")
    smn = Buf(sml.t, "smn")
    smd = Buf(sml.t, "smd")
    SML = {}
    _smo = [0]

    def small(name, n):
        o = _smo[0]
        _smo[0] += n
        assert _smo[0] <= 512
        SML[name] = (o, n)
        return sml.t[:, o:o + n]

    gates_t = es.enter_context(nc.sbuf_tensor("gates", [128, 512], F32))
    GATES = Buf(gates_t, "gatesB")
    GATESD = Buf(gates_t, "gatesD")
    egl_t = es.enter_context(nc.sbuf_tensor("egl", [128, 4, 8], F32))
    EGLB = [Buf(egl_t, "egl%d" % i) for i in range(4)]
    halo_h = withsem(sb([128, DM], F32, "halo_h"))
    for i in range(4):
        withsem(TTB[i])

    pj_t = [ps([128, 512], F32, f"pj{i}") for i in range(2)]
    PJ = [Buf(t, f"pj{i}", excl=True) for i, t in enumerate(pj_t)]
    ptu_t = ps([128, 1024], BF16, "ptu")
    PTU = Buf(ptu_t, "ptu", excl=True)
    pts_t = ps([128, 1024], BF16, "pts")
    PTSB = Buf(pts_t, "pts", excl=True)
    PTS = []
    for i in range(8):
        PTS.append((PTSB, i * 128))
        PTS.append((PTU, i * 128))
    pf_t = [ps([128, 512], F32, f"pf{i}") for i in range(2)]
    PFB = [Buf(pf_t[i], f"pf{i}", excl=True) for i in range(2)]
    PF = [(PFB[i % 2], (i // 2) * 128) for i in range(8)]
    pw_t = [ps([128, 512], F32, f"pw{i}") for i in range(2)]
    PWB = [Buf(pw_t[i], f"pw{i}", excl=True) for i in range(2)]
    PW = [(PWB[i % 2], (i // 2) * 256) for i in range(4)]
    rot = {}

    def nxt(pool, key):
        i = rot.get(key, 0)
        rot[key] = i + 1
        return pool[i % len(pool)]

    def pjn():
        return nxt(PJ, "pj")

    def pfn():
        b, o = nxt(PF, "pf")
        return b, b.t[:, o:o + 128]

    def pwn():
        b, o = nxt(PW, "pw")
        return b, b.t[:, o:o + 256]

    def ptsn():
        b, o = nxt(PTS, "pts")
        return b, b.t[:, o:o + 128]

    def fbn():
        return nxt(FB, "fb")

    def slot3(s):
        return s.t[:, :].rearrange("p (k c) -> p k c", k=KC)

    def slot_wo(s):
        return s.t[:, :].rearrange("p (k c) -> p k c", k=4)

    win3 = None
    wout3 = None

    wcache = {}
    slot_hw = {sl.name: withsem(Buf(None, sl.name + "_hw")) for sl in slots}
    d_wscr = nc.dram_tensor("wscr", [48, 128, 4096], BF16).ap()

    def _load_cached(key, slot_view, src_ap, n):
        s = nxt(slots, "slot")
        dst = slot_view(s)
        if key not in wcache:
            scr = d_wscr[len(wcache)]
            sbuf = withsem(Buf(None, "wscr%d" % len(wcache)))
            wcache[key] = (scr, sbuf)
            DMA("gpsimd", dst, src_ap, s, [], [s])
            DMA("sync", scr, s.t[:, :], sbuf, [s], [sbuf])
        else:
            scr, sbuf = wcache[key]
            DMA("sync", s.t[:, :], scr, slot_hw[s.name], [sbuf], [s])
        return s

    def load_w(c0, c1):
        n = c1 - c0
        return _load_cached((layer, "i", c0, c1), lambda s: slot3(s)[:, :, 0:n], win3[:, :, c0:c1], n)

    def load_wo(m):
        return _load_cached((layer, "o", m), lambda s: slot_wo(s)[:, :, :], wout3[:, m * 4:(m + 1) * 4, :], 4096)

    DMA("sync", pm.t[:], d_pm, pm, [], [pm])
    for t in range(NT):
        DMA("sync", hres_t[:, t, :], d_hin[t * 128:(t + 1) * 128, :], HT[t], [], [HT[t]])
    MSET(halo_h.t[:], 0.0, [halo_h])
    DMA("sync", halo_h.t[0:3, :], d_halo, halo_h, [], [halo_h])

    negb = small("negb", 2)
    ib2 = small("ib2", 4)
    aneg = small("aneg", 8)
    lbv = small("lbv", 4)
    omlb = small("omlb", 4)
    ss4 = small("ss4", 4)
    rs4 = small("rs4", 4)
    rr4 = small("rr4", 4)
    ssn = small("ssn", 2)
    rsn = small("rsn", 2)
    decp = small("decp", NDEC)
    acs_s = small("acs", 8)
    ee_s = small("ee", 8)
    etot_s = small("etot", 8)
    dec_s = small("dec", 8)
    dd_s = small("dd", 8)
    allst = list(STF.values()) + [STDEC]
    SLOC = [withsem(Buf(None, f"sloc{l}")) for l in range(2)]
    SGAT = [withsem(Buf(None, f"sgat{l}")) for l in range(2)]
    HLOC = withsem(Buf(None, "hloc"))
    HGAT = withsem(Buf(None, "hgat"))

    def load_layer_params(l):
        DMA("sync", nwbc.t[:], d_nw_all[l].partition_broadcast(128), nwbc, [], [nwbc])
        DMA("sync", pp.t[:], d_pp_all[l], pp, [], [pp])
        DMA("sync", pr.t[:], d_pr_all[l].partition_broadcast(128), pr, [], [pr])
        DMA("sync", gwf.t[:], d_gw_all[l], gwf, [], [gwf])
        CP(gwb.t[:], gwf.t[:], [gwf], [gwb])
        DMA("gpsimd", wsm.t[:, :, 0:16], win3[:, :, 1024:1040], wsm, [], [wsm])
        DMA("gpsimd", wsm.t[:, :, 16:24], win3[:, :, 3088:3096], wsm, [], [wsm])
        DMA("gpsimd", wsm.t[:, :, 24:32], win3[:, :, 7192:7200], wsm, [], [wsm])
        TS(negb, pp.t[:, PP_GB:PP_GB + 2], -1.0, None, ALU.mult, None, [pp], [smp])
        TS(ib2, pr.t[:, PR_IB:PR_IB + 4], math.log(128 ** -0.5), None, ALU.add, None, [pr], [smp])
        ACT(aneg, pr.t[:, PR_ALOG:PR_ALOG + 8], AF.Exp, [pr], [smp])
        TS(aneg, aneg, -1.0, None, ALU.mult, None, [smp], [smp])
        lg3 = pp.t[:, PP_LB:PP_LB + 8].rearrange("p (b l) -> p b l", l=2)
        if l == 0:
            MSET(lbv, 0.0, [smp], eng="vector")
        else:
            TT(lbv, lg3[:, :, 1], lg3[:, :, 0], ALU.subtract, [pp], [smp])
            ACT(lbv, lbv, AF.Sigmoid, [smp], [smp])
        TS(omlb, lbv, -1.0, 1.0, ALU.mult, ALU.add, [smp], [smp])

    cgroups = []
    for blk in range(2):
        cgroups.append((OFF_A + blk * 128, 128, DEC_A + blk))
    for h in range(4):
        cgroups.append((OFF_B + h * 132, 132, DEC_B + h))
    for h in range(4):
        cgroups.append((OFF_C + h * 128, 128, DEC_C + h))
    for h in range(8):
        cgroups.append((OFF_D + h * 64, 64, DEC_D + h))

    def init_pass():
        MSET(stf_t[:, 0:NSTATE], 0.0, allst, eng="vector")
        if not P2:
            MSET(stf_t[:, NSTATE:NSUM], 1.0, allst, eng="vector")
        else:
            for j in range(3):
                DMA("sync", cmbv[:, 0:NSUM], d_sgat[layer][j * 128:(j + 1) * 128, :], cmb, [SGAT[layer]], [cmb])
                TS(decp, cmbv[:, NSTATE:NSUM], pm.t[:, j:j + 1], pm.t[:, 3 + j:4 + j], ALU.mult, ALU.add,
                   [cmb, pm], [smd])
                TS(cmbv[:, 0:NSTATE], cmbv[:, 0:NSTATE], pm.t[:, j:j + 1], None, ALU.mult, None, [cmb, pm], [cmb])
                for (o, n, g) in cgroups:
                    STT(stf_t[:, o:o + n], stf_t[:, o:o + n], decp[:, g:g + 1], cmbv[:, o:o + n], ALU.mult, ALU.add,
                        allst + [cmb, smd], allst)
        CP(stb_t[:, :], stf_t[:, 0:NSTATE], allst, list(STB.values()))

    def end_p1():
        l = layer
        DMA("sync", d_sloc[l], stf_t[:, :], SLOC[l], allst, [SLOC[l]])
        S.coll("gpsimd", lambda e: e.collective_compute("AllGather", ALU.bypass, replica_groups=GROUPS,
                                                        ins=[d_sloc[l]], outs=[d_sgat[l]]),
               SGAT[l].sem, [SLOC[l]], [SGAT[l]])

    def exchange_halo():
        DMA("sync", d_hloc, hres_t[125:128, NT - 1, :], HLOC, [HT[NT - 1]], [HLOC])
        S.coll("gpsimd", lambda e: e.collective_compute("AllGather", ALU.bypass, replica_groups=GROUPS,
                                                        ins=[d_hloc], outs=[d_hgat]),
               HGAT.sem, [HLOC], [HGAT])
        MSET(halo_h.t[:], 0.0, [halo_h])
        for j in range(3):
            for half in range(2):
                tb = nxt(TTB, "ttb")
                hs = slice(half * 512, (half + 1) * 512)
                DMA("sync", tb.t[0:3, :], d_hgat[j * 3:(j + 1) * 3, hs], tb, [HGAT], [tb])
                STT(halo_h.t[0:3, hs], tb.t[0:3, :], pm.t[0:3, 6 + j:7 + j], halo_h.t[0:3, hs], ALU.mult, ALU.add,
                    [tb, pm, halo_h], [halo_h])

    def tokc(t):
        return slice(3 + t * 128, 3 + (t + 1) * 128)

    def rstd_of(out_ap, ss_ap, n):
        TS(out_ap, ss_ap, 1.0 / n, EPS, ALU.mult, ALU.add, [smn], [smn])
        ACT(out_ap, out_ap, AF.Ln, [smn], [smn])
        ACT(out_ap, out_ap, AF.Exp, [smn], [smn], scale=-0.5)

    def make_uT_tile(hbuf, h_ap, dst_cols, ncol):
        ACT(junk.t[:], h_ap, AF.Square, [hbuf], [smn], accum_out=ss4[:, 0:1])
        rstd_of(rs4[:, 0:1], ss4[:, 0:1], DM)
        STT(ubf.t[:], h_ap, rs4[:, 0:1], nwbc.t[:], ALU.mult, ALU.mult, [hbuf, smn, nwbc], [ubf])
        for k in range(KC):
            TR(ptu_t[:, k * 128:(k + 1) * 128], ubf.t[:, k * 128:(k + 1) * 128], identb.t[:], [ubf, identb], [PTU])
        src = ptu_t[:, :].rearrange("p (k c) -> p k c", k=KC)[:, :, 0:ncol]
        CP(uT.t[:, :, dst_cols], src, [PTU], [uT])

    def proj_fm(slot, off, M, cols=slice(3, 515), n=512):
        pj = pjn()
        v = slot3(slot)
        for k in range(KC):
            MM(pj.t[0:M, 0:n], v[:, k, off:off + M], uT.t[:, k, cols], k == 0, k == KC - 1, [slot, uT], [pj])
        return pj

    def proj_tm(slot, ncols, t):
        pj = pjn()
        v = slot3(slot)
        for k in range(KC):
            MM(pj.t[:, 0:ncols], uT.t[:, k, tokc(t)], v[:, k, 0:ncols], k == 0, k == KC - 1, [slot, uT], [pj])
        return pj

    def proj_small(c0, c1, t):
        b, ap = pfn()
        n = c1 - c0
        for k in range(KC):
            MM(ap[:, 0:n], uT.t[:, k, tokc(t)], wsm.t[:, k, c0:c1], k == 0, k == KC - 1, [wsm, uT], [b])
        return b, ap

    def conv_block(s_idx, slot, off, hidx, wcol, bcol, out_ap, out_buf):
        pj = proj_fm(slot, off, 128)
        raw = fbn()
        if s_idx == 0:
            b, ap = pfn()
            v = slot3(slot)
            for k in range(KC):
                MM(ap[:, 0:3], v[:, k, off:off + 128], uT.t[:, k, 0:3], k == 0, k == KC - 1, [slot, uT], [b])
            CP(raw.t[:, 0:3], ap[:, 0:3], [b], [raw])
        else:
            CP(raw.t[:, 0:3], halo_raw.t[:, hidx, :], [halo_raw], [raw])
        ACT(raw.t[:, 3:515], pj.t[:, :], AF.Copy, [pj], [raw])
        CP(halo_raw.t[:, hidx, :], raw.t[:, 512:515], [raw], [halo_raw])
        acc = fbn()
        w = lambda k: pp.t[:, wcol + k:wcol + k + 1]
        ACT(acc.t[:, 0:512], raw.t[:, 0:512], AF.Identity, [raw, pp], [acc], scale=w(0), bias=pp.t[:, bcol:bcol + 1])
        for k in (1, 2, 3):
            STT(acc.t[:, 0:512], raw.t[:, k:k + 512], w(k), acc.t[:, 0:512], ALU.mult, ALU.add, [raw, pp, acc], [acc])
        ACT(out_ap, acc.t[:, 0:512], AF.Silu, [acc], [out_buf])

    def finish_mixer(s_idx, m, t, mix):
        if debug:
            tg = s_idx * TPS + t
            dbg_ids.append(S.dma("sync", lambda e: e.dma_start(out=d_dbg[tg * 128:(tg + 1) * 128, m * 512:(m + 1) * 512], in_=mix.t[:]),
                                 mix.sem, [mix], []))
        for b4 in range(4):
            pb, pap = ptsn()
            TR(pap, mix.t[:, b4 * 128:(b4 + 1) * 128], identb.t[:], [mix, identb], [pb])
            ACT(mixT.t[:, b4, t * 128:(t + 1) * 128], pap, AF.Copy, [pb], [mixT])

    def out_proj(s_idx, m):
        s = load_wo(m)
        v = slot_wo(s)
        for t in range(TPS):
            tg = s_idx * TPS + t
            for half in range(2):
                pj = pjn()
                for kc in range(4):
                    MM(pj.t[:, :], mixT.t[:, kc, t * 128:(t + 1) * 128], v[:, kc, half * 512:(half + 1) * 512],
                       kc == 0, kc == 3, [s, mixT], [pj])
                hap = hres_t[:, tg, half * 512:(half + 1) * 512]
                TT(hap, hap, pj.t[:, :], ALU.add, [HT[tg], pj], [HT[tg]])

    def scalar_decay(t, nh, a_ap, a_buf, groups, ybuf, yap, dvh, pad):
        bpa, pa = pfn()
        MM(pa[:, 0:nh], trif.t[:], a_ap, True, True, [trif, a_buf], [bpa])
        MM(pa[:, 64:64 + nh], onesf.t[:], a_ap, True, True, [onesf, a_buf], [bpa])
        CP(acs_s[:, 0:nh], pa[:, 0:nh], [bpa], [smd])
        if P2:
            ACT(ee_s[:, 0:nh], pa[:, 0:nh], AF.Exp, [bpa], [smd])
        ACT(etot_s[:, 0:nh], pa[:, 64:64 + nh], AF.Exp, [bpa], [smd])
        TT(dd_s[:, 0:nh], pa[:, 64:64 + nh], acs_s[:, 0:nh], ALU.subtract, [bpa, smd], [smd])
        ACT(dec_s[:, 0:nh], dd_s[:, 0:nh], AF.Exp, [smd], [smd])
        for g in groups:
            ncg = g["ncg"]
            heads = g["heads"]
            if P2:
                bqk, pqk = pfn()
                MM(pqk, g["kT"], g["qT"], True, True, g["qkbufs"], [bqk])
                byo, pyo = pwn()
                MM(pyo[:, 0:ncg], g["qT"], g["sb"][:, 0:ncg], True, True, g["qkbufs"] + [g["stb"]], [byo])
                byd, pyd = pwn()
            vd = nxt(VDB, "vdb")
            for j, h in enumerate(heads):
                cs = slice(j * pad, j * pad + dvh)
                if P2:
                    lh = nxt(SMF, "smf")
                    ACT(lh.t[:], suf.t[:], AF.Identity, [suf, a_buf], [lh], scale=a_ap[:, h:h + 1])
                    bsg, psg = pfn()
                    MM(psg, lh.t[:], trif.t[:], True, False, [lh, trif], [bsg])
                    MM(psg, identb.t[:], negb_.t[:], False, True, [identb, negb_], [bsg])
                    esg = nxt(SMF, "smf")
                    ACT(esg.t[:], psg, AF.Exp, [bsg], [esg])
                    wb = nxt(SMB, "smb")
                    TT(wb.t[:], pqk, esg.t[:], ALU.mult, [bqk, esg], [wb])
                    MM(pyd[:, cs], wb.t[:], g["v"][:, cs], True, True, [wb, g["vbuf"]], [byd])
                if len(heads) == 1:
                    TS(vd.t[:, cs], g["v"][:, cs], dec_s[:, h:h + 1], None, ALU.mult, None, [g["vbuf"], smd], [vd])
            if len(heads) > 1:
                nhh = len(heads)
                so = SML["dec"][0] + heads[0]
                decb = bass.AP(sml.t, so, [[512, 128], [1, nhh], [0, dvh]])
                TT(vd.t[:, 0:ncg].rearrange("p (h c) -> p h c", c=dvh),
                   g["v"][:, 0:ncg].rearrange("p (h c) -> p h c", c=dvh), decb, ALU.mult, [g["vbuf"], smd], [vd])
            bu, pu = pwn()
            MM(pu[:, 0:ncg], g["ktok"], vd.t[:, 0:ncg], True, True, g["ktokbufs"] + [vd], [bu])
            nhh = len(heads)
            h0 = heads[0]
            if P2:
                if nhh == 1:
                    for j, h in enumerate(heads):
                        cs = slice(j * pad, j * pad + dvh)
                        ACT(yap(h), pyd[:, cs], AF.Copy, [byd], [ybuf])
                        STT(yap(h), pyo[:, cs], ee_s[:, h:h + 1], yap(h), ALU.mult, ALU.add, [byo, smd, ybuf], [ybuf])
                else:
                    ycols = ybuf.t[:, h0 * dvh:h0 * dvh + ncg]
                    ACT(ycols, pyd[:, 0:ncg], AF.Copy, [byd], [ybuf])
                    eeb = bass.AP(sml.t, SML["ee"][0] + h0, [[512, 128], [1, nhh], [0, dvh]])
                    tmpb = fbn()
                    TT(tmpb.t[:, 0:ncg].rearrange("p (h c) -> p h c", c=dvh),
                       pyo[:, 0:ncg].rearrange("p (h c) -> p h c", c=dvh), eeb, ALU.mult, [byo, smd], [tmpb])
                    TT(ycols, ycols, tmpb.t[:, 0:ncg], ALU.add, [ybuf, tmpb], [ybuf])
            if nhh == 1:
                h = h0
                cs = slice(0, dvh)
                STT(g["sf"][:, cs], g["sf"][:, cs], etot_s[:, h:h + 1], pu[:, cs], ALU.mult, ALU.add,
                    [g["stf"], smd, bu], [g["stf"]])
            else:
                etb = bass.AP(sml.t, SML["etot"][0] + h0, [[512, 128], [1, nhh], [0, dvh]])
                sf3 = g["sf"][:, 0:ncg].rearrange("p (h c) -> p h c", c=dvh)
                TT(sf3, sf3, etb, ALU.mult, [g["stf"], smd], [g["stf"]])
                TT(g["sf"][:, 0:ncg], g["sf"][:, 0:ncg], pu[:, 0:ncg], ALU.add, [g["stf"], bu], [g["stf"]])
            if not P2:
                dc = stf_t[:, NSTATE + g["decoff"]:NSTATE + g["decoff"] + nhh]
                TT(dc, dc, etot_s[:, h0:h0 + nhh], ALU.mult, [STDEC, smd], [STDEC])
            ACT(g["sb"], g["sf"], AF.Copy, [g["stf"]], [g["stb"]])


    xst = withsem(sb([128, 4, 512], F32, "xst"))
    cmb = xst
    cmbv = xst.t[:, :, :].rearrange("p a b -> p (a b)")
    yb = sb([128, 4, 132], F32, "yb")
    VTD = [sb([128, 512], BF16, f"vtd{i}") for i in range(2)]
    DBGS = [S.dma_sem(f"dbg{i}") for i in range(2)] if debug else None
    for i, mb in enumerate(MIXB):
        mb.sem = DBGS[i] if debug else None

    def recur_vd(s_idx, m, hpb, dk, qTb, kTb, kdtokb, vTb, zGb, soff, doff, nwoff, midx):
        for t in range(TPS):
            pouts = []
            for h in range(4):
                blk = h // hpb
                r0 = (h % hpb) * dk
                rows = slice(r0, r0 + dk)
                hc = slice(h * 128, (h + 1) * 128)
                scol = soff + blk * 128
                sf = stf_t[rows, scol:scol + 128]
                sbf = stb_t[rows, scol:scol + 128]
                tcs = slice(t * 128, (t + 1) * 128)
                if P2:
                    bsc, psc = pfn()
                    MM(psc, kTb.t[rows, blk, tcs], qTb.t[rows, blk, tcs], True, True, [kTb, qTb], [bsc])
                    sm = nxt(SMB, "smb")
                    TT(sm.t[:], psc, trif.t[:], ALU.mult, [bsc, trif], [sm])
                    bo, po = pwn()
                for c in range(2):
                    tr = slice(c * 64, (c + 1) * 64)
                    if P2:
                        MM(po[tr, 0:128], sm.t[tr, tr], vTb.t[tr, t, hc], True, False, [sm, vTb], [bo])
                        MM(po[tr, 0:128], qTb.t[rows, blk, t * 128 + c * 64:t * 128 + (c + 1) * 64], sbf,
                           False, True, [qTb, STB[(m, h)]], [bo])
                    bu, pu = pfn()
                    MM(pu[rows, 0:128], kdtokb.t[tr, t, blk * 128 + r0:blk * 128 + r0 + dk], vTb.t[tr, t, hc],
                       True, True, [kdtokb, vTb], [bu])
                    eg = egl_t[rows, blk, t * 2 + c:t * 2 + c + 1]
                    STT(sf, sf, eg, pu[rows, 0:128], ALU.mult, ALU.add, [STF[(m, h)], EGLB[blk], bu], [STF[(m, h)]])
                    ACT(sbf, sf, AF.Copy, [STF[(m, h)]], [STB[(m, h)]])
                    if not P2:
                        dc = stf_t[rows, NSTATE + doff + blk:NSTATE + doff + blk + 1]
                        TT(dc, dc, eg, ALU.mult, [STDEC, EGLB[blk]], [STDEC])
                if P2:
                    ACT(junk.t[:, 0:128], po[:, 0:128], AF.Square, [bo], [smn], accum_out=ss4[:, h:h + 1])
                    pouts.append((bo, po))
            if P2:
                rstd_of(rs4[:, 0:4], ss4[:, 0:4], 128)
                nwz = nxt(TTB, "ttb")
                TT(nwz.t[:], pr.t[:, nwoff:nwoff + 512], zGb.t[:, t, 0:512], ALU.mult, [pr, zGb], [nwz])
                mix = nxt(MIXB, "mixb")
                for h in range(4):
                    hc = slice(h * 128, (h + 1) * 128)
                    STT(mix.t[:, hc], pouts[h][1][:, 0:128], rs4[:, h:h + 1], nwz.t[:, hc], ALU.mult, ALU.mult,
                        [pouts[h][0], smn, nwz], [mix])
                finish_mixer(s_idx, midx, t, mix)
        if P2:
            out_proj(s_idx, midx)

    def vd_front(blk, kf, csb, sG, qslot, qoff, qscale, qTb, kTb, kdTb):
        ACT(egl_t[:, blk, :], csb.t[:, 63:512:64], AF.Exp, [csb], [EGLB[blk]], scale=sG)
        tmp = fbn()
        if STOP == 31:
            return
        if P2:
            qf = fbn()
            pq = proj_fm(qslot, qoff, 128)
            ACT(qf.t[:, 0:512], pq.t[:, :], AF.Identity, [pq], [qf], scale=qscale)
            if STOP == 32:
                return
            ACT(tmp.t[:, 0:512], csb.t[:, 0:512], AF.Exp, [csb], [tmp], scale=sG)
            TT(qTb.t[:, blk, 0:512], qf.t[:, 0:512], tmp.t[:, 0:512], ALU.mult, [qf, tmp], [qTb])
            if STOP == 33:
                return
            ACT(tmp.t[:, 0:512], csb.t[:, 0:512], AF.Exp, [csb], [tmp], scale=-sG)
            TT(kTb.t[:, blk, 0:512], kf.t[:, 0:512], tmp.t[:, 0:512], ALU.mult, [kf, tmp], [kTb])
        if STOP == 34:
            return
        glb = bass.AP(csb.t, 63, [[515, 128], [64, 8], [0, 64]])
        cs3 = csb.t[:, 0:512].rearrange("p (c k) -> p c k", k=64)
        TT(tmp.t[:, 0:512].rearrange("p (c k) -> p c k", k=64), glb, cs3, ALU.subtract, [csb], [tmp])
        if STOP == 35:
            return
        ACT(tmp.t[:, 0:512], tmp.t[:, 0:512], AF.Exp, [tmp], [tmp], scale=sG)
        if STOP == 36:
            return
        TT(kdTb.t[:, blk, 0:512], kf.t[:, 0:512], tmp.t[:, 0:512], ALU.mult, [kf, tmp], [kdTb])
        if STOP == 37:
            return

    def scan(csb, src):
        S.op("vector", lambda e: e.tensor_tensor_scan(out=csb.t[:, 0:512], data0=rst.t[:, :], data1=src.t[:, 0:512],
                                                      initial=0.0, op0=ALU.mult, op1=ALU.add), [rst, src], [csb],
             cost=1250)

    def vz_and_kdtok(nblk, vslot_cols, zslot_cols, vTb, zGb, kdTb, kdtokb):
        s_v = load_w(*vslot_cols)
        for t in range(TPS):
            pv = proj_tm(s_v, 512, t)
            ACT(vTb.t[:, t, 0:512], pv.t[:, :], AF.Copy, [pv], [vTb])
        if STOP == 41:
            return
        if P2:
            s_z = load_w(*zslot_cols)
            for t in range(TPS):
                pz = proj_tm(s_z, 512, t)
                ACT(zGb.t[:, t, 0:512], pz.t[:, :], AF.Silu, [pz], [zGb])
        if STOP == 42:
            return
        for t in range(TPS):
            for blk in range(nblk):
                pb, pap = ptsn()
                TR(pap, kdTb.t[:, blk, t * 128:(t + 1) * 128], identb.t[:], [kdTb, identb], [pb])
                ACT(kdtokb.t[:, t, blk * 128:(blk + 1) * 128], pap, AF.Copy, [pb], [kdtokb])

    def mixer_A(s_idx):
        qTb, kTb, kdTb, vTb, zGb, kdtokb = BFB
        s_qk = load_w(0, 512)
        pg = pjn()
        for k in range(KC):
            MM(pg.t[0:16, :], wsm.t[:, k, 0:16], uT.t[:, k, 3:515], k == 0, k == KC - 1, [wsm, uT], [pg])
        ACT(grt.t[:], pg.t[0:16, :], AF.Copy, [pg], [grt])
        if STOP < 1:
            return
        for blk in range(2):
            kf = fbn()
            pk = proj_fm(s_qk, 256 + blk * 128, 128)
            ACT(kf.t[:, 0:512], pk.t[:, :], AF.Copy, [pk], [kf])
            px = pjn()
            MM(px.t[:, :], gwb.t[:, blk * 128:(blk + 1) * 128], grt.t[:], True, True, [gwb, grt], [px])
            sp = fbn()
            ACT(sp.t[:, 0:512], px.t[:, :], AF.Exp, [px, smp], [sp], scale=-1.0, bias=negb[:, blk:blk + 1])
            ACT(sp.t[:, 0:512], sp.t[:, 0:512], AF.Ln, [sp], [sp], bias=1.0)
            csb = fbn()
            if STOP < 2:
                continue
            scan(csb, sp)
            if STOP < 3:
                continue
            vd_front(blk, kf, csb, -1.0 / 16, s_qk, blk * 128, 0.125, qTb, kTb, kdTb)
        if STOP < 4 or 30 < STOP < 40:
            return
        vz_and_kdtok(2, (512, 1024), (1040, 1552), vTb, zGb, kdTb, kdtokb)
        if STOP < 5 or 40 < STOP < 50:
            return
        recur_vd(s_idx, "A", 2, 64, qTb, kTb, kdtokb, vTb, zGb, OFF_A, DEC_A, PR_NWA, 0)

    def mixer_C(s_idx):
        qTb, kTb, kdTb, vTb, zGb, kdtokb = BFB
        s_f = load_w(4632, 5144)
        s_q = load_w(4120, 4632) if P2 else None
        for blk in range(4):
            pf_ = proj_fm(s_f, blk * 128, 128)
            ff = fbn()
            ACT(ff.t[:, 0:512], pf_.t[:, :], AF.Sigmoid, [pf_], [ff])
            TS(ff.t[:, 0:512], ff.t[:, 0:512], omlb[:, blk:blk + 1], lbv[:, blk:blk + 1], ALU.mult, ALU.add,
               [ff, smp], [ff])
            kf = fbn()
            TS(kf.t[:, 0:512], ff.t[:, 0:512], -1.0, 1.0, ALU.mult, ALU.add, [ff], [kf])
            lf = fbn()
            ACT(lf.t[:, 0:512], ff.t[:, 0:512], AF.Ln, [ff], [lf])
            csb = fbn()
            scan(csb, lf)
            vd_front(blk, kf, csb, 1.0, s_q, blk * 128, 128 ** -0.5, qTb, kTb, kdTb)
        vz_and_kdtok(4, (5144, 5656), (5656, 6168), vTb, zGb, kdTb, kdtokb)
        recur_vd(s_idx, "C", 1, 128, qTb, kTb, kdtokb, vTb, zGb, OFF_C, DEC_C, PR_NWC, 2)

    def mixer_B(s_idx):
        qTb, kTb, ktokb, vTb, oGb, zGb = BFB
        if P2:
            s_q = load_w(1552, 2064)
            for blk in range(4):
                conv_block(s_idx, s_q, blk * 128, blk, PP_MCW + blk * 4, PP_MCB + blk, qTb.t[:, blk, 0:512], qTb)
        s_k = load_w(2064, 2576)
        for blk in range(4):
            conv_block(s_idx, s_k, blk * 128, 4 + blk, PP_MCW + (4 + blk) * 4, PP_MCB + 4 + blk,
                       kTb.t[:, blk, 0:512], kTb)
        s_v = load_w(2576, 3088)
        for t in range(TPS):
            bg, pgi = proj_small(16, 24, t)
            gi = gates_t[:, t * 16:t * 16 + 4]
            TT(gi, pgi[:, 0:4], ib2, ALU.add, [bg, smp], [GATES])
            ACT(gi, gi, AF.Exp, [GATES], [GATES])
            gf = gates_t[:, t * 16 + 4:t * 16 + 8]
            TT(gf, pgi[:, 4:8], pr.t[:, PR_FB:PR_FB + 4], ALU.add, [bg, pr], [GATES])
            ACT(gf, gf, AF.Exp, [GATES], [GATES], scale=-1.0)
            ACT(gf, gf, AF.Ln, [GATES], [GATES], bias=1.0)
            TS(gf, gf, -1.0, None, ALU.mult, None, [GATES], [GATES])
            pv = proj_tm(s_v, 512, t)
            vt4 = vTb.t[:, t, :].rearrange("p (h c) -> p h c", c=132)
            for h in range(4):
                TS(vt4[:, h, 0:128], pv.t[:, h * 128:(h + 1) * 128], gi[:, h:h + 1], None, ALU.mult, None,
                   [pv, GATES], [vTb])
            CP(vt4[:, :, 128], gi, [GATES], [vTb])
        if P2:
            s_o = load_w(3096, 3608)
            for t in range(TPS):
                po_ = proj_tm(s_o, 512, t)
                ACT(oGb.t[:, t, 0:512], po_.t[:, :], AF.Sigmoid, [po_], [oGb])
            s_z = load_w(3608, 4120)
            for t in range(TPS):
                pz = proj_tm(s_z, 512, t)
                ACT(zGb.t[:, t, 0:512], pz.t[:, :], AF.Silu, [pz], [zGb])
        for t in range(TPS):
            for h in range(4):
                pb, pap = ptsn()
                TR(pap, kTb.t[:, h, t * 128:(t + 1) * 128], identb.t[:], [kTb, identb], [pb])
                ACT(ktokb.t[:, t, h * 128:(h + 1) * 128], pap, AF.Copy, [pb], [ktokb])
        for t in range(TPS):
            tcs = slice(t * 128, (t + 1) * 128)
            vt4 = vTb.t[:, t, :].rearrange("p (h c) -> p h c", c=132)
            groups = []
            for h in range(4):
                groups.append(dict(qT=qTb.t[:, h, tcs], kT=kTb.t[:, h, tcs], qkbufs=[qTb, kTb],
                                   ktok=ktokb.t[:, t, h * 128:(h + 1) * 128], ktokbufs=[ktokb],
                                   v=vt4[:, h, :], vbuf=vTb, heads=[h], ncg=129,
                                   sf=stf_t[:, OFF_B + h * 132:OFF_B + (h + 1) * 132],
                                   sb=stb_t[:, OFF_B + h * 132:OFF_B + (h + 1) * 132],
                                   stf=STF[("B", h)], stb=STB[("B", h)], decoff=DEC_B + h))
            a_ap = gates_t[:, t * 16 + 4:t * 16 + 8]
            scalar_decay(t, 4, a_ap, GATES, groups, yb, lambda h: yb.t[:, h, 0:129], 129, 132)
            if P2:
                TS(rr4, yb.t[:, :, 128], -1.0, None, ALU.mult, None, [yb], [smn])
                TT(rr4, rr4, yb.t[:, :, 128], ALU.max, [yb, smn], [smn])
                TS(rr4, rr4, 1.0, None, ALU.max, None, [smn], [smn])
                S.op("vector", lambda e: e.reciprocal(out=rr4, in_=rr4), [smn], [smn])
                hb = nxt(TTB, "ttb")
                for h in range(4):
                    hc = slice(h * 128, (h + 1) * 128)
                    STT(hb.t[:, hc], yb.t[:, h, 0:128], rr4[:, h:h + 1], oGb.t[:, t, hc], ALU.mult, ALU.mult,
                        [yb, smn, oGb], [hb])
                for h in range(4):
                    hc = slice(h * 128, (h + 1) * 128)
                    ACT(junk.t[:, 0:128], hb.t[:, hc], AF.Square, [hb], [smn], accum_out=ss4[:, h:h + 1])
                rstd_of(rs4[:, 0:4], ss4[:, 0:4], 128)
                nwz = nxt(TTB, "ttb")
                TT(nwz.t[:], pr.t[:, PR_NWB:PR_NWB + 512], zGb.t[:, t, 0:512], ALU.mult, [pr, zGb], [nwz])
                mix = nxt(MIXB, "mixb")
                for h in range(4):
                    hc = slice(h * 128, (h + 1) * 128)
                    STT(mix.t[:, hc], hb.t[:, hc], rs4[:, h:h + 1], nwz.t[:, hc], ALU.mult, ALU.mult,
                        [hb, smn, nwz], [mix])
                finish_mixer(s_idx, 1, t, mix)
        if P2:
            out_proj(s_idx, 1)

    def mixer_D(s_idx):
        bcTb, btokb, _, _, _, zGb = BFB
        s_x = load_w(6168, 6680)
        for blk in range(4):
            conv_block(s_idx, s_x, blk * 128, 8 + blk, PP_SCW + blk * 4, PP_SCB + blk, xst.t[:, blk, :], xst)
        s_bc = load_w(6680, 7192)
        for blk in (range(4) if P2 else range(2)):
            conv_block(s_idx, s_bc, blk * 128, 12 + blk, PP_SCW + (4 + blk) * 4, PP_SCB + 4 + blk,
                       bcTb.t[:, blk, 0:512], bcTb)
        if P2:
            s_z = load_w(7200, 7712)
            for t in range(TPS):
                pz = proj_tm(s_z, 512, t)
                ACT(zGb.t[:, t, 0:512], pz.t[:, :], AF.Silu, [pz], [zGb])
        for t in range(TPS):
            for g in range(2):
                pb, pap = ptsn()
                TR(pap, bcTb.t[:, g, t * 128:(t + 1) * 128], identb.t[:], [bcTb, identb], [pb])
                ACT(btokb.t[:, t, g * 128:(g + 1) * 128], pap, AF.Copy, [pb], [btokb])
        for t in range(TPS):
            tcs = slice(t * 128, (t + 1) * 128)
            bd, pdt = proj_small(24, 32, t)
            dtv = gates_t[:, 256 + t * 32:256 + t * 32 + 8]
            TT(dtv, pdt[:, 0:8], pr.t[:, PR_DTB:PR_DTB + 8], ALU.add, [bd, pr], [GATESD])
            ACT(dtv, dtv, AF.Exp, [GATESD], [GATESD])
            ACT(dtv, dtv, AF.Ln, [GATESD], [GATESD], bias=1.0)
            av = gates_t[:, 256 + t * 32 + 8:256 + t * 32 + 16]
            TT(av, dtv, aneg, ALU.mult, [GATESD, smp], [GATESD])
            xs = nxt(TTB, "ttb")
            for blk in range(4):
                bx, pxr = pwn()
                S.op("tensor", lambda e, pxr=pxr, blk=blk, tcs=tcs: e.transpose(pxr[:, 0:128], xst.t[:, blk, tcs], identf.t[:]),
                     [xst, identf], [bx], cost=400)
                CP(xs.t[:, blk * 128:(blk + 1) * 128], pxr[:, 0:128], [bx], [xs])
            vt = nxt(VTD, "vtd")
            dtb = bass.AP(gates_t, 256 + t * 32, [[512, 128], [1, 8], [0, 64]])
            TT(vt.t[:, :].rearrange("p (h c) -> p h c", c=64), xs.t[:, :].rearrange("p (h c) -> p h c", c=64),
               dtb, ALU.mult, [xs, GATESD], [vt])
            groups = []
            for g in range(2):
                groups.append(dict(qT=bcTb.t[:, 2 + g, tcs], kT=bcTb.t[:, g, tcs], qkbufs=[bcTb],
                                   ktok=btokb.t[:, t, g * 128:(g + 1) * 128], ktokbufs=[btokb],
                                   v=vt.t[:, g * 256:(g + 1) * 256], vbuf=vt, heads=[4 * g + j for j in range(4)],
                                   ncg=256, sf=stf_t[:, OFF_D + g * 256:OFF_D + (g + 1) * 256],
                                   sb=stb_t[:, OFF_D + g * 256:OFF_D + (g + 1) * 256],
                                   stf=STF[("D", g)], stb=STB[("D", g)], decoff=DEC_D + 4 * g))
            Y = nxt(TTB, "ttb")
            scalar_decay(t, 8, av, GATESD, groups, Y, lambda h: Y.t[:, h * 64:(h + 1) * 64], 64, 64)
            if P2:
                dsb = bass.AP(pr.t, PR_D, [[NPR, 128], [1, 8], [0, 64]])
                xs3 = xs.t[:, :].rearrange("p (h c) -> p h c", c=64)
                TT(xs3, xs3, dsb, ALU.mult, [xs, pr], [xs])
                TT(Y.t[:], Y.t[:], xs.t[:], ALU.add, [Y, xs], [Y])
                TT(Y.t[:], Y.t[:], zGb.t[:, t, 0:512], ALU.mult, [Y, zGb], [Y])
                for g in range(2):
                    ACT(junk.t[:, 0:256], Y.t[:, g * 256:(g + 1) * 256], AF.Square, [Y], [smn],
                        accum_out=ssn[:, g:g + 1])
                rstd_of(rsn[:, 0:2], ssn[:, 0:2], 256)
                mix = nxt(MIXB, "mixb")
                for g in range(2):
                    gs = slice(g * 256, (g + 1) * 256)
                    STT(mix.t[:, gs], Y.t[:, gs], rsn[:, g:g + 1], pr.t[:, PR_NWD + g * 256:PR_NWD + (g + 1) * 256],
                        ALU.mult, ALU.mult, [Y, smn, pr], [mix])
                finish_mixer(s_idx, 3, t, mix)
        if P2:
            out_proj(s_idx, 3)

    finals = []
    for layer in range(NLAYERS):
        win3 = d_win_all[layer].rearrange("(k p) c -> p k c", p=128)
        wout3 = d_wout_all[layer].rearrange("(k p) c -> p k c", p=128)
        load_layer_params(layer)
        for P2 in (False, True):
            last = P2 and layer == NLAYERS - 1
            init_pass()
            for s_idx in range(NS):
                if s_idx == 0:
                    make_uT_tile(halo_h, halo_h.t[:], slice(0, 3), 3)
                else:
                    CP(uT.t[:, :, 0:3], uT.t[:, :, 512:515], [uT], [uT])
                for t in range(TPS):
                    tg = s_idx * TPS + t
                    make_uT_tile(HT[tg], hres_t[:, tg, :], tokc(t), 128)
                if "A" in MIXERS:
                    mixer_A(s_idx)
                if "B" in MIXERS:
                    mixer_B(s_idx)
                if "C" in MIXERS:
                    mixer_C(s_idx)
                if "D" in MIXERS:
                    mixer_D(s_idx)
            if not P2:
                end_p1()
            elif not last:
                exchange_halo()

    DMA("sync", nwbc.t[:], d_fnw.partition_broadcast(128), nwbc, [], [nwbc])
    for tg in range(NT):
        ACT(junk.t[:], hres_t[:, tg, :], AF.Square, [HT[tg]], [smn], accum_out=ss4[:, 0:1])
        rstd_of(rs4[:, 0:1], ss4[:, 0:1], DM)
        for half in range(2):
            ob = nxt(TTB, "ttb")
            hs = slice(half * 512, (half + 1) * 512)
            STT(ob.t[:], hres_t[:, tg, hs], rs4[:, 0:1], nwbc.t[:, hs], ALU.mult, ALU.mult,
                [HT[tg], smn, nwbc], [ob])
            finals.append(DMA("sync", d_out[tg * 128:(tg + 1) * 128, hs], ob.t[:], ob, [ob], []))
    finals.extend(dbg_ids)
    S.wait_all("sync", finals)
    print("n semaphores", len(S.sems), flush=True)
    S.schedule(reorder=REORDER)
    print("ops:", len(S.ops), "model makespan us:", S.makespan / 1e3, flush=True)
    S.emit()
    S.close()
    es.close()
    return nc


def pack_layer(inp, l):
    f = lambda a: np.ascontiguousarray(np.asarray(a, dtype=np.float32))
    pp = np.zeros((128, NPP), np.float32)
    pp[:, PP_GB:PP_GB + 2] = f(inp["gla_gate_b"][l]).reshape(2, 128).T
    pp[:, PP_MCW:PP_MCW + 32] = f(inp["ml_conv_w"][l]).reshape(4, 8, 128).transpose(2, 1, 0).reshape(128, 32)
    pp[:, PP_MCB:PP_MCB + 8] = f(inp["ml_conv_b"][l]).reshape(8, 128).T
    pp[:, PP_SCW:PP_SCW + 32] = f(inp["ssd_conv_w"][l]).reshape(4, 8, 128).transpose(2, 1, 0).reshape(128, 32)
    pp[:, PP_SCB:PP_SCB + 8] = f(inp["ssd_conv_b"][l]).reshape(8, 128).T
    pp[:, PP_LB:PP_LB + 8] = f(inp["hg_lb_logits"]).reshape(2, 4, 128).transpose(2, 1, 0).reshape(128, 8)
    pr = np.concatenate([f(inp["ml_i_b"][l]), f(inp["ml_f_b"][l]), f(inp["ssd_dt_bias"][l]), f(inp["ssd_A_log"][l]),
                         f(inp["ssd_D"][l]), f(inp["gla_norm_w"][l]), f(inp["ml_norm_w"][l]), f(inp["hg_norm_w"][l]),
                         f(inp["ssd_norm_w"][l])]).astype(np.float32)
    assert pr.shape[0] == NPR
    return pp, pr


def core_pm(q):
    pm = np.zeros((128, 9), np.float32)
    for j in range(3):
        pm[:, j] = 1.0 if j < q else 0.0
        pm[:, 3 + j] = 1.0 - pm[:, j]
        pm[:, 6 + j] = 1.0 if j == q - 1 else 0.0
    return pm


_NC_CACHE = {}


def pack_inputs(inputs):
    f = lambda a: np.ascontiguousarray(np.asarray(a, dtype=np.float32))
    x = f(inputs["x"])
    pps, prs = zip(*[pack_layer(inputs, l) for l in range(2)])
    shared = dict(w_in=f(inputs["w_in"]), w_out=f(inputs["w_out"]), nw=f(inputs["norm_w"]),
                  fnw=f(inputs["final_norm_w"]), pp=np.ascontiguousarray(np.stack(pps)),
                  pr=np.ascontiguousarray(np.stack(prs)), gw=f(inputs["gla_gate_w"]))
    in_maps = []
    for c in range(NCORES):
        b, q = c // 4, c % 4
        d = dict(shared)
        d["hin"] = np.ascontiguousarray(x[b, q * T:(q + 1) * T, :])
        d["halo"] = (np.zeros((3, DM), np.float32) if q == 0
                     else np.ascontiguousarray(x[b, q * T - 3:q * T, :]))
        d["pm"] = core_pm(q)
        in_maps.append(d)
    return in_maps


def kernel(**inputs):
    if "nc" not in _NC_CACHE:
        _NC_CACHE["nc"] = build()
    in_maps = pack_inputs(inputs)
    res = run_bass_kernel_spmd(_NC_CACHE["nc"], in_maps, core_ids=list(range(NCORES)))
    B = np.asarray(inputs["x"]).shape[0]
    out = np.zeros((B, 4 * T, DM), np.float32)
    for c in range(NCORES):
        out[c // 4, (c % 4) * T:(c % 4 + 1) * T, :] = np.asarray(res.results[c]["hout"], dtype=np.float32)
    return out
```

```python
import math
import numpy as np
from contextlib import ExitStack
import concourse.bass as bass
import concourse.mybir as mybir
from concourse.bass_utils import run_bass_kernel_spmd

F32 = mybir.dt.float32
BF16 = mybir.dt.bfloat16
AF = mybir.ActivationFunctionType
ALU = mybir.AluOpType

NCORES = 8
SYNC_LAT = 150.0
SELF_DIST = 1 << 30
T = 2048
NT = 16
ST = 512
TPS = 4
NS = 4
DM = 1024
KC = 8
EPS = 1e-6
DPROJ = 7712
NSTATE = 1808
NDEC = 18
NSUM = NSTATE + NDEC
OFF_A, OFF_B, OFF_C, OFF_D = 0, 256, 784, 1296
DEC_A, DEC_B, DEC_C, DEC_D = 0, 2, 6, 10
NPP = 90
NPR = 2080
PP_GB, PP_MCW, PP_MCB, PP_SCW, PP_SCB, PP_LB = 0, 2, 34, 42, 74, 82
PR_IB, PR_FB, PR_DTB, PR_ALOG, PR_D, PR_NWA, PR_NWB, PR_NWC, PR_NWD = 0, 4, 8, 16, 24, 32, 544, 1056, 1568


class Buf:
    __slots__ = ("t", "name", "w", "r", "sem", "excl")

    def __init__(self, t, name, excl=False):
        self.t = t
        self.name = name
        self.w = None
        self.r = []
        self.sem = None
        self.excl = excl


class Sched:
    def __init__(self, nc):
        self.nc = nc
        self.engs = []
        self.sems = {}
        self._ctx = []
        self.nself = {"tensor"}
        self.ops = []

    def add_engine(self, name):
        key = "e_" + name
        self.sems[key] = self._alloc(key)
        self.engs.append(name)

    def _alloc(self, key):
        cm = self.nc.semaphore(key)
        h = cm.__enter__()
        self._ctx.append(cm)
        return h

    def dma_sem(self, name):
        key = "d_" + name
        self.sems[key] = self._alloc(key)
        return key

    def _record(self, eng, fn, kind, sem, reads, writes, cost, lat):
        ex = [r for r in reads if r.excl]
        if ex:
            writes = list(writes) + [r for r in ex if r not in writes]
            reads = [r for r in reads if not r.excl]
        deps = set()
        for r in reads:
            if r.w is not None:
                deps.add(r.w)
        for w in writes:
            if w.w is not None:
                deps.add(w.w)
            deps.update(w.r)
        oid = len(self.ops)
        self.ops.append(dict(id=oid, eng=eng, fn=fn, kind=kind, sem=sem, deps=deps, cost=cost, lat=lat))
        for r in reads:
            r.r.append(oid)
        for w in writes:
            w.w = oid
            w.r = []
        return oid

    def op(self, eng, fn, reads=(), writes=(), cost=300):
        return self._record(eng, fn, "op", "e_" + eng, reads, writes, cost, cost)

    def dma(self, eng, fn, semkey, reads=(), writes=(), cost=100, lat=4000):
        return self._record(eng, fn, "dma", semkey, reads, writes, cost, lat)

    def coll(self, eng, fn, semkey, reads=(), writes=()):
        return self._record(eng, fn, "coll", semkey, reads, writes, 2000, 40000)

    def wait_all(self, eng, ids):
        oid = len(self.ops)
        self.ops.append(dict(id=oid, eng=eng, fn=None, kind="wait", sem=None, deps=set(ids), cost=10, lat=10))
        return oid

    def schedule(self, reorder=True):
        import heapq
        ops = self.ops
        n = len(ops)
        ndeps = [len(o["deps"]) for o in ops]
        users = [[] for _ in range(n)]
        for o in ops:
            for d in o["deps"]:
                users[d].append(o["id"])
        prio = [0.0] * n
        for o in reversed(ops):
            i = o["id"]
            m = 0.0
            for u in users[i]:
                if prio[u] > m:
                    m = prio[u]
            prio[i] = m + o["lat"] + SYNC_LAT
        finish = [0.0] * n
        ready_t = [0.0] * n
        efree = {e: 0.0 for e in self.engs}
        pending = {e: [] for e in self.engs}
        avail = {e: [] for e in self.engs}
        queues = {e: [] for e in self.engs}
        for o in ops:
            if ndeps[o["id"]] == 0:
                heapq.heappush(pending[o["eng"]], (0.0, o["id"]))
        done = 0
        if not reorder:
            for o in ops:
                queues[o["eng"]].append(o["id"])
            self.queues = queues
            self.makespan = 0
            return
        while done < n:
            best = None
            for e in self.engs:
                pe, av = pending[e], avail[e]
                while pe and pe[0][0] <= efree[e]:
                    _, i = heapq.heappop(pe)
                    heapq.heappush(av, (-prio[i], i))
                if av:
                    cand = (efree[e], av[0][1], e, True)
                elif pe:
                    cand = (pe[0][0], pe[0][1], e, False)
                else:
                    continue
                if best is None or cand[:2] < best[:2]:
                    best = cand
            assert best is not None, "scheduler deadlock"
            st, i, e, from_av = best
            if from_av:
                heapq.heappop(avail[e])
            else:
                heapq.heappop(pending[e])
            o = ops[i]
            efree[e] = st + o["cost"]
            finish[i] = st + o["lat"] + SYNC_LAT
            queues[e].append(i)
            done += 1
            for u in users[i]:
                ndeps[u] -= 1
                if finish[i] > ready_t[u]:
                    ready_t[u] = finish[i]
                if ndeps[u] == 0:
                    heapq.heappush(pending[ops[u]["eng"]], (ready_t[u], u))
        self.queues = queues
        self.makespan = max(finish)

    def emit(self):
        nc = self.nc
        ops = self.ops
        semval = {}
        cnt = {}
        for e in self.engs:
            for i in self.queues[e]:
                o = ops[i]
                if o["kind"] == "op":
                    cnt[o["sem"]] = cnt.get(o["sem"], 0) + 1
                    semval[i] = (o["sem"], cnt[o["sem"]], 1)
        for o in ops:
            if o["kind"] in ("dma", "coll"):
                inc = 16 if o["kind"] == "dma" else 1
                cnt[o["sem"]] = cnt.get(o["sem"], 0) + inc
                semval[o["id"]] = (o["sem"], cnt[o["sem"]], inc)
        with nc.Block() as block:
            for e in self.engs:
                deco = getattr(block, e)
                q = self.queues[e]
                own = "e_" + e

                def body(h, q=q, e=e, own=own):
                    seen = {}
                    for i in q:
                        o = ops[i]
                        need = {}
                        for d in o["deps"]:
                            k, v, _ = semval[d]
                            if k == own and e in self.nself:
                                continue
                            if k == own and ops[d]["kind"] == "op" and i in semval and semval[i][0] == own \
                                    and semval[i][1] - v >= SELF_DIST:
                                continue
                            if need.get(k, 0) < v:
                                need[k] = v
                        for k, v in need.items():
                            if seen.get(k, 0) >= v:
                                continue
                            seen[k] = v
                            h.wait_ge(self.sems[k], v)
                        if o["fn"] is not None:
                            k, v, inc = semval[i]
                            o["fn"](h).then_inc(self.sems[k], inc)
                deco(body)

    def close(self):
        for cm in reversed(self._ctx):
            cm.__exit__(None, None, None)


def build(debug=False, MIXERS="ABCD", STOP=99, NLAYERS=2, REORDER=True):
    nc = bass.Bass("TRN2", target_bir_lowering=False)
    es = ExitStack()
    S = Sched(nc)
    for n in ("sync", "gpsimd", "tensor", "vector", "scalar"):
        S.add_engine(n)

    def dram(name, shape, dt=F32, kind="ExternalInput"):
        return nc.dram_tensor(name, list(shape), dt, kind=kind).ap()

    layer = 0
    P2 = False
    last = False

    d_hin = dram("hin", [T, DM])
    d_halo = dram("halo", [3, DM])
    d_win_all = dram("w_in", [2, DM, DPROJ])
    d_wout_all = dram("w_out", [2, 2048, DM])
    d_nw_all = dram("nw", [2, DM])
    d_fnw = dram("fnw", [DM])
    d_pp_all = dram("pp", [2, 128, NPP])
    d_pr_all = dram("pr", [2, NPR])
    d_gw_all = dram("gw", [2, 16, 256])
    d_pm = dram("pm", [128, 9])
    d_out = dram("hout", [T, DM], kind="ExternalOutput")
    if debug:
        d_dbg = dram("dbg", [T, 2048], BF16, kind="ExternalOutput")
    d_sloc = [nc.dram_tensor(f"sloc{l}", [128, NSUM], F32).ap() for l in range(2)]
    d_sgat = [nc.dram_tensor(f"sgat{l}", [4 * 128, NSUM], F32).ap() for l in range(2)]
    d_hloc = nc.dram_tensor("hloc", [3, DM], F32).ap()
    d_hgat = nc.dram_tensor("hgat", [12, DM], F32).ap()
    GROUPS = [[0, 1, 2, 3], [4, 5, 6, 7]]

    cnt = [0]
    dbg_ids = []

    def sb(shape, dt, name=None):
        cnt[0] += 1
        name = "s_" + (name or f"sb{cnt[0]}")
        t = es.enter_context(nc.sbuf_tensor(name, list(shape), dt))
        return Buf(t, name)

    def ps(shape, dt, name):
        return es.enter_context(nc.psum_tensor(name, list(shape), dt))

    def withsem(b):
        b.sem = S.dma_sem(b.name)
        return b

    def fsz(ap):
        n = 1
        for (_, c) in list(ap.ap)[1:]:
            n *= c
        return n

    def is_psum(ap):
        return "PSum" in type(ap.tensor).__name__

    def ACT(out, in_, func, R, W, **kw):
        c = 200 + 0.85 * fsz(out)
        return S.op("scalar", lambda e: e.activation(out=out, in_=in_, func=func, **kw), R, W, cost=c)

    def _dve_cost(out, ins):
        c = 70 + 1.05 * fsz(out)
        if any(is_psum(a) for a in ins):
            c += 60
        return c

    def TT(out, in0, in1, op, R, W, eng="vector"):
        return S.op(eng, lambda e: e.tensor_tensor(out=out, in0=in0, in1=in1, op=op), R, W,
                    cost=_dve_cost(out, [in0, in1]))

    def TS(out, in0, s1, s2, op0, op1, R, W, eng="vector"):
        c = _dve_cost(out, [in0])
        if s2 is None:
            return S.op(eng, lambda e: e.tensor_scalar(out=out, in0=in0, scalar1=s1, scalar2=None, op0=op0), R, W, cost=c)
        return S.op(eng, lambda e: e.tensor_scalar(out=out, in0=in0, scalar1=s1, scalar2=s2, op0=op0, op1=op1), R, W, cost=c)

    def STT(out, in0, scalar, in1, op0, op1, R, W, eng="vector"):
        return S.op(eng, lambda e: e.scalar_tensor_tensor(out=out, in0=in0, scalar=scalar, in1=in1, op0=op0, op1=op1), R, W,
                    cost=_dve_cost(out, [in0, in1]))

    def CP(out, in_, R, W, eng="vector"):
        return S.op(eng, lambda e: e.tensor_copy(out=out, in_=in_), R, W, cost=_dve_cost(out, [in_]))

    def MSET(ap, val, W, eng="gpsimd"):
        return S.op(eng, lambda e: e.memset(ap, val), (), W, cost=200 + fsz(ap))

    def MM(out, lhsT, rhs, start, stop, R, W):
        n = fsz(rhs)
        f32 = "float32" in str(rhs.tensor.dtype)
        c = (64 + 1.7 * n) if f32 else (64 + 0.45 * n)
        return S.op("tensor", lambda e: e.matmul(out, lhsT=lhsT, rhs=rhs, start=start, stop=stop), R, W, cost=c)

    def TR(out, in_, ident, R, W):
        return S.op("tensor", lambda e: e.transpose(out, in_, ident), R, W, cost=150)

    def DMA(q, out, in_, buf_sem, R, W):
        nbytes = fsz(out) * 128 * 4
        issue = 1000 if q == "gpsimd" else 80
        return S.dma(q, lambda e: e.dma_start(out=out, in_=in_), buf_sem.sem, R, W, cost=issue,
                     lat=issue + 2000 + nbytes / 120.0)

    identf = sb([128, 128], F32, "identf")
    identb = sb([128, 128], BF16, "identb")
    trif = sb([128, 128], F32, "trif")
    suf = sb([128, 128], F32, "suf")
    negb_ = sb([128, 128], BF16, "negmask")
    onesf = sb([128, 128], F32, "onesf")
    rst = sb([128, 512], F32, "rst")
    MSET(onesf.t[:], 1.0, [onesf])
    MSET(identf.t[:], 1.0, [identf])
    S.op("gpsimd", lambda e: e.affine_select(out=identf.t[:], in_=identf.t[:], pattern=[[-1, 128]],
                                             compare_op=ALU.is_equal, fill=0.0, base=0, channel_multiplier=1),
         [identf], [identf])
    CP(identb.t[:], identf.t[:], [identf], [identb])
    MSET(trif.t[:], 1.0, [trif])
    S.op("gpsimd", lambda e: e.affine_select(out=trif.t[:], in_=trif.t[:], pattern=[[1, 128]],
                                             compare_op=ALU.is_ge, fill=0.0, base=0, channel_multiplier=-1),
         [trif], [trif])
    TS(suf.t[:], trif.t[:], -1.0, 1.0, ALU.mult, ALU.add, [trif], [suf])
    TS(negb_.t[:], suf.t[:], -30000.0, None, ALU.mult, None, [suf], [negb_])
    MSET(rst.t[:], 1.0, [rst])
    rst3 = rst.t[:, :].rearrange("p (c k) -> p c k", k=64)
    MSET(rst3[:, :, 0:1], 0.0, [rst])

    hres_t = es.enter_context(nc.sbuf_tensor("hres", [128, NT, DM], F32))
    HT = [withsem(Buf(hres_t, f"ht{t}")) for t in range(NT)]
    nwbc = withsem(sb([128, DM], F32, "nwbc"))
    uT = sb([128, KC, 515], BF16, "uT")
    ubf = sb([128, DM], BF16, "ubf")
    junk = sb([128, DM], BF16, "junk")
    NSLOT = 3
    slots = [withsem(sb([128, 4096], BF16, f"wslot{i}")) for i in range(NSLOT)]
    wsm = withsem(sb([128, KC, 32], BF16, "wsm"))
    pp = withsem(sb([128, NPP], F32, "pp"))
    pr = withsem(sb([128, NPR], F32, "pr"))
    gwf = withsem(sb([16, 256], F32, "gwf"))
    gwb = sb([16, 256], BF16, "gwb")
    grt = sb([16, 512], BF16, "grt")
    pm = withsem(sb([128, 9], F32, "pm"))
    stf_t = es.enter_context(nc.sbuf_tensor("stf", [128, NSUM], F32))
    stb_t = es.enter_context(nc.sbuf_tensor("stb", [128, NSTATE], BF16))
    STF = {(m, h): Buf(stf_t, "stf%s%d" % (m, h)) for m in "ABCD" for h in range(4)}
    STB = {(m, h): Buf(stb_t, "stb%s%d" % (m, h)) for m in "ABCD" for h in range(4)}
    STDEC = Buf(stf_t, "stdec")
    stsem = withsem(Buf(stf_t, "stout"))
    halo_raw = sb([128, 16, 3], F32, "halo_raw")
    BFB = [sb([128, TPS, 528], BF16, f"bfb{i}") for i in range(6)]
    FB = [sb([128, 515], F32, f"fb{i}") for i in range(6)]
    TTB = [sb([128, 512], F32, f"ttb{i}") for i in range(4)]
    SMF = [sb([128, 128], F32, f"smf{i}") for i in range(4)]
    SMB = [sb([128, 128], BF16, f"smb{i}") for i in range(4)]
    VDB = [sb([128, 512], BF16, f"vdb{i}") for i in range(2)]
    MIXB = [sb([128, 512], BF16, f"mix{i}") for i in range(2)]
    mixT = sb([128, 4, ST], BF16, "mixT")
    sml = sb([128, 512], F32, "sml")
    smp = Buf(sml.t, "smp")
    smn = Buf(sml.t, "smn")
    smd = Buf(sml.t, "smd")
    SML = {}
    _smo = [0]

    def small(name, n):
        o = _smo[0]
        _smo[0] += n
        assert _smo[0] <= 512
        SML[name] = (o, n)
        return sml.t[:, o:o + n]

    gates_t = es.enter_context(nc.sbuf_tensor("gates", [128, 512], F32))
    GATES = Buf(gates_t, "gatesB")
    GATESD = Buf(gates_t, "gatesD")
    egl_t = es.enter_context(nc.sbuf_tensor("egl", [128, 4, 8], F32))
    EGLB = [Buf(egl_t, "egl%d" % i) for i in range(4)]
    halo_h = withsem(sb([128, DM], F32, "halo_h"))
    for i in range(4):
        withsem(TTB[i])

    pj_t = [ps([128, 512], F32, f"pj{i}") for i in range(2)]
    PJ = [Buf(t, f"pj{i}", excl=True) for i, t in enumerate(pj_t)]
    ptu_t = ps([128, 1024], BF16, "ptu")
    PTU = Buf(ptu_t, "ptu", excl=True)
    pts_t = ps([128, 1024], BF16, "pts")
    PTSB = Buf(pts_t, "pts", excl=True)
    PTS = []
    for i in range(8):
        PTS.append((PTSB, i * 128))
        PTS.append((PTU, i * 128))
    pf_t = [ps([128, 512], F32, f"pf{i}") for i in range(2)]
    PFB = [Buf(pf_t[i], f"pf{i}", excl=True) for i in range(2)]
    PF = [(PFB[i % 2], (i // 2) * 128) for i in range(8)]
    pw_t = [ps([128, 512], F32, f"pw{i}") for i in range(2)]
    PWB = [Buf(pw_t[i], f"pw{i}", excl=True) for i in range(2)]
    PW = [(PWB[i % 2], (i // 2) * 256) for i in range(4)]
    rot = {}

    def nxt(pool, key):
        i = rot.get(key, 0)
        rot[key] = i + 1
        return pool[i % len(pool)]

    def pjn():
        return nxt(PJ, "pj")

    def pfn():
        b, o = nxt(PF, "pf")
        return b, b.t[:, o:o + 128]

    def pwn():
        b, o = nxt(PW, "pw")
        return b, b.t[:, o:o + 256]

    def ptsn():
        b, o = nxt(PTS, "pts")
        return b, b.t[:, o:o + 128]

    def fbn():
        return nxt(FB, "fb")

    def slot3(s):
        return s.t[:, :].rearrange("p (k c) -> p k c", k=KC)

    def slot_wo(s):
        return s.t[:, :].rearrange("p (k c) -> p k c", k=4)

    win3 = None
    wout3 = None

    wcache = {}
    slot_hw = {sl.name: withsem(Buf(None, sl.name + "_hw")) for sl in slots}
    d_wscr = nc.dram_tensor("wscr", [48, 128, 4096], BF16).ap()

    def _load_cached(key, slot_view, src_ap, n):
        s = nxt(slots, "slot")
        dst = slot_view(s)
        if key not in wcache:
            scr = d_wscr[len(wcache)]
            sbuf = withsem(Buf(None, "wscr%d" % len(wcache)))
            wcache[key] = (scr, sbuf)
            DMA("gpsimd", dst, src_ap, s, [], [s])
            DMA("sync", scr, s.t[:, :], sbuf, [s], [sbuf])
        else:
            scr, sbuf = wcache[key]
            DMA("sync", s.t[:, :], scr, slot_hw[s.name], [sbuf], [s])
        return s

    def load_w(c0, c1):
        n = c1 - c0
        return _load_cached((layer, "i", c0, c1), lambda s: slot3(s)[:, :, 0:n], win3[:, :, c0:c1], n)

    def load_wo(m):
        return _load_cached((layer, "o", m), lambda s: slot_wo(s)[:, :, :], wout3[:, m * 4:(m + 1) * 4, :], 4096)

    DMA("sync", pm.t[:], d_pm, pm, [], [pm])
    for t in range(NT):
        DMA("sync", hres_t[:, t, :], d_hin[t * 128:(t + 1) * 128, :], HT[t], [], [HT[t]])
    MSET(halo_h.t[:], 0.0, [halo_h])
    DMA("sync", halo_h.t[0:3, :], d_halo, halo_h, [], [halo_h])

    negb = small("negb", 2)
    ib2 = small("ib2", 4)
    aneg = small("aneg", 8)
    lbv = small("lbv", 4)
    omlb = small("omlb", 4)
    ss4 = small("ss4", 4)
    rs4 = small("rs4", 4)
    rr4 = small("rr4", 4)
    ssn = small("ssn", 2)
    rsn = small("rsn", 2)
    decp = small("decp", NDEC)
    acs_s = small("acs", 8)
    ee_s = small("ee", 8)
    etot_s = small("etot", 8)
    dec_s = small("dec", 8)
    dd_s = small("dd", 8)
    allst = list(STF.values()) + [STDEC]
    SLOC = [withsem(Buf(None, f"sloc{l}")) for l in range(2)]
    SGAT = [withsem(Buf(None, f"sgat{l}")) for l in range(2)]
    HLOC = withsem(Buf(None, "hloc"))
    HGAT = withsem(Buf(None, "hgat"))

    def load_layer_params(l):
        DMA("sync", nwbc.t[:], d_nw_all[l].partition_broadcast(128), nwbc, [], [nwbc])
        DMA("sync", pp.t[:], d_pp_all[l], pp, [], [pp])
        DMA("sync", pr.t[:], d_pr_all[l].partition_broadcast(128), pr, [], [pr])
        DMA("sync", gwf.t[:], d_gw_all[l], gwf, [], [gwf])
        CP(gwb.t[:], gwf.t[:], [gwf], [gwb])
        DMA("gpsimd", wsm.t[:, :, 0:16], win3[:, :, 1024:1040], wsm, [], [wsm])
        DMA("gpsimd", wsm.t[:, :, 16:24], win3[:, :, 3088:3096], wsm, [], [wsm])
        DMA("gpsimd", wsm.t[:, :, 24:32], win3[:, :, 7192:7200], wsm, [], [wsm])
        TS(negb, pp.t[:, PP_GB:PP_GB + 2], -1.0, None, ALU.mult, None, [pp], [smp])
        TS(ib2, pr.t[:, PR_IB:PR_IB + 4], math.log(128 ** -0.5), None, ALU.add, None, [pr], [smp])
        ACT(aneg, pr.t[:, PR_ALOG:PR_ALOG + 8], AF.Exp, [pr], [smp])
        TS(aneg, aneg, -1.0, None, ALU.mult, None, [smp], [smp])
        lg3 = pp.t[:, PP_LB:PP_LB + 8].rearrange("p (b l) -> p b l", l=2)
        if l == 0:
            MSET(lbv, 0.0, [smp], eng="vector")
        else:
            TT(lbv, lg3[:, :, 1], lg3[:, :, 0], ALU.subtract, [pp], [smp])
            ACT(lbv, lbv, AF.Sigmoid, [smp], [smp])
        TS(omlb, lbv, -1.0, 1.0, ALU.mult, ALU.add, [smp], [smp])

    cgroups = []
    for blk in range(2):
        cgroups.append((OFF_A + blk * 128, 128, DEC_A + blk))
    for h in range(4):
        cgroups.append((OFF_B + h * 132, 132, DEC_B + h))
    for h in range(4):
        cgroups.append((OFF_C + h * 128, 128, DEC_C + h))
    for h in range(8):
        cgroups.append((OFF_D + h * 64, 64, DEC_D + h))

    def init_pass():
        MSET(stf_t[:, 0:NSTATE], 0.0, allst, eng="vector")
        if not P2:
            MSET(stf_t[:, NSTATE:NSUM], 1.0, allst, eng="vector")
        else:
            for j in range(3):
                DMA("sync", cmbv[:, 0:NSUM], d_sgat[layer][j * 128:(j + 1) * 128, :], cmb, [SGAT[layer]], [cmb])
                TS(decp, cmbv[:, NSTATE:NSUM], pm.t[:, j:j + 1], pm.t[:, 3 + j:4 + j], ALU.mult, ALU.add,
                   [cmb, pm], [smd])
                TS(cmbv[:, 0:NSTATE], cmbv[:, 0:NSTATE], pm.t[:, j:j + 1], None, ALU.mult, None, [cmb, pm], [cmb])
                for (o, n, g) in cgroups:
                    STT(stf_t[:, o:o + n], stf_t[:, o:o + n], decp[:, g:g + 1], cmbv[:, o:o + n], ALU.mult, ALU.add,
                        allst + [cmb, smd], allst)
        CP(stb_t[:, :], stf_t[:, 0:NSTATE], allst, list(STB.values()))

    def end_p1():
        l = layer
        DMA("sync", d_sloc[l], stf_t[:, :], SLOC[l], allst, [SLOC[l]])
        S.coll("gpsimd", lambda e: e.collective_compute("AllGather", ALU.bypass, replica_groups=GROUPS,
                                                        ins=[d_sloc[l]], outs=[d_sgat[l]]),
               SGAT[l].sem, [SLOC[l]], [SGAT[l]])

    def exchange_halo():
        DMA("sync", d_hloc, hres_t[125:128, NT - 1, :], HLOC, [HT[NT - 1]], [HLOC])
        S.coll("gpsimd", lambda e: e.collective_compute("AllGather", ALU.bypass, replica_groups=GROUPS,
                                                        ins=[d_hloc], outs=[d_hgat]),
               HGAT.sem, [HLOC], [HGAT])
        MSET(halo_h.t[:], 0.0, [halo_h])
        for j in range(3):
            for half in range(2):
                tb = nxt(TTB, "ttb")
                hs = slice(half * 512, (half + 1) * 512)
                DMA("sync", tb.t[0:3, :], d_hgat[j * 3:(j + 1) * 3, hs], tb, [HGAT], [tb])
                STT(halo_h.t[0:3, hs], tb.t[0:3, :], pm.t[0:3, 6 + j:7 + j], halo_h.t[0:3, hs], ALU.mult, ALU.add,
                    [tb, pm, halo_h], [halo_h])

    def tokc(t):
        return slice(3 + t * 128, 3 + (t + 1) * 128)

    def rstd_of(out_ap, ss_ap, n):
        TS(out_ap, ss_ap, 1.0 / n, EPS, ALU.mult, ALU.add, [smn], [smn])
        ACT(out_ap, out_ap, AF.Ln, [smn], [smn])
        ACT(out_ap, out_ap, AF.Exp, [smn], [smn], scale=-0.5)

    def make_uT_tile(hbuf, h_ap, dst_cols, ncol):
        ACT(junk.t[:], h_ap, AF.Square, [hbuf], [smn], accum_out=ss4[:, 0:1])
        rstd_of(rs4[:, 0:1], ss4[:, 0:1], DM)
        STT(ubf.t[:], h_ap, rs4[:, 0:1], nwbc.t[:], ALU.mult, ALU.mult, [hbuf, smn, nwbc], [ubf])
        for k in range(KC):
            TR(ptu_t[:, k * 128:(k + 1) * 128], ubf.t[:, k * 128:(k + 1) * 128], identb.t[:], [ubf, identb], [PTU])
        src = ptu_t[:, :].rearrange("p (k c) -> p k c", k=KC)[:, :, 0:ncol]
        CP(uT.t[:, :, dst_cols], src, [PTU], [uT])

    def proj_fm(slot, off, M, cols=slice(3, 515), n=512):
        pj = pjn()
        v = slot3(slot)
        for k in range(KC):
            MM(pj.t[0:M, 0:n], v[:, k, off:off + M], uT.t[:, k, cols], k == 0, k == KC - 1, [slot, uT], [pj])
        return pj

    def proj_tm(slot, ncols, t):
        pj = pjn()
        v = slot3(slot)
        for k in range(KC):
            MM(pj.t[:, 0:ncols], uT.t[:, k, tokc(t)], v[:, k, 0:ncols], k == 0, k == KC - 1, [slot, uT], [pj])
        return pj

    def proj_small(c0, c1, t):
        b, ap = pfn()
        n = c1 - c0
        for k in range(KC):
            MM(ap[:, 0:n], uT.t[:, k, tokc(t)], wsm.t[:, k, c0:c1], k == 0, k == KC - 1, [wsm, uT], [b])
        return b, ap

    def conv_block(s_idx, slot, off, hidx, wcol, bcol, out_ap, out_buf):
        pj = proj_fm(slot, off, 128)
        raw = fbn()
        if s_idx == 0:
            b, ap = pfn()
            v = slot3(slot)
            for k in range(KC):
                MM(ap[:, 0:3], v[:, k, off:off + 128], uT.t[:, k, 0:3], k == 0, k == KC - 1, [slot, uT], [b])
            CP(raw.t[:, 0:3], ap[:, 0:3], [b], [raw])
        else:
            CP(raw.t[:, 0:3], halo_raw.t[:, hidx, :], [halo_raw], [raw])
        ACT(raw.t[:, 3:515], pj.t[:, :], AF.Copy, [pj], [raw])
        CP(halo_raw.t[:, hidx, :], raw.t[:, 512:515], [raw], [halo_raw])
        acc = fbn()
        w = lambda k: pp.t[:, wcol + k:wcol + k + 1]
        ACT(acc.t[:, 0:512], raw.t[:, 0:512], AF.Identity, [raw, pp], [acc], scale=w(0), bias=pp.t[:, bcol:bcol + 1])
        for k in (1, 2, 3):
            STT(acc.t[:, 0:512], raw.t[:, k:k + 512], w(k), acc.t[:, 0:512], ALU.mult, ALU.add, [raw, pp, acc], [acc])
        ACT(out_ap, acc.t[:, 0:512], AF.Silu, [acc], [out_buf])

    def finish_mixer(s_idx, m, t, mix):
        if debug:
            tg = s_idx * TPS + t
            dbg_ids.append(S.dma("sync", lambda e: e.dma_start(out=d_dbg[tg * 128:(tg + 1) * 128, m * 512:(m + 1) * 512], in_=mix.t[:]),
                                 mix.sem, [mix], []))
        for b4 in range(4):
            pb, pap = ptsn()
            TR(pap, mix.t[:, b4 * 128:(b4 + 1) * 128], identb.t[:], [mix, identb], [pb])
            ACT(mixT.t[:, b4, t * 128:(t + 1) * 128], pap, AF.Copy, [pb], [mixT])

    def out_proj(s_idx, m):
        s = load_wo(m)
        v = slot_wo(s)
        for t in range(TPS):
            tg = s_idx * TPS + t
            for half in range(2):
                pj = pjn()
                for kc in range(4):
                    MM(pj.t[:, :], mixT.t[:, kc, t * 128:(t + 1) * 128], v[:, kc, half * 512:(half + 1) * 512],
                       kc == 0, kc == 3, [s, mixT], [pj])
                hap = hres_t[:, tg, half * 512:(half + 1) * 512]
                TT(hap, hap, pj.t[:, :], ALU.add, [HT[tg], pj], [HT[tg]])

    def scalar_decay(t, nh, a_ap, a_buf, groups, ybuf, yap, dvh, pad):
        bpa, pa = pfn()
        MM(pa[:, 0:nh], trif.t[:], a_ap, True, True, [trif, a_buf], [bpa])
        MM(pa[:, 64:64 + nh], onesf.t[:], a_ap, True, True, [onesf, a_buf], [bpa])
        CP(acs_s[:, 0:nh], pa[:, 0:nh], [bpa], [smd])
        if P2:
            ACT(ee_s[:, 0:nh], pa[:, 0:nh], AF.Exp, [bpa], [smd])
        ACT(etot_s[:, 0:nh], pa[:, 64:64 + nh], AF.Exp, [bpa], [smd])
        TT(dd_s[:, 0:nh], pa[:, 64:64 + nh], acs_s[:, 0:nh], ALU.subtract, [bpa, smd], [smd])
        ACT(dec_s[:, 0:nh], dd_s[:, 0:nh], AF.Exp, [smd], [smd])
        for g in groups:
            ncg = g["ncg"]
            heads = g["heads"]
            if P2:
                bqk, pqk = pfn()
                MM(pqk, g["kT"], g["qT"], True, True, g["qkbufs"], [bqk])
                byo, pyo = pwn()
                MM(pyo[:, 0:ncg], g["qT"], g["sb"][:, 0:ncg], True, True, g["qkbufs"] + [g["stb"]], [byo])
                byd, pyd = pwn()
            vd = nxt(VDB, "vdb")
            for j, h in enumerate(heads):
                cs = slice(j * pad, j * pad + dvh)
                if P2:
                    lh = nxt(SMF, "smf")
                    ACT(lh.t[:], suf.t[:], AF.Identity, [suf, a_buf], [lh], scale=a_ap[:, h:h + 1])
                    bsg, psg = pfn()
                    MM(psg, lh.t[:], trif.t[:], True, False, [lh, trif], [bsg])
                    MM(psg, identb.t[:], negb_.t[:], False, True, [identb, negb_], [bsg])
                    esg = nxt(SMF, "smf")
                    ACT(esg.t[:], psg, AF.Exp, [bsg], [esg])
                    wb = nxt(SMB, "smb")
                    TT(wb.t[:], pqk, esg.t[:], ALU.mult, [bqk, esg], [wb])
                    MM(pyd[:, cs], wb.t[:], g["v"][:, cs], True, True, [wb, g["vbuf"]], [byd])
                if len(heads) == 1:
                    TS(vd.t[:, cs], g["v"][:, cs], dec_s[:, h:h + 1], None, ALU.mult, None, [g["vbuf"], smd], [vd])
            if len(heads) > 1:
                nhh = len(heads)
                so = SML["dec"][0] + heads[0]
                decb = bass.AP(sml.t, so, [[512, 128], [1, nhh], [0, dvh]])
                TT(vd.t[:, 0:ncg].rearrange("p (h c) -> p h c", c=dvh),
                   g["v"][:, 0:ncg].rearrange("p (h c) -> p h c", c=dvh), decb, ALU.mult, [g["vbuf"], smd], [vd])
            bu, pu = pwn()
            MM(pu[:, 0:ncg], g["ktok"], vd.t[:, 0:ncg], True, True, g["ktokbufs"] + [vd], [bu])
            nhh = len(heads)
            h0 = heads[0]
            if P2:
                if nhh == 1:
                    for j, h in enumerate(heads):
                        cs = slice(j * pad, j * pad + dvh)
                        ACT(yap(h), pyd[:, cs], AF.Copy, [byd], [ybuf])
                        STT(yap(h), pyo[:, cs], ee_s[:, h:h + 1], yap(h), ALU.mult, ALU.add, [byo, smd, ybuf], [ybuf])
                else:
                    ycols = ybuf.t[:, h0 * dvh:h0 * dvh + ncg]
                    ACT(ycols, pyd[:, 0:ncg], AF.Copy, [byd], [ybuf])
                    eeb = bass.AP(sml.t, SML["ee"][0] + h0, [[512, 128], [1, nhh], [0, dvh]])
                    tmpb = fbn()
                    TT(tmpb.t[:, 0:ncg].rearrange("p (h c) -> p h c", c=dvh),
                       pyo[:, 0:ncg].rearrange("p (h c) -> p h c", c=dvh), eeb, ALU.mult, [byo, smd], [tmpb])
                    TT(ycols, ycols, tmpb.t[:, 0:ncg], ALU.add, [ybuf, tmpb], [ybuf])
            if nhh == 1:
                h = h0
                cs = slice(0, dvh)
                STT(g["sf"][:, cs], g["sf"][:, cs], etot_s[:, h:h + 1], pu[:, cs], ALU.mult, ALU.add,
                    [g["stf"], smd, bu], [g["stf"]])
            else:
                etb = bass.AP(sml.t, SML["etot"][0] + h0, [[512, 128], [1, nhh], [0, dvh]])
                sf3 = g["sf"][:, 0:ncg].rearrange("p (h c) -> p h c", c=dvh)
                TT(sf3, sf3, etb, ALU.mult, [g["stf"], smd], [g["stf"]])
                TT(g["sf"][:, 0:ncg], g["sf"][:, 0:ncg], pu[:, 0:ncg], ALU.add, [g["stf"], bu], [g["stf"]])
            if not P2:
                dc = stf_t[:, NSTATE + g["decoff"]:NSTATE + g["decoff"] + nhh]
                TT(dc, dc, etot_s[:, h0:h0 + nhh], ALU.mult, [STDEC, smd], [STDEC])
            ACT(g["sb"], g["sf"], AF.Copy, [g["stf"]], [g["stb"]])


    xst = withsem(sb([128, 4, 512], F32, "xst"))
    cmb = xst
    cmbv = xst.t[:, :, :].rearrange("p a b -> p (a b)")
    yb = sb([128, 4, 132], F32, "yb")
    VTD = [sb([128, 512], BF16, f"vtd{i}") for i in range(2)]
    DBGS = [S.dma_sem(f"dbg{i}") for i in range(2)] if debug else None
    for i, mb in enumerate(MIXB):
        mb.sem = DBGS[i] if debug else None

    def recur_vd(s_idx, m, hpb, dk, qTb, kTb, kdtokb, vTb, zGb, soff, doff, nwoff, midx):
        for t in range(TPS):
            pouts = []
            for h in range(4):
                blk = h // hpb
                r0 = (h % hpb) * dk
                rows = slice(r0, r0 + dk)
                hc = slice(h * 128, (h + 1) * 128)
                scol = soff + blk * 128
                sf = stf_t[rows, scol:scol + 128]
                sbf = stb_t[rows, scol:scol + 128]
                tcs = slice(t * 128, (t + 1) * 128)
                if P2:
                    bsc, psc = pfn()
                    MM(psc, kTb.t[rows, blk, tcs], qTb.t[rows, blk, tcs], True, True, [kTb, qTb], [bsc])
                    sm = nxt(SMB, "smb")
                    TT(sm.t[:], psc, trif.t[:], ALU.mult, [bsc, trif], [sm])
                    bo, po = pwn()
                for c in range(2):
                    tr = slice(c * 64, (c + 1) * 64)
                    if P2:
                        MM(po[tr, 0:128], sm.t[tr, tr], vTb.t[tr, t, hc], True, False, [sm, vTb], [bo])
                        MM(po[tr, 0:128], qTb.t[rows, blk, t * 128 + c * 64:t * 128 + (c + 1) * 64], sbf,
                           False, True, [qTb, STB[(m, h)]], [bo])
                    bu, pu = pfn()
                    MM(pu[rows, 0:128], kdtokb.t[tr, t, blk * 128 + r0:blk * 128 + r0 + dk], vTb.t[tr, t, hc],
                       True, True, [kdtokb, vTb], [bu])
                    eg = egl_t[rows, blk, t * 2 + c:t * 2 + c + 1]
                    STT(sf, sf, eg, pu[rows, 0:128], ALU.mult, ALU.add, [STF[(m, h)], EGLB[blk], bu], [STF[(m, h)]])
                    ACT(sbf, sf, AF.Copy, [STF[(m, h)]], [STB[(m, h)]])
                    if not P2:
                        dc = stf_t[rows, NSTATE + doff + blk:NSTATE + doff + blk + 1]
                        TT(dc, dc, eg, ALU.mult, [STDEC, EGLB[blk]], [STDEC])
                if P2:
                    ACT(junk.t[:, 0:128], po[:, 0:128], AF.Square, [bo], [smn], accum_out=ss4[:, h:h + 1])
                    pouts.append((bo, po))
            if P2:
                rstd_of(rs4[:, 0:4], ss4[:, 0:4], 128)
                nwz = nxt(TTB, "ttb")
                TT(nwz.t[:], pr.t[:, nwoff:nwoff + 512], zGb.t[:, t, 0:512], ALU.mult, [pr, zGb], [nwz])
                mix = nxt(MIXB, "mixb")
                for h in range(4):
                    hc = slice(h * 128, (h + 1) * 128)
                    STT(mix.t[:, hc], pouts[h][1][:, 0:128], rs4[:, h:h + 1], nwz.t[:, hc], ALU.mult, ALU.mult,
                        [pouts[h][0], smn, nwz], [mix])
                finish_mixer(s_idx, midx, t, mix)
        if P2:
            out_proj(s_idx, midx)

    def vd_front(blk, kf, csb, sG, qslot, qoff, qscale, qTb, kTb, kdTb):
        ACT(egl_t[:, blk, :], csb.t[:, 63:512:64], AF.Exp, [csb], [EGLB[blk]], scale=sG)
        tmp = fbn()
        if STOP == 31:
            return
        if P2:
            qf = fbn()
            pq = proj_fm(qslot, qoff, 128)
            ACT(qf.t[:, 0:512], pq.t[:, :], AF.Identity, [pq], [qf], scale=qscale)
            if STOP == 32:
                return
            ACT(tmp.t[:, 0:512], csb.t[:, 0:512], AF.Exp, [csb], [tmp], scale=sG)
            TT(qTb.t[:, blk, 0:512], qf.t[:, 0:512], tmp.t[:, 0:512], ALU.mult, [qf, tmp], [qTb])
            if STOP == 33:
                return
            ACT(tmp.t[:, 0:512], csb.t[:, 0:512], AF.Exp, [csb], [tmp], scale=-sG)
            TT(kTb.t[:, blk, 0:512], kf.t[:, 0:512], tmp.t[:, 0:512], ALU.mult, [kf, tmp], [kTb])
        if STOP == 34:
            return
        glb = bass.AP(csb.t, 63, [[515, 128], [64, 8], [0, 64]])
        cs3 = csb.t[:, 0:512].rearrange("p (c k) -> p c k", k=64)
        TT(tmp.t[:, 0:512].rearrange("p (c k) -> p c k", k=64), glb, cs3, ALU.subtract, [csb], [tmp])
        if STOP == 35:
            return
        ACT(tmp.t[:, 0:512], tmp.t[:, 0:512], AF.Exp, [tmp], [tmp], scale=sG)
        if STOP == 36:
            return
        TT(kdTb.t[:, blk, 0:512], kf.t[:, 0:512], tmp.t[:, 0:512], ALU.mult, [kf, tmp], [kdTb])
        if STOP == 37:
            return

    def scan(csb, src):
        S.op("vector", lambda e: e.tensor_tensor_scan(out=csb.t[:, 0:512], data0=rst.t[:, :], data1=src.t[:, 0:512],
                                                      initial=0.0, op0=ALU.mult, op1=ALU.add), [rst, src], [csb],
             cost=1250)

    def vz_and_kdtok(nblk, vslot_cols, zslot_cols, vTb, zGb, kdTb, kdtokb):
        s_v = load_w(*vslot_cols)
        for t in range(TPS):
            pv = proj_tm(s_v, 512, t)
            ACT(vTb.t[:, t, 0:512], pv.t[:, :], AF.Copy, [pv], [vTb])
        if STOP == 41:
            return
        if P2:
            s_z = load_w(*zslot_cols)
            for t in range(TPS):
                pz = proj_tm(s_z, 512, t)
                ACT(zGb.t[:, t, 0:512], pz.t[:, :], AF.Silu, [pz], [zGb])
        if STOP == 42:
            return
        for t in range(TPS):
            for blk in range(nblk):
                pb, pap = ptsn()
                TR(pap, kdTb.t[:, blk, t * 128:(t + 1) * 128], identb.t[:], [kdTb, identb], [pb])
                ACT(kdtokb.t[:, t, blk * 128:(blk + 1) * 128], pap, AF.Copy, [pb], [kdtokb])

    def mixer_A(s_idx):
        qTb, kTb, kdTb, vTb, zGb, kdtokb = BFB
        s_qk = load_w(0, 512)
        pg = pjn()
        for k in range(KC):
            MM(pg.t[0:16, :], wsm.t[:, k, 0:16], uT.t[:, k, 3:515], k == 0, k == KC - 1, [wsm, uT], [pg])
        ACT(grt.t[:], pg.t[0:16, :], AF.Copy, [pg], [grt])
        if STOP < 1:
            return
        for blk in range(2):
            kf = fbn()
            pk = proj_fm(s_qk, 256 + blk * 128, 128)
            ACT(kf.t[:, 0:512], pk.t[:, :], AF.Copy, [pk], [kf])
            px = pjn()
            MM(px.t[:, :], gwb.t[:, blk * 128:(blk + 1) * 128], grt.t[:], True, True, [gwb, grt], [px])
            sp = fbn()
            ACT(sp.t[:, 0:512], px.t[:, :], AF.Exp, [px, smp], [sp], scale=-1.0, bias=negb[:, blk:blk + 1])
            ACT(sp.t[:, 0:512], sp.t[:, 0:512], AF.Ln, [sp], [sp], bias=1.0)
            csb = fbn()
            if STOP < 2:
                continue
            scan(csb, sp)
            if STOP < 3:
                continue
            vd_front(blk, kf, csb, -1.0 / 16, s_qk, blk * 128, 0.125, qTb, kTb, kdTb)
        if STOP < 4 or 30 < STOP < 40:
            return
        vz_and_kdtok(2, (512, 1024), (1040, 1552), vTb, zGb, kdTb, kdtokb)
        if STOP < 5 or 40 < STOP < 50:
            return
        recur_vd(s_idx, "A", 2, 64, qTb, kTb, kdtokb, vTb, zGb, OFF_A, DEC_A, PR_NWA, 0)

    def mixer_C(s_idx):
        qTb, kTb, kdTb, vTb, zGb, kdtokb = BFB
        s_f = load_w(4632, 5144)
        s_q = load_w(4120, 4632) if P2 else None
        for blk in range(4):
            pf_ = proj_fm(s_f, blk * 128, 128)
            ff = fbn()
            ACT(ff.t[:, 0:512], pf_.t[:, :], AF.Sigmoid, [pf_], [ff])
            TS(ff.t[:, 0:512], ff.t[:, 0:512], omlb[:, blk:blk + 1], lbv[:, blk:blk + 1], ALU.mult, ALU.add,
               [ff, smp], [ff])
            kf = fbn()
            TS(kf.t[:, 0:512], ff.t[:, 0:512], -1.0, 1.0, ALU.mult, ALU.add, [ff], [kf])
            lf = fbn()
            ACT(lf.t[:, 0:512], ff.t[:, 0:512], AF.Ln, [ff], [lf])
            csb = fbn()
            scan(csb, lf)
            vd_front(blk, kf, csb, 1.0, s_q, blk * 128, 128 ** -0.5, qTb, kTb, kdTb)
        vz_and_kdtok(4, (5144, 5656), (5656, 6168), vTb, zGb, kdTb, kdtokb)
        recur_vd(s_idx, "C", 1, 128, qTb, kTb, kdtokb, vTb, zGb, OFF_C, DEC_C, PR_NWC, 2)

    def mixer_B(s_idx):
        qTb, kTb, ktokb, vTb, oGb, zGb = BFB
        if P2:
            s_q = load_w(1552, 2064)
            for blk in range(4):
                conv_block(s_idx, s_q, blk * 128, blk, PP_MCW + blk * 4, PP_MCB + blk, qTb.t[:, blk, 0:512], qTb)
        s_k = load_w(2064, 2576)
        for blk in range(4):
            conv_block(s_idx, s_k, blk * 128, 4 + blk, PP_MCW + (4 + blk) * 4, PP_MCB + 4 + blk,
                       kTb.t[:, blk, 0:512], kTb)
        s_v = load_w(2576, 3088)
        for t in range(TPS):
            bg, pgi = proj_small(16, 24, t)
            gi = gates_t[:, t * 16:t * 16 + 4]
            TT(gi, pgi[:, 0:4], ib2, ALU.add, [bg, smp], [GATES])
            ACT(gi, gi, AF.Exp, [GATES], [GATES])
            gf = gates_t[:, t * 16 + 4:t * 16 + 8]
            TT(gf, pgi[:, 4:8], pr.t[:, PR_FB:PR_FB + 4], ALU.add, [bg, pr], [GATES])
            ACT(gf, gf, AF.Exp, [GATES], [GATES], scale=-1.0)
            ACT(gf, gf, AF.Ln, [GATES], [GATES], bias=1.0)
            TS(gf, gf, -1.0, None, ALU.mult, None, [GATES], [GATES])
            pv = proj_tm(s_v, 512, t)
            vt4 = vTb.t[:, t, :].rearrange("p (h c) -> p h c", c=132)
            for h in range(4):
                TS(vt4[:, h, 0:128], pv.t[:, h * 128:(h + 1) * 128], gi[:, h:h + 1], None, ALU.mult, None,
                   [pv, GATES], [vTb])
            CP(vt4[:, :, 128], gi, [GATES], [vTb])
        if P2:
            s_o = load_w(3096, 3608)
            for t in range(TPS):
                po_ = proj_tm(s_o, 512, t)
                ACT(oGb.t[:, t, 0:512], po_.t[:, :], AF.Sigmoid, [po_], [oGb])
            s_z = load_w(3608, 4120)
            for t in range(TPS):
                pz = proj_tm(s_z, 512, t)
                ACT(zGb.t[:, t, 0:512], pz.t[:, :], AF.Silu, [pz], [zGb])
        for t in range(TPS):
            for h in range(4):
                pb, pap = ptsn()
                TR(pap, kTb.t[:, h, t * 128:(t + 1) * 128], identb.t[:], [kTb, identb], [pb])
                ACT(ktokb.t[:, t, h * 128:(h + 1) * 128], pap, AF.Copy, [pb], [ktokb])
        for t in range(TPS):
            tcs = slice(t * 128, (t + 1) * 128)
            vt4 = vTb.t[:, t, :].rearrange("p (h c) -> p h c", c=132)
            groups = []
            for h in range(4):
                groups.append(dict(qT=qTb.t[:, h, tcs], kT=kTb.t[:, h, tcs], qkbufs=[qTb, kTb],
                                   ktok=ktokb.t[:, t, h * 128:(h + 1) * 128], ktokbufs=[ktokb],
                                   v=vt4[:, h, :], vbuf=vTb, heads=[h], ncg=129,
                                   sf=stf_t[:, OFF_B + h * 132:OFF_B + (h + 1) * 132],
                                   sb=stb_t[:, OFF_B + h * 132:OFF_B + (h + 1) * 132],
                                   stf=STF[("B", h)], stb=STB[("B", h)], decoff=DEC_B + h))
            a_ap = gates_t[:, t * 16 + 4:t * 16 + 8]
            scalar_decay(t, 4, a_ap, GATES, groups, yb, lambda h: yb.t[:, h, 0:129], 129, 132)
            if P2:
                TS(rr4, yb.t[:, :, 128], -1.0, None, ALU.mult, None, [yb], [smn])
                TT(rr4, rr4, yb.t[:, :, 128], ALU.max, [yb, smn], [smn])
                TS(rr4, rr4, 1.0, None, ALU.max, None, [smn], [smn])
                S.op("vector", lambda e: e.reciprocal(out=rr4, in_=rr4), [smn], [smn])
                hb = nxt(TTB, "ttb")
                for h in range(4):
                    hc = slice(h * 128, (h + 1) * 128)
                    STT(hb.t[:, hc], yb.t[:, h, 0:128], rr4[:, h:h + 1], oGb.t[:, t, hc], ALU.mult, ALU.mult,
                        [yb, smn, oGb], [hb])
                for h in range(4):
                    hc = slice(h * 128, (h + 1) * 128)
                    ACT(junk.t[:, 0:128], hb.t[:, hc], AF.Square, [hb], [smn], accum_out=ss4[:, h:h + 1])
                rstd_of(rs4[:, 0:4], ss4[:, 0:4], 128)
                nwz = nxt(TTB, "ttb")
                TT(nwz.t[:], pr.t[:, PR_NWB:PR_NWB + 512], zGb.t[:, t, 0:512], ALU.mult, [pr, zGb], [nwz])
                mix = nxt(MIXB, "mixb")
                for h in range(4):
                    hc = slice(h * 128, (h + 1) * 128)
                    STT(mix.t[:, hc], hb.t[:, hc], rs4[:, h:h + 1], nwz.t[:, hc], ALU.mult, ALU.mult,
                        [hb, smn, nwz], [mix])
                finish_mixer(s_idx, 1, t, mix)
        if P2:
            out_proj(s_idx, 1)

    def mixer_D(s_idx):
        bcTb, btokb, _, _, _, zGb = BFB
        s_x = load_w(6168, 6680)
        for blk in range(4):
            conv_block(s_idx, s_x, blk * 128, 8 + blk, PP_SCW + blk * 4, PP_SCB + blk, xst.t[:, blk, :], xst)
        s_bc = load_w(6680, 7192)
        for blk in (range(4) if P2 else range(2)):
            conv_block(s_idx, s_bc, blk * 128, 12 + blk, PP_SCW + (4 + blk) * 4, PP_SCB + 4 + blk,
                       bcTb.t[:, blk, 0:512], bcTb)
        if P2:
            s_z = load_w(7200, 7712)
            for t in range(TPS):
                pz = proj_tm(s_z, 512, t)
                ACT(zGb.t[:, t, 0:512], pz.t[:, :], AF.Silu, [pz], [zGb])
        for t in range(TPS):
            for g in range(2):
                pb, pap = ptsn()
                TR(pap, bcTb.t[:, g, t * 128:(t + 1) * 128], identb.t[:], [bcTb, identb], [pb])
                ACT(btokb.t[:, t, g * 128:(g + 1) * 128], pap, AF.Copy, [pb], [btokb])
        for t in range(TPS):
            tcs = slice(t * 128, (t + 1) * 128)
            bd, pdt = proj_small(24, 32, t)
            dtv = gates_t[:, 256 + t * 32:256 + t * 32 + 8]
            TT(dtv, pdt[:, 0:8], pr.t[:, PR_DTB:PR_DTB + 8], ALU.add, [bd, pr], [GATESD])
            ACT(dtv, dtv, AF.Exp, [GATESD], [GATESD])
            ACT(dtv, dtv, AF.Ln, [GATESD], [GATESD], bias=1.0)
            av = gates_t[:, 256 + t * 32 + 8:256 + t * 32 + 16]
            TT(av, dtv, aneg, ALU.mult, [GATESD, smp], [GATESD])
            xs = nxt(TTB, "ttb")
            for blk in range(4):
                bx, pxr = pwn()
                S.op("tensor", lambda e, pxr=pxr, blk=blk, tcs=tcs: e.transpose(pxr[:, 0:128], xst.t[:, blk, tcs], identf.t[:]),
                     [xst, identf], [bx], cost=400)
                CP(xs.t[:, blk * 128:(blk + 1) * 128], pxr[:, 0:128], [bx], [xs])
            vt = nxt(VTD, "vtd")
            dtb = bass.AP(gates_t, 256 + t * 32, [[512, 128], [1, 8], [0, 64]])
            TT(vt.t[:, :].rearrange("p (h c) -> p h c", c=64), xs.t[:, :].rearrange("p (h c) -> p h c", c=64),
               dtb, ALU.mult, [xs, GATESD], [vt])
            groups = []
            for g in range(2):
                groups.append(dict(qT=bcTb.t[:, 2 + g, tcs], kT=bcTb.t[:, g, tcs], qkbufs=[bcTb],
                                   ktok=btokb.t[:, t, g * 128:(g + 1) * 128], ktokbufs=[btokb],
                                   v=vt.t[:, g * 256:(g + 1) * 256], vbuf=vt, heads=[4 * g + j for j in range(4)],
                                   ncg=256, sf=stf_t[:, OFF_D + g * 256:OFF_D + (g + 1) * 256],
                                   sb=stb_t[:, OFF_D + g * 256:OFF_D + (g + 1) * 256],
                                   stf=STF[("D", g)], stb=STB[("D", g)], decoff=DEC_D + 4 * g))
            Y = nxt(TTB, "ttb")
            scalar_decay(t, 8, av, GATESD, groups, Y, lambda h: Y.t[:, h * 64:(h + 1) * 64], 64, 64)
            if P2:
                dsb = bass.AP(pr.t, PR_D, [[NPR, 128], [1, 8], [0, 64]])
                xs3 = xs.t[:, :].rearrange("p (h c) -> p h c", c=64)
                TT(xs3, xs3, dsb, ALU.mult, [xs, pr], [xs])
                TT(Y.t[:], Y.t[:], xs.t[:], ALU.add, [Y, xs], [Y])
                TT(Y.t[:], Y.t[:], zGb.t[:, t, 0:512], ALU.mult, [Y, zGb], [Y])
                for g in range(2):
                    ACT(junk.t[:, 0:256], Y.t[:, g * 256:(g + 1) * 256], AF.Square, [Y], [smn],
                        accum_out=ssn[:, g:g + 1])
                rstd_of(rsn[:, 0:2], ssn[:, 0:2], 256)
                mix = nxt(MIXB, "mixb")
                for g in range(2):
                    gs = slice(g * 256, (g + 1) * 256)
                    STT(mix.t[:, gs], Y.t[:, gs], rsn[:, g:g + 1], pr.t[:, PR_NWD + g * 256:PR_NWD + (g + 1) * 256],
                        ALU.mult, ALU.mult, [Y, smn, pr], [mix])
                finish_mixer(s_idx, 3, t, mix)
        if P2:
            out_proj(s_idx, 3)

    finals = []
    for layer in range(NLAYERS):
        win3 = d_win_all[layer].rearrange("(k p) c -> p k c", p=128)
        wout3 = d_wout_all[layer].rearrange("(k p) c -> p k c", p=128)
        load_layer_params(layer)
        for P2 in (False, True):
            last = P2 and layer == NLAYERS - 1
            init_pass()
            for s_idx in range(NS):
                if s_idx == 0:
                    make_uT_tile(halo_h, halo_h.t[:], slice(0, 3), 3)
                else:
                    CP(uT.t[:, :, 0:3], uT.t[:, :, 512:515], [uT], [uT])
                for t in range(TPS):
                    tg = s_idx * TPS + t
                    make_uT_tile(HT[tg], hres_t[:, tg, :], tokc(t), 128)
                if "A" in MIXERS:
                    mixer_A(s_idx)
                if "B" in MIXERS:
                    mixer_B(s_idx)
                if "C" in MIXERS:
                    mixer_C(s_idx)
                if "D" in MIXERS:
                    mixer_D(s_idx)
            if not P2:
                end_p1()
            elif not last:
                exchange_halo()

    DMA("sync", nwbc.t[:], d_fnw.partition_broadcast(128), nwbc, [], [nwbc])
    for tg in range(NT):
        ACT(junk.t[:], hres_t[:, tg, :], AF.Square, [HT[tg]], [smn], accum_out=ss4[:, 0:1])
        rstd_of(rs4[:, 0:1], ss4[:, 0:1], DM)
        for half in range(2):
            ob = nxt(TTB, "ttb")
            hs = slice(half * 512, (half + 1) * 512)
            STT(ob.t[:], hres_t[:, tg, hs], rs4[:, 0:1], nwbc.t[:, hs], ALU.mult, ALU.mult,
                [HT[tg], smn, nwbc], [ob])
            finals.append(DMA("sync", d_out[tg * 128:(tg + 1) * 128, hs], ob.t[:], ob, [ob], []))
    finals.extend(dbg_ids)
    S.wait_all("sync", finals)
    print("n semaphores", len(S.sems), flush=True)
    S.schedule(reorder=REORDER)
    print("ops:", len(S.ops), "model makespan us:", S.makespan / 1e3, flush=True)
    S.emit()
    S.close()
    es.close()
    return nc


def pack_layer(inp, l):
    f = lambda a: np.ascontiguousarray(np.asarray(a, dtype=np.float32))
    pp = np.zeros((128, NPP), np.float32)
    pp[:, PP_GB:PP_GB + 2] = f(inp["gla_gate_b"][l]).reshape(2, 128).T
    pp[:, PP_MCW:PP_MCW + 32] = f(inp["ml_conv_w"][l]).reshape(4, 8, 128).transpose(2, 1, 0).reshape(128, 32)
    pp[:, PP_MCB:PP_MCB + 8] = f(inp["ml_conv_b"][l]).reshape(8, 128).T
    pp[:, PP_SCW:PP_SCW + 32] = f(inp["ssd_conv_w"][l]).reshape(4, 8, 128).transpose(2, 1, 0).reshape(128, 32)
    pp[:, PP_SCB:PP_SCB + 8] = f(inp["ssd_conv_b"][l]).reshape(8, 128).T
    pp[:, PP_LB:PP_LB + 8] = f(inp["hg_lb_logits"]).reshape(2, 4, 128).transpose(2, 1, 0).reshape(128, 8)
    pr = np.concatenate([f(inp["ml_i_b"][l]), f(inp["ml_f_b"][l]), f(inp["ssd_dt_bias"][l]), f(inp["ssd_A_log"][l]),
                         f(inp["ssd_D"][l]), f(inp["gla_norm_w"][l]), f(inp["ml_norm_w"][l]), f(inp["hg_norm_w"][l]),
                         f(inp["ssd_norm_w"][l])]).astype(np.float32)
    assert pr.shape[0] == NPR
    return pp, pr


def core_pm(q):
    pm = np.zeros((128, 9), np.float32)
    for j in range(3):
        pm[:, j] = 1.0 if j < q else 0.0
        pm[:, 3 + j] = 1.0 - pm[:, j]
        pm[:, 6 + j] = 1.0 if j == q - 1 else 0.0
    return pm


_NC_CACHE = {}


def pack_inputs(inputs):
    f = lambda a: np.ascontiguousarray(np.asarray(a, dtype=np.float32))
    x = f(inputs["x"])
    pps, prs = zip(*[pack_layer(inputs, l) for l in range(2)])
    shared = dict(w_in=f(inputs["w_in"]), w_out=f(inputs["w_out"]), nw=f(inputs["norm_w"]),
                  fnw=f(inputs["final_norm_w"]), pp=np.ascontiguousarray(np.stack(pps)),
                  pr=np.ascontiguousarray(np.stack(prs)), gw=f(inputs["gla_gate_w"]))
    in_maps = []
    for c in range(NCORES):
        b, q = c // 4, c % 4
        d = dict(shared)
        d["hin"] = np.ascontiguousarray(x[b, q * T:(q + 1) * T, :])
        d["halo"] = (np.zeros((3, DM), np.float32) if q == 0
                     else np.ascontiguousarray(x[b, q * T - 3:q * T, :]))
        d["pm"] = core_pm(q)
        in_maps.append(d)
    return in_maps


def kernel(**inputs):
    if "nc" not in _NC_CACHE:
        _NC_CACHE["nc"] = build()
    in_maps = pack_inputs(inputs)
    res = run_bass_kernel_spmd(_NC_CACHE["nc"], in_maps, core_ids=list(range(NCORES)))
    B = np.asarray(inputs["x"]).shape[0]
    out = np.zeros((B, 4 * T, DM), np.float32)
    for c in range(NCORES):
        out[c // 4, (c % 4) * T:(c % 4 + 1) * T, :] = np.asarray(res.results[c]["hout"], dtype=np.float32)
    return out
```
